# Optimizing a Trainium2 kernel written in Bass

```python
import jax, jax.numpy as jnp
from jax import lax
import numpy as np

D_MODEL = 1024
BATCH = 2
SEQ = 8192
DEPTH = 1

D_MIX = 2 * D_MODEL
D_POOL = D_MIX // 2
D_SGU = D_MIX - D_POOL
POOL_WINDOWS = (2, 4, 8, 16)
N_POOL_GROUPS = len(POOL_WINDOWS)
POOL_GROUP = D_POOL // N_POOL_GROUPS
N_SGU_HEADS = 4
SGU_HEAD = D_SGU // N_SGU_HEADS
CHUNK = 128
D_PLE = 256
D_IN = D_POOL + 2 * D_SGU + D_MIX
DEEPNORM_ALPHA = (2 * DEPTH) ** 0.25
DEEPNORM_BETA = (8 * DEPTH) ** -0.25
LN_EPS = 1e-5

kernel_name = "hybrid_pool_sgu_deepnorm_block"


def layer_norm(x, g, b):
    xf = x.astype(jnp.float32)
    mu = jnp.mean(xf, axis=-1, keepdims=True)
    var = jnp.mean(jnp.square(xf - mu), axis=-1, keepdims=True)
    return ((xf - mu) * lax.rsqrt(var + LN_EPS)).astype(x.dtype) * g + b


def pool_mixer(a, pool_w, pool_scale):
    bsz, s, _ = a.shape
    a4 = a.reshape(bsz, s, N_POOL_GROUPS, POOL_GROUP)
    cs = jnp.cumsum(a4.astype(jnp.float32), axis=1)
    t = jnp.arange(s)
    outs = []
    for g, w in enumerate(POOL_WINDOWS):
        c = cs[:, :, g]
        lower = jnp.pad(c, ((0, 0), (w, 0), (0, 0)))[:, :s]
        cnt = jnp.minimum(t + 1, w).astype(jnp.float32)[None, :, None]
        outs.append((c - lower) / cnt)
    pooled = jnp.stack(outs, axis=2).astype(a.dtype) - a4
    mixed = jnp.einsum('bsgc,gcd->bsgd', pooled, pool_w)
    return mixed.reshape(bsz, s, D_POOL) * pool_scale


def spatial_gating(u, v, ln_g, ln_b, w_s, b_s):
    bsz, s, _ = u.shape
    u = jax.nn.gelu(u)
    v = jax.nn.gelu(v)
    vh = v.reshape(bsz, s, N_SGU_HEADS, SGU_HEAD)
    vh = layer_norm(vh, ln_g.reshape(N_SGU_HEADS, SGU_HEAD), ln_b.reshape(N_SGU_HEADS, SGU_HEAD))
    vc = vh.reshape(bsz, s // CHUNK, CHUNK, N_SGU_HEADS, SGU_HEAD)
    mask = jnp.tril(jnp.ones((CHUNK, CHUNK), dtype=bool))
    w = jnp.where(mask[None], w_s, jnp.zeros_like(w_s))
    sv = jnp.einsum('hij,bnjhc->bnihc', w, vc) + b_s.T[None, None, :, :, None]
    return u * sv.reshape(bsz, s, D_SGU)


def setup_inputs(seed: int = 0) -> dict:
    key = jax.random.key(seed)
    ks = jax.random.split(key, 16)
    nrm = jax.random.normal
    f32 = jnp.float32
    x = nrm(ks[0], (BATCH, SEQ, D_MODEL), f32)
    p = nrm(ks[1], (DEPTH, BATCH, SEQ, D_PLE), f32)
    w_in = nrm(ks[2], (DEPTH, D_MODEL, D_IN), f32) * D_MODEL ** -0.5
    pool_w = nrm(ks[3], (DEPTH, N_POOL_GROUPS, POOL_GROUP, POOL_GROUP), f32) * POOL_GROUP ** -0.5
    pool_scale = 1.0 + 0.1 * nrm(ks[4], (DEPTH, D_POOL), f32)
    sgu_ln_g = 1.0 + 0.02 * nrm(ks[5], (DEPTH, D_SGU), f32)
    sgu_ln_b = 0.02 * nrm(ks[6], (DEPTH, D_SGU), f32)
    sgu_w = nrm(ks[7], (DEPTH, N_SGU_HEADS, CHUNK, CHUNK), f32) * (0.5 * CHUNK ** -0.5)
    sgu_b = 1.0 + 0.01 * nrm(ks[8], (DEPTH, N_SGU_HEADS, CHUNK), f32)
    w_out = nrm(ks[9], (DEPTH, D_MIX, D_MODEL), f32) * (D_MIX ** -0.5 * DEEPNORM_BETA)
    ln_g = 1.0 + 0.02 * nrm(ks[10], (DEPTH, D_MODEL), f32)
    ln_b = 0.02 * nrm(ks[11], (DEPTH, D_MODEL), f32)
    ple_w = nrm(ks[12], (DEPTH, D_PLE, D_MODEL), f32) * D_PLE ** -0.5
    ple_gate_w = nrm(ks[13], (DEPTH, D_MODEL, D_MODEL), f32) * D_MODEL ** -0.5
    ple_gate_b = 0.02 * nrm(ks[14], (DEPTH, D_MODEL), f32)
    return {"x": x, "p": p, "w_in": w_in, "pool_w": pool_w, "pool_scale": pool_scale,
            "sgu_ln_g": sgu_ln_g, "sgu_ln_b": sgu_ln_b, "sgu_w": sgu_w, "sgu_b": sgu_b,
            "w_out": w_out, "ln_g": ln_g, "ln_b": ln_b, "ple_w": ple_w,
            "ple_gate_w": ple_gate_w, "ple_gate_b": ple_gate_b}


def reference(x, p, w_in, pool_w, pool_scale, sgu_ln_g, sgu_ln_b, sgu_w, sgu_b,
              w_out, ln_g, ln_b, ple_w, ple_gate_w, ple_gate_b):
    for i in range(DEPTH):
        h = jnp.einsum('bsd,de->bse', x, w_in[i])
        a, u, v, z = jnp.split(h, [D_POOL, D_POOL + D_SGU, D_POOL + 2 * D_SGU], axis=-1)
        y_pool = pool_mixer(a, pool_w[i], pool_scale[i])
        y_sgu = spatial_gating(u, v, sgu_ln_g[i], sgu_ln_b[i], sgu_w[i], sgu_b[i])
        y = jnp.concatenate([y_pool, y_sgu], axis=-1) * jax.nn.silu(z)
        mix = jnp.einsum('bse,ed->bsd', y, w_out[i])
        x = layer_norm(DEEPNORM_ALPHA * x + mix, ln_g[i], ln_b[i])
        gate = jax.nn.sigmoid(jnp.einsum('bsd,de->bse', x, ple_gate_w[i]) + ple_gate_b[i])
        x = x + gate * jnp.einsum('bsk,kd->bsd', p[i], ple_w[i])
    return x
```

```python
import numpy as np
from contextlib import ExitStack

import concourse.bass as bass
import concourse.mybir as mybir
from concourse.bass_utils import run_bass_kernel_spmd

F32 = mybir.dt.float32
BF16 = mybir.dt.bfloat16
AF = mybir.ActivationFunctionType
ALU = mybir.AluOpType

N_CORES = 8
D = 1024
TOK = 2048
SB = 1024
NSB = TOK // SB
BLK = 512
NBLK = SB // BLK
HALO = 16
XTW = HALO + SB
ALPHA = 2.0 ** 0.25
LN_EPS = 1e-5
WINDOWS = (2, 4, 8, 16)
INTERLEAVE_STEPS = 6
import os
FLAG_LATE_PIECES = os.environ.get('K_LATE', '1') == '1'
FLAG_CONST_LATE = os.environ.get('K_CONST', '1') == '1'
FLAG_XT_EARLY = os.environ.get('K_XT', '1') == '1'


class Sem:
    def __init__(self, handle, step):
        self.h = handle
        self.step = step
        self.val = 0

    def advance(self, n=1):
        self.val += self.step * n
        return (self, self.val)


class Buf:
    __slots__ = ("w", "r", "name", "psum")

    def __init__(self, name="", psum=False):
        self.w = {}
        self.r = {}
        self.name = name
        self.psum = psum


class Stream:
    def __init__(self, name, prog, is_pe=False):
        self.name = name
        self.prog = prog
        self.items = []
        self.waited = {}
        self.is_pe = is_pe


def _merge(d, tok):
    s, v = tok
    if d.get(s, 0) < v:
        d[s] = v


def emit(stream, fn, reads=(), writes=(), dma_sem=None, n_dma=1, fills=()):
    deps = {}
    for b in fills:
        for s, v in b.r.items():
            _merge(deps, (s, v))
        for s, v in b.w.items():
            if s is not stream.prog:
                _merge(deps, (s, v))
    for b in reads:
        for s, v in b.w.items():
            _merge(deps, (s, v))
        if b.psum:
            for s, v in b.r.items():
                if s is not stream.prog:
                    _merge(deps, (s, v))
    for b in writes:
        for s, v in b.w.items():
            _merge(deps, (s, v))
        for s, v in b.r.items():
            _merge(deps, (s, v))
    for s, v in deps.items():
        if stream.is_pe and s is stream.prog:
            continue
        if stream.waited.get(s, 0) >= v:
            continue
        stream.waited[s] = v
        stream.items.append(("wait", s, v))
    if dma_sem is not None:
        tok = dma_sem.advance(n_dma)
        stream.items.append(("dma", fn, dma_sem))
    else:
        tok = stream.prog.advance()
        stream.items.append(("op", fn, stream.prog))
    for b in reads:
        _merge(b.r, tok)
    for b in writes:
        b.w = {tok[0]: tok[1]}
        b.r = {}
    for b in fills:
        _merge(b.w, tok)
    return tok


def replay(stream, eng):
    for it in stream.items:
        if it[0] == "wait":
            eng.wait_ge(it[1].h, it[2])
        elif it[0] == "op":
            inst = it[1](eng)
            inst.then_inc(it[2].h, 1)
        else:
            it[1](eng, it[2].h)


def build_program(dbg=False):
    nc = bass.Bass("TRN2", target_bir_lowering=False)
    es = ExitStack()

    def dram_in(name, shape):
        return nc.dram_tensor(name, list(shape), F32, kind="ExternalInput").ap()

    x_d = dram_in("x", [TOK, D])
    xh_d = dram_in("xh", [HALO, D])
    p_d = dram_in("p", [TOK, 256])
    icnt_d = dram_in("icnt", [128, 4 * 16])
    win_d = dram_in("w_in", [D, 5120])
    poolw_d = dram_in("pool_w", [4, 256, 256])
    wout_d = dram_in("w_out", [2048, D])
    wp_d = dram_in("ple_w", [256, D])
    wg_d = dram_in("ple_gate_w", [D, D])
    bg_d = dram_in("ple_gate_b", [1, D])
    vecs_d = dram_in("vecs", [128, 40])
    gbc_d = dram_in("gbc", [128, D])
    bbc_d = dram_in("bbc", [128, D])
    bsb_d = dram_in("bsb", [128, 512])
    tril_d = dram_in("tril", [128, 128])
    ident_d = dram_in("ident", [128, 128])
    sgw_d = dram_in("sgw", [128, 512])
    out_d = nc.dram_tensor("out", [TOK, D], F32, kind="ExternalOutput").ap()
    if dbg:
        dbg_y = nc.dram_tensor("dbg_y", [128, 16 * SB], BF16, kind="ExternalOutput").ap()

    def sb_t(name, shape, dt):
        return es.enter_context(nc.sbuf_tensor("s_" + name, list(shape), dt))

    def ps_t(name, shape, dt):
        return es.enter_context(nc.psum_tensor(name, list(shape), dt))

    def new_sem(name, step):
        return Sem(es.enter_context(nc.semaphore(name)), step)

    S_pe = Stream("pe", new_sem("p_pe", 1), is_pe=True)
    S_act = Stream("act", new_sem("p_act", 1))
    S_dve = Stream("dve", new_sem("p_dve", 1))
    S_pool = Stream("pool", new_sem("p_pool", 1))
    S_sp = Stream("sp", new_sem("p_sp", 1))

    win = [sb_t(f"win{i}", [128, 8, 512], BF16) for i in range(2)]
    win_hb = [[Buf(f"win{i}a"), Buf(f"win{i}b")] for i in range(2)]
    win_sem = [[new_sem(f"d_win{i}a", 16), new_sem(f"d_win{i}b", 16)] for i in range(2)]
    wpool = sb_t("wpool", [128, 4, 2, 256], BF16)
    wout = sb_t("wout", [128, 16, D], BF16)
    wg = sb_t("wg", [128, 8, D], BF16)
    wp = sb_t("wp", [128, 2, D], BF16)
    wmT = sb_t("wmT", [128, 4, 128], BF16)
    xT = sb_t("xT", [128, 8, XTW], BF16)
    yT = sb_t("yT", [128, 16, SB], BF16)
    NROW = 4
    row = [sb_t(f"row{i}", [128, D], F32) for i in range(NROW)]
    row_b = [Buf(f"row{i}") for i in range(NROW)]
    row_sem = [new_sem(f"d_row{i}", 16) for i in range(NROW)]
    abuf = [sb_t(f"abuf{i}", [128, 2, 528], F32) for i in range(2)]
    abuf_b = [Buf() for _ in range(2)]
    abuf_hb = [Buf() for _ in range(2)]
    abuf_sem = [new_sem(f"d_abuf{i}", 16) for i in range(2)]
    f4b = [sb_t(f"f4b{i}", [128, 2, 528], F32) for i in range(2)]
    f4b_b = [Buf() for _ in range(2)]
    f4c = [sb_t(f"f4c{i}", [128, D], F32) for i in range(2)]
    f4c_b = [Buf() for _ in range(2)]
    b2 = [sb_t(f"b2_{i}", [128, D], BF16) for i in range(8)]
    b2_b = [Buf() for _ in range(8)]
    pooled, pooled_b = b2[0:2], b2_b[0:2]
    szb, szb_b = b2[2:4], b2_b[2:4]
    gub, gub_b = b2[4:6], b2_b[4:6]
    vnb, vnb_b = b2[6:8], b2_b[6:8]
    xhat, xhat_b = b2[4:6], b2_b[4:6]
    xnT, xnT_b = b2[6:8], b2_b[6:8]
    NXB = 2
    xb = [sb_t(f"xb{i}", [128, D], BF16) for i in range(NXB)]
    xb_b = [Buf() for _ in range(NXB)]
    xb_sem = [new_sem(f"d_xb{i}", 16) for i in range(NXB)]
    pin = [sb_t(f"pin{i}", [128, 256], F32) for i in range(2)]
    pin_b = [Buf() for _ in range(2)]
    pin_sem = [new_sem(f"d_pin{i}", 16) for i in range(2)]
    pT = [sb_t(f"pT{i}", [128, 2, 128], BF16) for i in range(3)]
    pT_b = [Buf() for _ in range(3)]
    ident_f = sb_t("ident_f", [128, 128], F32)
    ident_b = sb_t("ident_b", [128, 128], BF16)
    vecs = sb_t("vecs", [128, 40], F32)
    cst = sb_t("cst", [128, 8, 128], F32)
    gbc = sb_t("gbc", [128, D], F32)
    bbc = sb_t("bbc", [128, D], F32)
    bsb = sb_t("bsb", [128, 512], F32)
    tril = sb_t("tril", [128, 128], F32)
    icnt = sb_t("icnt", [128, 64], F32)
    ones_f = sb_t("ones_f", [128, 128], F32)
    ones2 = sb_t("ones2", [128, 128], BF16)
    bg2 = sb_t("bg2", [128, D], BF16)
    bgf = f4c[0][0:1, :]
    bgt = f4c[1][0:1, :]
    bghi = b2[6][0:1, :]
    bglo = b2[7][0:1, :]
    wmTf2 = b2[5][:].bitcast(F32)
    sgw2 = b2[4][:].bitcast(F32)
    negh = sb_t("negh", [128, 8], F32)
    warm = sb_t("warm", [128, 2], F32)
    fix_t = sb_t("fix_t", [128, 2, 16], F32)
    st6 = [sb_t(f"st6_{i}", [128, 4, 6], F32) for i in range(2)]
    mv = [sb_t(f"mv{i}", [128, 4, 2], F32) for i in range(2)]
    ve = [sb_t(f"ve{i}", [128, 4], F32) for i in range(2)]
    rstd = [sb_t(f"rstd{i}", [128, 4], F32) for i in range(2)]
    stat_b = [Buf() for _ in range(2)]
    mv_b = [Buf() for _ in range(2)]
    ve_b = [Buf() for _ in range(2)]
    rstd_b = [Buf() for _ in range(2)]
    st6p = [sb_t(f"st6p{i}", [128, 2, 6], F32) for i in range(3)]
    mvp = [sb_t(f"mvp{i}", [128, 2], F32) for i in range(3)]
    vep = [sb_t(f"vep{i}", [128, 1], F32) for i in range(3)]
    rstdp = [sb_t(f"rstdp{i}", [128, 1], F32) for i in range(3)]
    statp_b = [Buf() for _ in range(3)]
    mvp_b = [Buf() for _ in range(3)]
    vep_b = [Buf() for _ in range(3)]
    rstdp_b = [Buf() for _ in range(3)]

    VEC_PS, VEC_SG, VEC_SB, VEC_LG, VEC_LB = 0, 8, 16, 24, 32

    NPS = 8
    psf = [ps_t(f"psf{i}", [128, 512], F32) for i in range(NPS)]
    psf_b = [Buf(f"psf{i}", psum=True) for i in range(NPS)]
    ps_ctr = [0]

    def ps_alloc():
        i = ps_ctr[0] % NPS
        ps_ctr[0] += 1
        return psf[i], psf_b[i]

    const_b = Buf("consts")
    xT_b = [[Buf() for _ in range(2)] for _ in range(8)]
    xTh_b = Buf()
    yT_b = [[Buf() for _ in range(NBLK)] for _ in range(16)]
    wpool_b, wp_b = Buf(), Buf()
    misc_sem = [new_sem(f"d_misc{i}", 16) for i in range(26)]
    misc_ctr = [0]

    def misc_dma(stream, out_ap, in_ap, writes, reads=()):
        sem = misc_sem[misc_ctr[0]]
        misc_ctr[0] += 1

        def fn(e, h):
            e.dma_start(out=out_ap, in_=in_ap).then_inc(h, 16)
        return emit(stream, fn, reads=reads, writes=writes, dma_sem=sem)

    win_v = win_d.rearrange("(k p) e -> p k e", p=128)
    fam_list = [("P", 0), ("P", 1), ("P", 2), ("P", 3),
                ("S", 0), ("S", 1), ("S", 2), ("S", 3), ("Z", 0), ("Z", 1)]
    all_fams = [(sbi, f) for sbi in range(NSB) for f in fam_list]

    def fam_cols(f):
        kind, i = f
        if kind == "P":
            return [(i * 256, 256), (3072 + i * 256, 256)]
        if kind == "Z":
            return [(3072 + 1024 + i * 512, 256), (3072 + 1024 + i * 512 + 256, 256)]
        return [(1024 + i * 256, 256), (2048 + i * 256, 256)]

    def emit_fam_load(fidx):
        if fidx >= len(all_fams):
            return
        slot = fidx % 2
        cols = fam_cols(all_fams[fidx][1])

        for hf, (c0, n) in enumerate(cols):
            def fn(e, h, hf=hf, c0=c0, n=n):
                e.dma_start(out=win[slot][:, :, hf * 256:hf * 256 + n], in_=win_v[:, :, c0:c0 + n]).then_inc(h, 16)
            extra = [win_hb[0][0]] if (fidx == 0 and hf == 1) else []
            emit(S_pool, fn, reads=extra, writes=[win_hb[slot][hf]], dma_sem=win_sem[slot][hf])

    ident_buf, vecs_buf, tril_buf, bsb_buf, icnt_buf, gb_buf = (Buf() for _ in range(6))
    sgw_buf = b2_b[4]
    wmTf_buf = b2_b[5]
    bgf_buf, bgt_buf, bghi_buf, bglo_buf = f4c_b[0], f4c_b[1], b2_b[6], b2_b[7]
    identb_buf = Buf()
    misc_dma(S_sp, ident_f[:], ident_d, [ident_buf])
    misc_dma(S_pool, ident_b[:], ident_d, [identb_buf])

    def emit_small_const_loads():
        misc_dma(S_sp, vecs[:], vecs_d, [vecs_buf])
        misc_dma(S_sp, icnt[:], icnt_d, [icnt_buf])

    ones_buf, negh_buf, ones2_buf, wmT_buf, cst_buf, bg2_buf = (Buf() for _ in range(6))

    def emit_const_compute():
        misc_dma(S_sp, tril[:], tril_d, [tril_buf])
        misc_dma(S_sp, sgw2, sgw_d, [sgw_buf])
        misc_dma(S_sp, bsb[:], bsb_d, [bsb_buf])
        misc_dma(S_sp, bgf, bg_d, [bgf_buf])
        misc_dma(S_sp, gbc[:], gbc_d, [gb_buf])
        misc_dma(S_sp, bbc[:], bbc_d, [gb_buf], reads=[gb_buf])
        emit(S_dve, lambda e: e.memset(ones_f[:], 1.0), writes=[ones_buf])
        emit(S_dve, lambda e: e.memset(negh[:], -0.5), writes=[negh_buf])
        emit(S_dve, lambda e: e.memset(ones2[:], 0.0), writes=[ones2_buf])
        emit(S_dve, lambda e: e.memset(ones2[0:2, :], 1.0), reads=[ones2_buf], writes=[ones2_buf])
        emit(S_pool, lambda e: e.memset(bg2[:], 0.0), writes=[bg2_buf])

        sgw3 = sgw2.rearrange("p (h j) -> p h j", h=4)
        emit(S_dve, lambda e: e.tensor_tensor(out=sgw3, in0=sgw3,
                                              in1=tril[:].unsqueeze(1).to_broadcast([128, 4, 128]),
                                              op=ALU.mult),
             reads=[tril_buf], writes=[sgw_buf])

    ps_w_box = []

    def emit_const_b():
        ps_w, ps_w_b = ps_alloc()
        ps_w_box.append((ps_w, ps_w_b))

        def fn_wmT(e):
            last = None
            for h in range(4):
                last = e.transpose(out=ps_w[:, h * 128:(h + 1) * 128], in_=sgw2[:, h * 128:(h + 1) * 128],
                                   identity=ident_f[:])
            return last
        emit(S_pe, fn_wmT, reads=[sgw_buf, ident_buf], writes=[ps_w_b])
        emit(S_act, lambda e: e.copy(out=wmT[:].rearrange("p h i -> p (h i)"), in_=ps_w[:]),
             reads=[ps_w_b], writes=[wmT_buf])
        emit(S_dve, lambda e: e.tensor_copy(out=wmTf2, in_=ps_w[:]),
             reads=[ps_w_b], writes=[wmTf_buf])

    def emit_const_c():
        ps_r, ps_r_b = ps_alloc()
        emit(S_pe, lambda e: e.matmul(ps_r[:], lhsT=ones_f[:], rhs=wmTf2, start=True, stop=True),
             reads=[ones_buf, wmTf_buf], writes=[ps_r_b])
        for k in range(8):
            h = k // 2
            emit(S_dve, lambda e, k=k, h=h: e.scalar_tensor_tensor(
                out=cst[:, k, :], in0=ps_r[:, h * 128:(h + 1) * 128],
                scalar=vecs[:, VEC_SB + k:VEC_SB + k + 1], in1=bsb[:, h * 128:(h + 1) * 128],
                op0=ALU.mult, op1=ALU.add),
                reads=[ps_r_b, vecs_buf, bsb_buf], fills=[cst_buf])
        emit(S_dve, lambda e: e.tensor_copy(out=bghi, in_=bgf), reads=[bgf_buf], writes=[bghi_buf])
        emit(S_dve, lambda e: e.tensor_copy(out=bgt, in_=bghi), reads=[bghi_buf], writes=[bgt_buf])
        emit(S_dve, lambda e: e.tensor_tensor(out=bgt, in0=bgf, in1=bgt, op=ALU.subtract),
             reads=[bgf_buf, bgt_buf], writes=[bgt_buf])
        emit(S_dve, lambda e: e.tensor_copy(out=bglo, in_=bgt), reads=[bgt_buf], writes=[bglo_buf])

    def emit_bg2():
        misc_dma(S_sp, bg2[0:1, :], bghi, [bg2_buf], reads=[bghi_buf, bg2_buf])
        misc_dma(S_sp, bg2[1:2, :], bglo, [bg2_buf], reads=[bglo_buf, bg2_buf])

    wout_bs = [Buf() for _ in range(4)]
    wg_bs = [Buf() for _ in range(2)]
    wout_v = wout_d.rearrange("(k p) d -> p k d", p=128)
    wg_v = wg_d.rearrange("(k p) d -> p k d", p=128)

    row_ctr = [0]

    def row_alloc():
        i = row_ctr[0] % NROW
        row_ctr[0] += 1
        return i

    pending = []

    def flush_pending():
        while pending:
            pending.pop(0)()

    fam_ctr = [0]
    evac_flip = [0]

    xb_ctr = [0]

    stage_ctr = [0]

    def xt_job(sbi, s, defer=False):
        if sbi == 0:
            pool_ = [(xb[0], xb_b[0]), (xb[1], xb_b[1]), (b2[4], b2_b[4]), (b2[5], b2_b[5]),
                     (b2[6], b2_b[6]), (b2[7], b2_b[7])]
        else:
            pool_ = [(xb[0], xb_b[0]), (xb[1], xb_b[1])]
        xbt, xbb = pool_[xb_ctr[0] % len(pool_)]
        xb_ctr[0] += 1
        si = stage_ctr[0]
        stage_ctr[0] += 1
        if sbi == 0:
            sidx = si % NROW
            stg, stg_bufs, stg_sem = row[sidx][:, :], [row_b[sidx]], row_sem[sidx]
        else:
            sidx = si % 2
            stg = abuf[sidx][:].rearrange("p c t -> p (c t)")[:, 0:D]
            stg_bufs, stg_sem = [abuf_b[sidx], abuf_hb[sidx]], abuf_sem[sidx]
        if s == "h":
            src = xh_d if sbi == 0 else x_d[sbi * SB - HALO:sbi * SB, :]
            np_ = HALO
        else:
            t0 = sbi * SB + s * 128
            src = x_d[t0:t0 + 128, :]
            np_ = 128

        def fn_ld(e, h):
            e.dma_start(out=stg[0:np_, :], in_=src).then_inc(h, 16)
        gate = [win_hb[0][0]] if (sbi == 0 and s in (4, 5, 6, 7)) else []
        emit(S_sp, fn_ld, reads=gate, writes=stg_bufs, dma_sem=stg_sem)
        if sbi == 0:
            emit(S_dve, lambda e: e.tensor_copy(out=xbt[0:np_, :], in_=stg[0:np_, :]),
                 reads=stg_bufs, writes=[xbb])
        else:
            emit(S_act, lambda e: e.copy(out=xbt[0:np_, :], in_=stg[0:np_, :]),
                 reads=stg_bufs, writes=[xbb])
        if defer:
            return lambda: xt_compute(s, xbt, xbb)
        xt_compute(s, xbt, xbb)

    def xt_compute(s, xbt, xbb):
        ps1, ps1_b = ps_alloc()
        pv = ps1[:].bitcast(BF16)
        if s == "h":
            def fn_t(e):
                last = None
                for k in range(8):
                    last = e.transpose(out=pv[:, k * HALO:(k + 1) * HALO],
                                       in_=xbt[0:HALO, k * 128:(k + 1) * 128],
                                       identity=ident_b[0:HALO, 0:HALO])
                return last
            emit(S_pe, fn_t, reads=[xbb, identb_buf], writes=[ps1_b])
            emit(S_dve, lambda e: e.tensor_copy(
                out=xT[:, :, 0:HALO], in_=pv[:, 0:8 * HALO].rearrange("p (k t) -> p k t", k=8)),
                reads=[ps1_b], writes=[xTh_b])
            return

        def fn_t(e):
            last = None
            for k in range(8):
                last = e.transpose(out=pv[:, k * 128:(k + 1) * 128],
                                   in_=xbt[:, k * 128:(k + 1) * 128], identity=ident_b[:])
            return last
        emit(S_pe, fn_t, reads=[xbb, identb_buf], writes=[ps1_b])
        c0 = HALO + s * 128
        src_ap = pv.rearrange("p (k t) -> p k t", k=8)
        dst_ap = xT[:, :, c0:c0 + 128]
        if evac_flip[0] % 2 == 0:
            emit(S_dve, lambda e: e.tensor_copy(out=dst_ap, in_=src_ap),
                 reads=[ps1_b], writes=[xT_b[s][0], xT_b[s][1]])
        else:
            emit(S_act, lambda e: e.copy(out=dst_ap, in_=src_ap),
                 reads=[ps1_b], writes=[xT_b[s][0], xT_b[s][1]])
        evac_flip[0] += 1

    def xT_reads(b):
        r = []
        for s in range(4 * b, 4 * b + 4):
            r += xT_b[s]
        return r

    def p_family(sbi, g, slot):
        w = WINDOWS[g]
        for b in range(NBLK):
            bc0 = HALO + b * BLK
            ab = (fam_ctr[0] * NBLK + b) % 2
            a_t, a_b, ah_b = abuf[ab], abuf_b[ab], abuf_hb[ab]
            psa = [ps_alloc(), ps_alloc()]

            def fn_a(e, psa=psa, bc0=bc0):
                last = None
                for c in range(2):
                    for k in range(8):
                        last = e.matmul(psa[c][0][:], lhsT=win[slot][:, k, c * 128:(c + 1) * 128],
                                        rhs=xT[:, k, bc0:bc0 + BLK], start=(k == 0), stop=(k == 7))
                return last
            emit(S_pe, fn_a, reads=[win_hb[slot][0]] + xT_reads(b), writes=[psa[0][1], psa[1][1]])
            for c in range(2):
                emit(S_act, lambda e, c=c, psa=psa, a_t=a_t: e.copy(out=a_t[:, c, HALO:HALO + BLK],
                                                                    in_=psa[c][0][:]),
                     reads=[psa[c][1]], fills=[a_b])
            if b == 0:
                psh, psh_b = ps_alloc()

                def fn_ah(e, psh=psh):
                    last = None
                    for c in range(2):
                        for k in range(8):
                            last = e.matmul(psh[:, c * HALO:(c + 1) * HALO],
                                            lhsT=win[slot][:, k, c * 128:(c + 1) * 128],
                                            rhs=xT[:, k, 0:HALO], start=(k == 0), stop=(k == 7))
                    return last
                emit(S_pe, fn_ah, reads=[win_hb[slot][0], xTh_b], writes=[psh_b])
                emit(S_act, lambda e, psh=psh, a_t=a_t: e.copy(
                    out=a_t[:, :, 0:HALO], in_=psh[:, 0:2 * HALO].rearrange("p (c t) -> p c t", c=2)),
                    reads=[psh_b, ah_b], writes=[ah_b])
            else:
                oth = abuf[1 - ab]
                emit(S_pool, lambda e, a_t=a_t, oth=oth: e.tensor_copy(out=a_t[:, :, 0:HALO],
                                                                       in_=oth[:, :, BLK:BLK + HALO]),
                     reads=[abuf_b[1 - ab], ah_b], writes=[ah_b])
            yield
            if not (sbi == 0 and g == 0):
                flush_pending()
            sA, sB = f4b[0], f4b[1]
            chain = [(a_t, 1, sA), (sA, 2, sB), (sB, 4, sA), (sA, 8, sB)]
            lo = 0
            src_b = [a_b, ah_b]
            fin, fin_b = None, None
            for step in range(g + 1):
                src, sh, dst = chain[step]
                lo_new = lo + sh
                dst_b = f4b_b[step % 2]
                emit(S_dve, lambda e, src=src, dst=dst, lo=lo, lo_new=lo_new, sh=sh: e.tensor_tensor(
                    out=dst[:, :, lo_new:528], in0=src[:, :, lo_new:528], in1=src[:, :, lo:528 - sh],
                    op=ALU.add),
                    reads=src_b, writes=[dst_b])
                src_b = [dst_b]
                lo = lo_new
                fin, fin_b = dst, dst_b
            pl, pl_b = pooled[b % 2], pooled_b[b % 2]
            pl3 = pl[:].rearrange("p (c t) -> p c t", c=2)
            emit(S_dve, lambda e, fin=fin, a_t=a_t, pl3=pl3: e.scalar_tensor_tensor(
                out=pl3, in0=fin[:, :, HALO:528], scalar=1.0 / w, in1=a_t[:, :, HALO:528],
                op0=ALU.mult, op1=ALU.subtract),
                reads=[fin_b, a_b], writes=[pl_b])
            if sbi == 0 and b == 0:
                emit(S_dve, lambda e, fin=fin: e.tensor_tensor(
                    out=fix_t[:], in0=fin[:, :, HALO:2 * HALO],
                    in1=icnt[:, g * 16:(g + 1) * 16].unsqueeze(1).to_broadcast([128, 2, 16]),
                    op=ALU.mult),
                    reads=[fin_b, icnt_buf, const_b], writes=[const_b])
                emit(S_dve, lambda e, a_t=a_t, pl3=pl3: e.tensor_tensor(
                    out=pl3[:, :, 0:HALO], in0=fix_t[:], in1=a_t[:, :, HALO:2 * HALO], op=ALU.subtract),
                    reads=[const_b, a_b, pl_b], writes=[pl_b])
            psz = [ps_alloc(), ps_alloc()]

            def fn_z(e, psz=psz, bc0=bc0):
                last = None
                for c in range(2):
                    for k in range(8):
                        last = e.matmul(psz[c][0][:], lhsT=win[slot][:, k, 256 + c * 128:256 + (c + 1) * 128],
                                        rhs=xT[:, k, bc0:bc0 + BLK], start=(k == 0), stop=(k == 7))
                return last
            emit(S_pe, fn_z, reads=[win_hb[slot][1]] + xT_reads(b), writes=[psz[0][1], psz[1][1]])
            sz_t, sz_b = szb[b % 2], szb_b[b % 2]
            for c in range(2):
                emit(S_act, lambda e, c=c, psz=psz, sz_t=sz_t: e.activation(
                    out=sz_t[:, c * BLK:(c + 1) * BLK], in_=psz[c][0][:], func=AF.Silu),
                    reads=[psz[c][1]], fills=[sz_b])

            def pw(pl=pl, pl_b=pl_b, sz_t=sz_t, sz_b=sz_b, b=b):
                psw = [ps_alloc(), ps_alloc()]

                def fn_pw(e):
                    last = None
                    for dc in range(2):
                        for kk in range(2):
                            last = e.matmul(psw[dc][0][:], lhsT=wpool[:, g, kk, dc * 128:(dc + 1) * 128],
                                            rhs=pl[:, kk * BLK:(kk + 1) * BLK], start=(kk == 0), stop=(kk == 1))
                    return last
                emit(S_pe, fn_pw, reads=[wpool_b, pl_b], writes=[psw[0][1], psw[1][1]])
                for dc in range(2):
                    ck = 2 * g + dc
                    emit(S_dve, lambda e, dc=dc, ck=ck: e.scalar_tensor_tensor(
                        out=yT[:, ck, b * BLK:(b + 1) * BLK], in0=psw[dc][0][:],
                        scalar=vecs[:, VEC_PS + ck:VEC_PS + ck + 1], in1=sz_t[:, dc * BLK:(dc + 1) * BLK],
                        op0=ALU.mult, op1=ALU.mult),
                        reads=[psw[dc][1], vecs_buf, sz_b], writes=[yT_b[ck][b]])
            pending.append(pw)
            yield

    z_ctr = [0]
    szq_b = [Buf() for _ in range(4)]

    def z_family(sbi, fz, slot):
        for b in range(NBLK):
            bc0 = HALO + b * BLK
            for c in range(4):
                ck = 8 + 4 * fz + c
                psz, psz_b = ps_alloc()

                def fn_z(e, psz=psz, c=c, bc0=bc0):
                    last = None
                    for k in range(8):
                        last = e.matmul(psz[:], lhsT=win[slot][:, k, c * 128:(c + 1) * 128],
                                        rhs=xT[:, k, bc0:bc0 + BLK], start=(k == 0), stop=(k == 7))
                    return last
                emit(S_pe, fn_z, reads=[win_hb[slot][c // 2]] + xT_reads(b), writes=[psz_b])
                zi = z_ctr[0] % 4
                z_ctr[0] += 1
                zt, zt_b = szb[zi // 2], szq_b[zi]
                emit(S_act, lambda e, psz=psz, zt=zt, zi=zi: e.activation(
                    out=zt[:, (zi % 2) * BLK:(zi % 2 + 1) * BLK], in_=psz[:], func=AF.Silu),
                    reads=[psz_b], writes=[zt_b], fills=[szb_b[zi // 2]])
                emit(S_dve, lambda e, zt=zt, zi=zi, ck=ck, b=b: e.tensor_tensor(
                    out=yT[:, ck, b * BLK:(b + 1) * BLK], in0=yT[:, ck, b * BLK:(b + 1) * BLK],
                    in1=zt[:, (zi % 2) * BLK:(zi % 2 + 1) * BLK], op=ALU.mult),
                    reads=[zt_b, szb_b[zi // 2], yT_b[ck][b]], writes=[yT_b[ck][b]])
                if c == 2:
                    flush_pending()
                yield

    def s_family(sbi, h, slot):
        for b in range(NBLK):
            bc0 = HALO + b * BLK
            it = (fam_ctr[0] * NBLK + b) % 2
            gv_t, gv_b = f4c[it], f4c_b[it]
            vn_t, vn_b = vnb[it], vnb_b[it]
            gu_t, gu_b = gub[it], gub_b[it]
            t1_t, t1_b = f4b[it], f4b_b[it]
            psv = [ps_alloc(), ps_alloc()]

            def fn_v(e, psv=psv, bc0=bc0):
                last = None
                for s in range(4):
                    for k in range(8):
                        last = e.matmul(psv[s // 2][0][:, (s % 2) * 256:(s % 2 + 1) * 256],
                                        lhsT=xT[:, k, bc0 + s * 128:bc0 + (s + 1) * 128],
                                        rhs=win[slot][:, k, 256:512], start=(k == 0), stop=(k == 7))
                return last
            emit(S_pe, fn_v, reads=[win_hb[slot][1]] + xT_reads(b), writes=[psv[0][1], psv[1][1]])
            for i in range(2):
                emit(S_act, lambda e, i=i, psv=psv, gv_t=gv_t: e.activation(
                    out=gv_t[:, i * 512:(i + 1) * 512], in_=psv[i][0][:], func=AF.Gelu_apprx_tanh),
                    reads=[psv[i][1]], fills=[gv_b])
            for s in range(4):
                emit(S_dve, lambda e, s=s, gv_t=gv_t, it=it: e.bn_stats(
                    out=st6[it][:, s, :], in_=gv_t[:, s * 256:(s + 1) * 256]),
                    reads=[gv_b], fills=[stat_b[it]])
            for s in range(4):
                emit(S_dve, lambda e, s=s, it=it: e.bn_aggr(out=mv[it][:, s, :], in_=st6[it][:, s, :]),
                     reads=[stat_b[it]], fills=[mv_b[it]])
            emit(S_pool, lambda e, it=it: e.tensor_scalar(
                out=ve[it][:], in0=mv[it][:, :, 1], scalar1=LN_EPS, scalar2=1.0, op0=ALU.add, op1=ALU.mult),
                reads=[mv_b[it]], writes=[ve_b[it]])
            emit(S_pool, lambda e, it=it: e.tensor_tensor(
                out=rstd[it][:], in0=ve[it][:], in1=negh[:, 0:4], op=ALU.pow),
                reads=[ve_b[it], negh_buf], writes=[rstd_b[it]])
            for s in range(4):
                emit(S_dve, lambda e, s=s, gv_t=gv_t, vn_t=vn_t, it=it: e.tensor_scalar(
                    out=vn_t[:, s * 256:(s + 1) * 256], in0=gv_t[:, s * 256:(s + 1) * 256],
                    scalar1=mv[it][:, s, 0:1], scalar2=rstd[it][:, s:s + 1],
                    op0=ALU.subtract, op1=ALU.mult),
                    reads=[gv_b, mv_b[it], rstd_b[it]], fills=[vn_b])
            yield
            psu = [ps_alloc(), ps_alloc()]

            def fn_u(e, psu=psu, bc0=bc0):
                last = None
                for c in range(2):
                    for k in range(8):
                        last = e.matmul(psu[c][0][:], lhsT=win[slot][:, k, c * 128:(c + 1) * 128],
                                        rhs=xT[:, k, bc0:bc0 + BLK], start=(k == 0), stop=(k == 7))
                return last
            emit(S_pe, fn_u, reads=[win_hb[slot][0]] + xT_reads(b), writes=[psu[0][1], psu[1][1]])
            for c in range(2):
                emit(S_act, lambda e, c=c, psu=psu, gu_t=gu_t: e.activation(
                    out=gu_t[:, c * BLK:(c + 1) * BLK], in_=psu[c][0][:], func=AF.Gelu_apprx_tanh),
                    reads=[psu[c][1]], fills=[gu_b])
            flush_pending()

            def sjob(vn_t=vn_t, vn_b=vn_b, gu_t=gu_t, gu_b=gu_b, t1_t=t1_t, t1_b=t1_b, b=b):
                pss = [ps_alloc(), ps_alloc()]

                def fn_s(e):
                    last = None
                    for c in range(2):
                        for s in range(4):
                            last = e.matmul(pss[c][0][:, s * 128:(s + 1) * 128],
                                            lhsT=vn_t[:, s * 256 + c * 128:s * 256 + (c + 1) * 128],
                                            rhs=wmT[:, h, :], start=True, stop=True)
                    return last
                emit(S_pe, fn_s, reads=[vn_b, wmT_buf], writes=[pss[0][1], pss[1][1]])
                for c in range(2):
                    k = 2 * h + c
                    emit(S_dve, lambda e, c=c, k=k: e.scalar_tensor_tensor(
                        out=t1_t[:, c, 0:BLK].rearrange("p (s i) -> p s i", s=4),
                        in0=pss[c][0][:].rearrange("p (s i) -> p s i", s=4),
                        scalar=vecs[:, VEC_SG + k:VEC_SG + k + 1],
                        in1=cst[:, k:k + 1, :].to_broadcast([128, 4, 128]),
                        op0=ALU.mult, op1=ALU.add),
                        reads=[pss[c][1], vecs_buf, cst_buf], fills=[t1_b])
                ck = 8 + 2 * h
                emit(S_dve, lambda e: e.tensor_tensor(
                    out=yT[:, ck:ck + 2, b * BLK:(b + 1) * BLK], in0=t1_t[:, :, 0:BLK],
                    in1=gu_t[:].rearrange("p (c t) -> p c t", c=2), op=ALU.mult),
                    reads=[t1_b, gu_b], writes=[yT_b[ck][b], yT_b[ck + 1][b]])
            pending.append(sjob)
            yield

    late_pieces = []
    for q in range(4):
        late_pieces.append(lambda q=q: misc_dma(S_pool, wout[:, 4 * q:4 * q + 4, :], wout_v[:, 4 * q:4 * q + 4, :],
                                                [wout_bs[q]]))
    for q in range(2):
        late_pieces.append(lambda q=q: misc_dma(S_pool, wg[:, 4 * q:4 * q + 4, :], wg_v[:, 4 * q:4 * q + 4, :],
                                                [wg_bs[q]]))
    late_pieces.append(lambda: misc_dma(S_pool, wp[:], wp_d.rearrange("(k p) d -> p k d", p=128), [wp_b]))

    def phase1(sbi, first_xt_rest):
        for fi, f in enumerate(fam_list):
            slot = fam_ctr[0] % 2
            kind, i = f
            if kind == "P":
                gen = p_family(sbi, i, slot)
            elif kind == "Z":
                gen = z_family(sbi, i, slot)
            else:
                gen = s_family(sbi, i, slot)
            last_fam = (fi == len(fam_list) - 1)
            nsteps = 0
            for _ in gen:
                nsteps += 1
                if fi == 0 and nsteps == 1 and first_xt_rest:
                    for s_ in first_xt_rest:
                        xt_job(sbi, s_)
                if sbi == 0 and fi == 1 and nsteps == 1:
                    emit_const_b()
                if sbi == 0 and fi == 1 and nsteps == 2:
                    emit_const_c()
                    emit_bg2()
                yield
            emit_fam_load(fam_ctr[0] + 2)
            if FLAG_LATE_PIECES:
                if late_pieces:
                    late_pieces.pop(0)()
            elif fi == 1 and sbi == 0:
                while late_pieces:
                    late_pieces.pop(0)()
            fam_ctr[0] += 1
            if fi == 0 and sbi == 0:
                emit_const_compute()
        flush_pending()
        yield

    NCH = SB // 128
    out_toks = []

    def phase2(sbi):
        st = {}

        def loads(j):
            t0 = sbi * SB + j * 128
            rs = row_alloc()
            ps_ = j % 2

            def fn_x(e, h):
                e.dma_start(out=row[rs][:, :], in_=x_d[t0:t0 + 128, :]).then_inc(h, 16)
            emit(S_sp, fn_x, writes=[row_b[rs]], dma_sem=row_sem[rs])

            def fn_p(e, h):
                e.dma_start(out=pin[ps_][:, :], in_=p_d[t0:t0 + 128, :]).then_inc(h, 16)
            emit(S_sp, fn_p, writes=[pin_b[ps_]], dma_sem=pin_sem[ps_])
            st[j] = {"row": rs, "pin": ps_}

        def stage_m(j):
            rs, pi = st[j]["row"], st[j]["pin"]
            b = (j * 128) // BLK
            tc0 = j * 128
            pst, pst_b = ps_alloc()

            def fn_pt(e):
                last = None
                for kk in range(2):
                    last = e.transpose(out=pst[:, kk * 128:(kk + 1) * 128],
                                       in_=pin[pi][:, kk * 128:(kk + 1) * 128], identity=ident_f[:])
                return last
            emit(S_pe, fn_pt, reads=[pin_b[pi], ident_buf], writes=[pst_b])
            pts = j % 3
            emit(S_act, lambda e: e.copy(out=pT[pts][:].rearrange("p k t -> p (k t)"), in_=pst[:, 0:256]),
                 reads=[pst_b, pT_b[pts]], writes=[pT_b[pts]])
            psm = [ps_alloc(), ps_alloc()]

            for hf_ in range(2):
                def fn_m(e, hf=hf_):
                    last = None
                    for k in range(16):
                        last = e.matmul(psm[hf][0][:], lhsT=yT[:, k, tc0:tc0 + 128],
                                        rhs=wout[:, k, hf * 512:(hf + 1) * 512], start=(k == 0), stop=(k == 15))
                    return last
                emit(S_pe, fn_m, reads=wout_bs + [yT_b[k][b] for k in range(16)], writes=[psm[hf_][1]])
            i2 = j % 2
            i3 = j % 3
            for hf in range(2):
                emit(S_dve, lambda e, hf=hf: e.scalar_tensor_tensor(
                    out=row[rs][:, hf * 512:(hf + 1) * 512], in0=row[rs][:, hf * 512:(hf + 1) * 512],
                    scalar=ALPHA, in1=psm[hf][0][:], op0=ALU.mult, op1=ALU.add),
                    reads=[psm[hf][1], row_b[rs]], fills=[row_b[rs]])
            for hf in range(2):
                emit(S_dve, lambda e, hf=hf: e.bn_stats(out=st6p[i3][:, hf, :],
                                                        in_=row[rs][:, hf * 512:(hf + 1) * 512]),
                     reads=[row_b[rs]], fills=[statp_b[i3]])
            emit(S_dve, lambda e: e.bn_aggr(out=mvp[i3][:], in_=st6p[i3][:].rearrange("p a b -> p (a b)")),
                 reads=[statp_b[i3]], writes=[mvp_b[i3]])
            emit(S_pool, lambda e: e.tensor_scalar(
                out=vep[i3][:], in0=mvp[i3][:, 1:2], scalar1=LN_EPS, scalar2=1.0, op0=ALU.add, op1=ALU.mult),
                reads=[mvp_b[i3]], writes=[vep_b[i3]])
            emit(S_pool, lambda e: e.tensor_tensor(
                out=rstdp[i3][:], in0=vep[i3][:], in1=negh[:, 0:1], op=ALU.pow),
                reads=[vep_b[i3], negh_buf], writes=[rstdp_b[i3]])
            emit(S_dve, lambda e: e.tensor_scalar(
                out=xhat[i2][:], in0=row[rs][:], scalar1=mvp[i3][:, 0:1], scalar2=rstdp[i3][:, 0:1],
                op0=ALU.subtract, op1=ALU.mult),
                reads=[row_b[rs], mvp_b[i3], rstdp_b[i3]], writes=[xhat_b[i2]])
            st[j]["pT"] = pts
            st[j]["i2"] = i2
            st[j]["i3"] = i3

        def stage_t(j):
            i2 = st[j]["i2"]
            pst_, psb_b = ps_alloc()
            psb = pst_[:].bitcast(BF16)

            def fn_t(e):
                last = None
                for k in range(8):
                    last = e.transpose(out=psb[:, k * 128:(k + 1) * 128], in_=xhat[i2][:, k * 128:(k + 1) * 128],
                                       identity=ident_b[:])
                return last
            emit(S_pe, fn_t, reads=[xhat_b[i2], identb_buf], writes=[psb_b])
            fast_tail = (sbi == NSB - 1 and j >= NCH - 1)
            for k in range(8):
                if fast_tail and k >= 4:
                    emit(S_dve, lambda e, k=k: e.tensor_scalar(
                        out=xnT[i2][:, k * 128:(k + 1) * 128], in0=psb[:, k * 128:(k + 1) * 128],
                        scalar1=vecs[:, VEC_LG + k:VEC_LG + k + 1], scalar2=vecs[:, VEC_LB + k:VEC_LB + k + 1],
                        op0=ALU.mult, op1=ALU.add),
                        reads=[psb_b, vecs_buf], fills=[xnT_b[i2]])
                    continue
                emit(S_act, lambda e, k=k: e.activation(
                    out=xnT[i2][:, k * 128:(k + 1) * 128], in_=psb[:, k * 128:(k + 1) * 128], func=AF.Identity,
                    bias=vecs[:, VEC_LB + k:VEC_LB + k + 1], scale=vecs[:, VEC_LG + k:VEC_LG + k + 1]),
                    reads=[psb_b, vecs_buf], fills=[xnT_b[i2]])

        def stage_g(j):
            rs, i2, pts, i3 = st[j]["row"], st[j]["i2"], st[j]["pT"], st[j]["i3"]
            t0 = sbi * SB + j * 128
            g_t, g_b = f4c[i2], f4c_b[i2]
            psp = [ps_alloc(), ps_alloc()]

            def fn_pl(e):
                last = None
                for hf in range(2):
                    for kk in range(2):
                        last = e.matmul(psp[hf][0][:], lhsT=pT[pts][:, kk, :],
                                        rhs=wp[:, kk, hf * 512:(hf + 1) * 512], start=(kk == 0), stop=(kk == 1))
                return last
            emit(S_pe, fn_pl, reads=[pT_b[pts], wp_b], writes=[psp[0][1], psp[1][1]])
            psg = [ps_alloc(), ps_alloc()]

            for hf_ in range(2):
                def fn_g(e, hf=hf_):
                    for k in range(8):
                        e.matmul(psg[hf][0][:], lhsT=xnT[i2][:, k * 128:(k + 1) * 128],
                                 rhs=wg[:, k, hf * 512:(hf + 1) * 512], start=(k == 0), stop=False)
                    return e.matmul(psg[hf][0][:], lhsT=ones2[:, :], rhs=bg2[:, hf * 512:(hf + 1) * 512],
                                    start=False, stop=True)
                emit(S_pe, fn_g, reads=[xnT_b[i2], ones2_buf, bg2_buf] + wg_bs, writes=[psg[hf_][1]])
            for hf in range(2):
                emit(S_act, lambda e, hf=hf: e.activation(
                    out=g_t[:, hf * 512:(hf + 1) * 512], in_=psg[hf][0][:], func=AF.Sigmoid),
                    reads=[psg[hf][1]], fills=[g_b])
            emit(S_dve, lambda e: e.scalar_tensor_tensor(
                out=row[rs][:], in0=row[rs][:], scalar=mvp[i3][:, 0:1], in1=gbc[:],
                op0=ALU.subtract, op1=ALU.mult),
                reads=[row_b[rs], mvp_b[i3], gb_buf], writes=[row_b[rs]])
            emit(S_dve, lambda e: e.scalar_tensor_tensor(
                out=row[rs][:], in0=row[rs][:], scalar=rstdp[i3][:, 0:1], in1=bbc[:],
                op0=ALU.mult, op1=ALU.add),
                reads=[row_b[rs], rstdp_b[i3], gb_buf], writes=[row_b[rs]])
            for hf in range(2):
                emit(S_dve, lambda e, hf=hf: e.tensor_tensor(
                    out=g_t[:, hf * 512:(hf + 1) * 512], in0=g_t[:, hf * 512:(hf + 1) * 512],
                    in1=psp[hf][0][:], op=ALU.mult),
                    reads=[g_b, psp[hf][1]], fills=[g_b])
            fin_eng = S_dve
            emit(fin_eng, lambda e: e.tensor_tensor(out=row[rs][:], in0=row[rs][:], in1=g_t[:], op=ALU.add),
                 reads=[row_b[rs], g_b], writes=[row_b[rs]])

            def fn_st(e, h):
                e.dma_start(out=out_d[t0:t0 + 128, :], in_=row[rs][:, :]).then_inc(h, 16)
            out_toks.append(emit(S_sp, fn_st, reads=[row_b[rs]], dma_sem=row_sem[rs]))

        loads(0)
        for j in range(NCH + 1):
            xt_def = []
            if sbi + 1 < NSB and j <= 6:
                todo = {0: ["h", 0], 1: [1, 2]}.get(j, [j + 1])
                for s_ in todo:
                    xt_def.append(xt_job(sbi + 1, s_, defer=True))
            if j + 1 < NCH:
                loads(j + 1)
            if j == 0:
                stage_m(0)
                yield j
                stage_m(1)
                yield j
            elif 2 <= j < NCH:
                stage_m(j)
                yield j
            if 1 <= j:
                stage_g(j - 1)
                yield j
            if j < NCH:
                stage_t(j)
                yield j
            for f_ in xt_def:
                f_()

    warm_b = Buf()
    emit(S_pool, lambda e: e.memset(warm[:], 0.0), writes=[warm_b])
    emit(S_act, lambda e: e.activation(out=warm[:], in_=warm[:], func=AF.Silu), writes=[warm_b])
    emit_fam_load(0)
    defs_ = [xt_job(0, s_, defer=True) for s_ in ["h", 0, 1, 2, 3]]
    for f_ in defs_:
        f_()
    misc_dma(S_pool, wpool[:], poolw_d.rearrange("g (kk p) d -> p g kk d", p=128), [wpool_b])
    emit_fam_load(1)
    emit_small_const_loads()
    if not FLAG_CONST_LATE:
        emit_const_compute()
        emit_bg2()
    g1 = phase1(0, [4, 5, 6, 7])
    for _ in g1:
        pass
    for sbi in range(NSB):
        if dbg and sbi == 0:
            def fn_dbg(e, h):
                e.dma_start(out=dbg_y, in_=yT[:].rearrange("p k t -> p (k t)")).then_inc(h, 16)
            out_toks.append(emit(S_sp, fn_dbg, reads=[yT_b[k][b] for k in range(16) for b in range(NBLK)],
                                 dma_sem=misc_sem[misc_ctr[0]]))
            misc_ctr[0] += 1
        g2 = phase2(sbi)
        g1n = phase1(sbi + 1, None) if sbi + 1 < NSB else None
        budget = 0
        for j in g2:
            if g1n is not None and j >= NCH - 1 and budget < INTERLEAVE_STEPS:
                budget += 1
                if next(g1n, "done") == "done":
                    g1n = None
        if g1n is not None:
            for _ in g1n:
                pass

    fin = {}
    for tok in out_toks:
        _merge(fin, tok)
    for s, v in fin.items():
        S_sp.items.append(("wait", s, v))

    with nc.Block() as block:
        @block.tensor
        def _(e):
            replay(S_pe, e)

        @block.scalar
        def _(e):
            replay(S_act, e)

        @block.vector
        def _(e):
            replay(S_dve, e)

        @block.gpsimd
        def _(e):
            replay(S_pool, e)

        @block.sync
        def _(e):
            replay(S_sp, e)
    es.close()
    return nc


def _chunkT(v):
    return np.ascontiguousarray(np.asarray(v, dtype=np.float32).reshape(-1, 128).T)


def make_in_maps(x, p, w_in, pool_w, pool_scale, sgu_ln_g, sgu_ln_b, sgu_w, sgu_b,
                 w_out, ln_g, ln_b, ple_w, ple_gate_w, ple_gate_b):
    f = np.float32
    x = np.asarray(x, f)
    p = np.asarray(p, f)[0]
    vecs = np.concatenate([_chunkT(pool_scale[0]), _chunkT(sgu_ln_g[0]), _chunkT(sgu_ln_b[0]),
                           _chunkT(ln_g[0]), _chunkT(ln_b[0])], axis=1)
    shared = {
        "w_in": np.ascontiguousarray(np.asarray(w_in, f)[0]),
        "pool_w": np.ascontiguousarray(np.asarray(pool_w, f)[0]),
        "w_out": np.ascontiguousarray(np.asarray(w_out, f)[0]),
        "ple_w": np.ascontiguousarray(np.asarray(ple_w, f)[0]),
        "ple_gate_w": np.ascontiguousarray(np.asarray(ple_gate_w, f)[0]),
        "ple_gate_b": np.ascontiguousarray(np.asarray(ple_gate_b, f)[0].reshape(1, D)),
        "vecs": np.ascontiguousarray(vecs),
        "gbc": np.ascontiguousarray(np.broadcast_to(np.asarray(ln_g, f)[0][None, :], (128, D))),
        "bbc": np.ascontiguousarray(np.broadcast_to(np.asarray(ln_b, f)[0][None, :], (128, D))),
        "bsb": np.ascontiguousarray(np.broadcast_to(np.asarray(sgu_b, f)[0].reshape(1, 512), (128, 512))),
        "tril": np.tril(np.ones((128, 128), f)),
        "ident": np.eye(128, dtype=f),
        "sgw": np.ascontiguousarray(np.asarray(sgu_w, f)[0].transpose(1, 0, 2).reshape(128, 512)),
    }
    in_maps = []
    for c in range(N_CORES):
        b, q = divmod(c, 4)
        t0 = q * TOK
        xc = np.ascontiguousarray(x[b, t0:t0 + TOK])
        if q == 0:
            xh = np.zeros((HALO, D), f)
        else:
            xh = np.ascontiguousarray(x[b, t0 - HALO:t0])
        ic = np.zeros((4, 16), f)
        for g, w in enumerate(WINDOWS):
            for t in range(16):
                ic[g, t] = 1.0 / (min(t + 1, w) if q == 0 else w)
        m = dict(shared)
        m["x"] = xc
        m["xh"] = xh
        m["p"] = np.ascontiguousarray(p[b, t0:t0 + TOK])
        m["icnt"] = np.ascontiguousarray(np.broadcast_to(ic.reshape(1, 64), (128, 64)))
        in_maps.append(m)
    return in_maps


_NC_CACHE = {}


def kernel(**inputs):
    in_maps = make_in_maps(**inputs)
    if "nc" not in _NC_CACHE:
        _NC_CACHE["nc"] = build_program()
    nc = _NC_CACHE["nc"]
    res = run_bass_kernel_spmd(nc, in_maps, core_ids=list(range(N_CORES)))
    out = np.empty((2, 4 * TOK, D), np.float32)
    for c in range(N_CORES):
        b, q = divmod(c, 4)
        out[b, q * TOK:(q + 1) * TOK] = res.results[c]["out"]
    return out
```

```python
import numpy as np
from contextlib import ExitStack

import concourse.bass as bass
import concourse.mybir as mybir
from concourse.bass_utils import run_bass_kernel_spmd

F32 = mybir.dt.float32
BF16 = mybir.dt.bfloat16
AF = mybir.ActivationFunctionType
ALU = mybir.AluOpType

N_CORES = 8
D = 1024
TOK = 2048
SB = 1024
NSB = TOK // SB
BLK = 512
NBLK = SB // BLK
HALO = 16
XTW = HALO + SB
ALPHA = 2.0 ** 0.25
LN_EPS = 1e-5
WINDOWS = (2, 4, 8, 16)
INTERLEAVE_STEPS = 6
import os
FLAG_LATE_PIECES = os.environ.get('K_LATE', '1') == '1'
FLAG_CONST_LATE = os.environ.get('K_CONST', '1') == '1'
FLAG_XT_EARLY = os.environ.get('K_XT', '1') == '1'


class Sem:
    def __init__(self, handle, step):
        self.h = handle
        self.step = step
        self.val = 0

    def advance(self, n=1):
        self.val += self.step * n
        return (self, self.val)


class Buf:
    __slots__ = ("w", "r", "name", "psum")

    def __init__(self, name="", psum=False):
        self.w = {}
        self.r = {}
        self.name = name
        self.psum = psum


class Stream:
    def __init__(self, name, prog, is_pe=False):
        self.name = name
        self.prog = prog
        self.items = []
        self.waited = {}
        self.is_pe = is_pe


def _merge(d, tok):
    s, v = tok
    if d.get(s, 0) < v:
        d[s] = v


def emit(stream, fn, reads=(), writes=(), dma_sem=None, n_dma=1, fills=()):
    deps = {}
    for b in fills:
        for s, v in b.r.items():
            _merge(deps, (s, v))
        for s, v in b.w.items():
            if s is not stream.prog:
                _merge(deps, (s, v))
    for b in reads:
        for s, v in b.w.items():
            _merge(deps, (s, v))
        if b.psum:
            for s, v in b.r.items():
                if s is not stream.prog:
                    _merge(deps, (s, v))
    for b in writes:
        for s, v in b.w.items():
            _merge(deps, (s, v))
        for s, v in b.r.items():
            _merge(deps, (s, v))
    for s, v in deps.items():
        if stream.is_pe and s is stream.prog:
            continue
        if stream.waited.get(s, 0) >= v:
            continue
        stream.waited[s] = v
        stream.items.append(("wait", s, v))
    if dma_sem is not None:
        tok = dma_sem.advance(n_dma)
        stream.items.append(("dma", fn, dma_sem))
    else:
        tok = stream.prog.advance()
        stream.items.append(("op", fn, stream.prog))
    for b in reads:
        _merge(b.r, tok)
    for b in writes:
        b.w = {tok[0]: tok[1]}
        b.r = {}
    for b in fills:
        _merge(b.w, tok)
    return tok


def replay(stream, eng):
    for it in stream.items:
        if it[0] == "wait":
            eng.wait_ge(it[1].h, it[2])
        elif it[0] == "op":
            inst = it[1](eng)
            inst.then_inc(it[2].h, 1)
        else:
            it[1](eng, it[2].h)


def build_program(dbg=False):
    nc = bass.Bass("TRN2", target_bir_lowering=False)
    es = ExitStack()

    def dram_in(name, shape):
        return nc.dram_tensor(name, list(shape), F32, kind="ExternalInput").ap()

    x_d = dram_in("x", [TOK, D])
    xh_d = dram_in("xh", [HALO, D])
    p_d = dram_in("p", [TOK, 256])
    icnt_d = dram_in("icnt", [128, 4 * 16])
    win_d = dram_in("w_in", [D, 5120])
    poolw_d = dram_in("pool_w", [4, 256, 256])
    wout_d = dram_in("w_out", [2048, D])
    wp_d = dram_in("ple_w", [256, D])
    wg_d = dram_in("ple_gate_w", [D, D])
    bg_d = dram_in("ple_gate_b", [1, D])
    vecs_d = dram_in("vecs", [128, 40])
    gbc_d = dram_in("gbc", [128, D])
    bbc_d = dram_in("bbc", [128, D])
    bsb_d = dram_in("bsb", [128, 512])
    tril_d = dram_in("tril", [128, 128])
    ident_d = dram_in("ident", [128, 128])
    sgw_d = dram_in("sgw", [128, 512])
    out_d = nc.dram_tensor("out", [TOK, D], F32, kind="ExternalOutput").ap()
    if dbg:
        dbg_y = nc.dram_tensor("dbg_y", [128, 16 * SB], BF16, kind="ExternalOutput").ap()

    def sb_t(name, shape, dt):
        return es.enter_context(nc.sbuf_tensor("s_" + name, list(shape), dt))

    def ps_t(name, shape, dt):
        return es.enter_context(nc.psum_tensor(name, list(shape), dt))

    def new_sem(name, step):
        return Sem(es.enter_context(nc.semaphore(name)), step)

    S_pe = Stream("pe", new_sem("p_pe", 1), is_pe=True)
    S_act = Stream("act", new_sem("p_act", 1))
    S_dve = Stream("dve", new_sem("p_dve", 1))
    S_pool = Stream("pool", new_sem("p_pool", 1))
    S_sp = Stream("sp", new_sem("p_sp", 1))

    win = [sb_t(f"win{i}", [128, 8, 512], BF16) for i in range(2)]
    win_hb = [[Buf(f"win{i}a"), Buf(f"win{i}b")] for i in range(2)]
    win_sem = [[new_sem(f"d_win{i}a", 16), new_sem(f"d_win{i}b", 16)] for i in range(2)]
    wpool = sb_t("wpool", [128, 4, 2, 256], BF16)
    wout = sb_t("wout", [128, 16, D], BF16)
    wg = sb_t("wg", [128, 8, D], BF16)
    wp = sb_t("wp", [128, 2, D], BF16)
    wmT = sb_t("wmT", [128, 4, 128], BF16)
    xT = sb_t("xT", [128, 8, XTW], BF16)
    yT = sb_t("yT", [128, 16, SB], BF16)
    NROW = 4
    row = [sb_t(f"row{i}", [128, D], F32) for i in range(NROW)]
    row_b = [Buf(f"row{i}") for i in range(NROW)]
    row_sem = [new_sem(f"d_row{i}", 16) for i in range(NROW)]
    abuf = [sb_t(f"abuf{i}", [128, 2, 528], F32) for i in range(2)]
    abuf_b = [Buf() for _ in range(2)]
    abuf_hb = [Buf() for _ in range(2)]
    abuf_sem = [new_sem(f"d_abuf{i}", 16) for i in range(2)]
    f4b = [sb_t(f"f4b{i}", [128, 2, 528], F32) for i in range(2)]
    f4b_b = [Buf() for _ in range(2)]
    f4c = [sb_t(f"f4c{i}", [128, D], F32) for i in range(2)]
    f4c_b = [Buf() for _ in range(2)]
    b2 = [sb_t(f"b2_{i}", [128, D], BF16) for i in range(8)]
    b2_b = [Buf() for _ in range(8)]
    pooled, pooled_b = b2[0:2], b2_b[0:2]
    szb, szb_b = b2[2:4], b2_b[2:4]
    gub, gub_b = b2[4:6], b2_b[4:6]
    vnb, vnb_b = b2[6:8], b2_b[6:8]
    xhat, xhat_b = b2[4:6], b2_b[4:6]
    xnT, xnT_b = b2[6:8], b2_b[6:8]
    NXB = 2
    xb = [sb_t(f"xb{i}", [128, D], BF16) for i in range(NXB)]
    xb_b = [Buf() for _ in range(NXB)]
    xb_sem = [new_sem(f"d_xb{i}", 16) for i in range(NXB)]
    pin = [sb_t(f"pin{i}", [128, 256], F32) for i in range(2)]
    pin_b = [Buf() for _ in range(2)]
    pin_sem = [new_sem(f"d_pin{i}", 16) for i in range(2)]
    pT = [sb_t(f"pT{i}", [128, 2, 128], BF16) for i in range(3)]
    pT_b = [Buf() for _ in range(3)]
    ident_f = sb_t("ident_f", [128, 128], F32)
    ident_b = sb_t("ident_b", [128, 128], BF16)
    vecs = sb_t("vecs", [128, 40], F32)
    cst = sb_t("cst", [128, 8, 128], F32)
    gbc = sb_t("gbc", [128, D], F32)
    bbc = sb_t("bbc", [128, D], F32)
    bsb = sb_t("bsb", [128, 512], F32)
    tril = sb_t("tril", [128, 128], F32)
    icnt = sb_t("icnt", [128, 64], F32)
    ones_f = sb_t("ones_f", [128, 128], F32)
    ones2 = sb_t("ones2", [128, 128], BF16)
    bg2 = sb_t("bg2", [128, D], BF16)
    bgf = f4c[0][0:1, :]
    bgt = f4c[1][0:1, :]
    bghi = b2[6][0:1, :]
    bglo = b2[7][0:1, :]
    wmTf2 = b2[5][:].bitcast(F32)
    sgw2 = b2[4][:].bitcast(F32)
    negh = sb_t("negh", [128, 8], F32)
    warm = sb_t("warm", [128, 2], F32)
    fix_t = sb_t("fix_t", [128, 2, 16], F32)
    st6 = [sb_t(f"st6_{i}", [128, 4, 6], F32) for i in range(2)]
    mv = [sb_t(f"mv{i}", [128, 4, 2], F32) for i in range(2)]
    ve = [sb_t(f"ve{i}", [128, 4], F32) for i in range(2)]
    rstd = [sb_t(f"rstd{i}", [128, 4], F32) for i in range(2)]
    stat_b = [Buf() for _ in range(2)]
    mv_b = [Buf() for _ in range(2)]
    ve_b = [Buf() for _ in range(2)]
    rstd_b = [Buf() for _ in range(2)]
    st6p = [sb_t(f"st6p{i}", [128, 2, 6], F32) for i in range(3)]
    mvp = [sb_t(f"mvp{i}", [128, 2], F32) for i in range(3)]
    vep = [sb_t(f"vep{i}", [128, 1], F32) for i in range(3)]
    rstdp = [sb_t(f"rstdp{i}", [128, 1], F32) for i in range(3)]
    statp_b = [Buf() for _ in range(3)]
    mvp_b = [Buf() for _ in range(3)]
    vep_b = [Buf() for _ in range(3)]
    rstdp_b = [Buf() for _ in range(3)]

    VEC_PS, VEC_SG, VEC_SB, VEC_LG, VEC_LB = 0, 8, 16, 24, 32

    NPS = 8
    psf = [ps_t(f"psf{i}", [128, 512], F32) for i in range(NPS)]
    psf_b = [Buf(f"psf{i}", psum=True) for i in range(NPS)]
    ps_ctr = [0]

    def ps_alloc():
        i = ps_ctr[0] % NPS
        ps_ctr[0] += 1
        return psf[i], psf_b[i]

    const_b = Buf("consts")
    xT_b = [[Buf() for _ in range(2)] for _ in range(8)]
    xTh_b = Buf()
    yT_b = [[Buf() for _ in range(NBLK)] for _ in range(16)]
    wpool_b, wp_b = Buf(), Buf()
    misc_sem = [new_sem(f"d_misc{i}", 16) for i in range(26)]
    misc_ctr = [0]

    def misc_dma(stream, out_ap, in_ap, writes, reads=()):
        sem = misc_sem[misc_ctr[0]]
        misc_ctr[0] += 1

        def fn(e, h):
            e.dma_start(out=out_ap, in_=in_ap).then_inc(h, 16)
        return emit(stream, fn, reads=reads, writes=writes, dma_sem=sem)

    win_v = win_d.rearrange("(k p) e -> p k e", p=128)
    fam_list = [("P", 0), ("P", 1), ("P", 2), ("P", 3),
                ("S", 0), ("S", 1), ("S", 2), ("S", 3), ("Z", 0), ("Z", 1)]
    all_fams = [(sbi, f) for sbi in range(NSB) for f in fam_list]

    def fam_cols(f):
        kind, i = f
        if kind == "P":
            return [(i * 256, 256), (3072 + i * 256, 256)]
        if kind == "Z":
            return [(3072 + 1024 + i * 512, 256), (3072 + 1024 + i * 512 + 256, 256)]
        return [(1024 + i * 256, 256), (2048 + i * 256, 256)]

    def emit_fam_load(fidx):
        if fidx >= len(all_fams):
            return
        slot = fidx % 2
        cols = fam_cols(all_fams[fidx][1])

        for hf, (c0, n) in enumerate(cols):
            def fn(e, h, hf=hf, c0=c0, n=n):
                e.dma_start(out=win[slot][:, :, hf * 256:hf * 256 + n], in_=win_v[:, :, c0:c0 + n]).then_inc(h, 16)
            extra = [win_hb[0][0]] if (fidx == 0 and hf == 1) else []
            emit(S_pool, fn, reads=extra, writes=[win_hb[slot][hf]], dma_sem=win_sem[slot][hf])

    ident_buf, vecs_buf, tril_buf, bsb_buf, icnt_buf, gb_buf = (Buf() for _ in range(6))
    sgw_buf = b2_b[4]
    wmTf_buf = b2_b[5]
    bgf_buf, bgt_buf, bghi_buf, bglo_buf = f4c_b[0], f4c_b[1], b2_b[6], b2_b[7]
    identb_buf = Buf()
    misc_dma(S_sp, ident_f[:], ident_d, [ident_buf])
    misc_dma(S_pool, ident_b[:], ident_d, [identb_buf])

    def emit_small_const_loads():
        misc_dma(S_sp, vecs[:], vecs_d, [vecs_buf])
        misc_dma(S_sp, icnt[:], icnt_d, [icnt_buf])

    ones_buf, negh_buf, ones2_buf, wmT_buf, cst_buf, bg2_buf = (Buf() for _ in range(6))

    def emit_const_compute():
        misc_dma(S_sp, tril[:], tril_d, [tril_buf])
        misc_dma(S_sp, sgw2, sgw_d, [sgw_buf])
        misc_dma(S_sp, bsb[:], bsb_d, [bsb_buf])
        misc_dma(S_sp, bgf, bg_d, [bgf_buf])
        misc_dma(S_sp, gbc[:], gbc_d, [gb_buf])
        misc_dma(S_sp, bbc[:], bbc_d, [gb_buf], reads=[gb_buf])
        emit(S_dve, lambda e: e.memset(ones_f[:], 1.0), writes=[ones_buf])
        emit(S_dve, lambda e: e.memset(negh[:], -0.5), writes=[negh_buf])
        emit(S_dve, lambda e: e.memset(ones2[:], 0.0), writes=[ones2_buf])
        emit(S_dve, lambda e: e.memset(ones2[0:2, :], 1.0), reads=[ones2_buf], writes=[ones2_buf])
        emit(S_pool, lambda e: e.memset(bg2[:], 0.0), writes=[bg2_buf])

        sgw3 = sgw2.rearrange("p (h j) -> p h j", h=4)
        emit(S_dve, lambda e: e.tensor_tensor(out=sgw3, in0=sgw3,
                                              in1=tril[:].unsqueeze(1).to_broadcast([128, 4, 128]),
                                              op=ALU.mult),
             reads=[tril_buf], writes=[sgw_buf])

    ps_w_box = []

    def emit_const_b():
        ps_w, ps_w_b = ps_alloc()
        ps_w_box.append((ps_w, ps_w_b))

        def fn_wmT(e):
            last = None
            for h in range(4):
                last = e.transpose(out=ps_w[:, h * 128:(h + 1) * 128], in_=sgw2[:, h * 128:(h + 1) * 128],
                                   identity=ident_f[:])
            return last
        emit(S_pe, fn_wmT, reads=[sgw_buf, ident_buf], writes=[ps_w_b])
        emit(S_act, lambda e: e.copy(out=wmT[:].rearrange("p h i -> p (h i)"), in_=ps_w[:]),
             reads=[ps_w_b], writes=[wmT_buf])
        emit(S_dve, lambda e: e.tensor_copy(out=wmTf2, in_=ps_w[:]),
             reads=[ps_w_b], writes=[wmTf_buf])

    def emit_const_c():
        ps_r, ps_r_b = ps_alloc()
        emit(S_pe, lambda e: e.matmul(ps_r[:], lhsT=ones_f[:], rhs=wmTf2, start=True, stop=True),
             reads=[ones_buf, wmTf_buf], writes=[ps_r_b])
        for k in range(8):
            h = k // 2
            emit(S_dve, lambda e, k=k, h=h: e.scalar_tensor_tensor(
                out=cst[:, k, :], in0=ps_r[:, h * 128:(h + 1) * 128],
                scalar=vecs[:, VEC_SB + k:VEC_SB + k + 1], in1=bsb[:, h * 128:(h + 1) * 128],
                op0=ALU.mult, op1=ALU.add),
                reads=[ps_r_b, vecs_buf, bsb_buf], fills=[cst_buf])
        emit(S_dve, lambda e: e.tensor_copy(out=bghi, in_=bgf), reads=[bgf_buf], writes=[bghi_buf])
        emit(S_dve, lambda e: e.tensor_copy(out=bgt, in_=bghi), reads=[bghi_buf], writes=[bgt_buf])
        emit(S_dve, lambda e: e.tensor_tensor(out=bgt, in0=bgf, in1=bgt, op=ALU.subtract),
             reads=[bgf_buf, bgt_buf], writes=[bgt_buf])
        emit(S_dve, lambda e: e.tensor_copy(out=bglo, in_=bgt), reads=[bgt_buf], writes=[bglo_buf])

    def emit_bg2():
        misc_dma(S_sp, bg2[0:1, :], bghi, [bg2_buf], reads=[bghi_buf, bg2_buf])
        misc_dma(S_sp, bg2[1:2, :], bglo, [bg2_buf], reads=[bglo_buf, bg2_buf])

    wout_bs = [Buf() for _ in range(4)]
    wg_bs = [Buf() for _ in range(2)]
    wout_v = wout_d.rearrange("(k p) d -> p k d", p=128)
    wg_v = wg_d.rearrange("(k p) d -> p k d", p=128)

    row_ctr = [0]

    def row_alloc():
        i = row_ctr[0] % NROW
        row_ctr[0] += 1
        return i

    pending = []

    def flush_pending():
        while pending:
            pending.pop(0)()

    fam_ctr = [0]
    evac_flip = [0]

    xb_ctr = [0]

    stage_ctr = [0]

    def xt_job(sbi, s, defer=False):
        if sbi == 0:
            pool_ = [(xb[0], xb_b[0]), (xb[1], xb_b[1]), (b2[4], b2_b[4]), (b2[5], b2_b[5]),
                     (b2[6], b2_b[6]), (b2[7], b2_b[7])]
        else:
            pool_ = [(xb[0], xb_b[0]), (xb[1], xb_b[1])]
        xbt, xbb = pool_[xb_ctr[0] % len(pool_)]
        xb_ctr[0] += 1
        si = stage_ctr[0]
        stage_ctr[0] += 1
        if sbi == 0:
            sidx = si % NROW
            stg, stg_bufs, stg_sem = row[sidx][:, :], [row_b[sidx]], row_sem[sidx]
        else:
            sidx = si % 2
            stg = abuf[sidx][:].rearrange("p c t -> p (c t)")[:, 0:D]
            stg_bufs, stg_sem = [abuf_b[sidx], abuf_hb[sidx]], abuf_sem[sidx]
        if s == "h":
            src = xh_d if sbi == 0 else x_d[sbi * SB - HALO:sbi * SB, :]
            np_ = HALO
        else:
            t0 = sbi * SB + s * 128
            src = x_d[t0:t0 + 128, :]
            np_ = 128

        def fn_ld(e, h):
            e.dma_start(out=stg[0:np_, :], in_=src).then_inc(h, 16)
        gate = [win_hb[0][0]] if (sbi == 0 and s in (4, 5, 6, 7)) else []
        emit(S_sp, fn_ld, reads=gate, writes=stg_bufs, dma_sem=stg_sem)
        if sbi == 0:
            emit(S_dve, lambda e: e.tensor_copy(out=xbt[0:np_, :], in_=stg[0:np_, :]),
                 reads=stg_bufs, writes=[xbb])
        else:
            emit(S_act, lambda e: e.copy(out=xbt[0:np_, :], in_=stg[0:np_, :]),
                 reads=stg_bufs, writes=[xbb])
        if defer:
            return lambda: xt_compute(s, xbt, xbb)
        xt_compute(s, xbt, xbb)

    def xt_compute(s, xbt, xbb):
        ps1, ps1_b = ps_alloc()
        pv = ps1[:].bitcast(BF16)
        if s == "h":
            def fn_t(e):
                last = None
                for k in range(8):
                    last = e.transpose(out=pv[:, k * HALO:(k + 1) * HALO],
                                       in_=xbt[0:HALO, k * 128:(k + 1) * 128],
                                       identity=ident_b[0:HALO, 0:HALO])
                return last
            emit(S_pe, fn_t, reads=[xbb, identb_buf], writes=[ps1_b])
            emit(S_dve, lambda e: e.tensor_copy(
                out=xT[:, :, 0:HALO], in_=pv[:, 0:8 * HALO].rearrange("p (k t) -> p k t", k=8)),
                reads=[ps1_b], writes=[xTh_b])
            return

        def fn_t(e):
            last = None
            for k in range(8):
                last = e.transpose(out=pv[:, k * 128:(k + 1) * 128],
                                   in_=xbt[:, k * 128:(k + 1) * 128], identity=ident_b[:])
            return last
        emit(S_pe, fn_t, reads=[xbb, identb_buf], writes=[ps1_b])
        c0 = HALO + s * 128
        src_ap = pv.rearrange("p (k t) -> p k t", k=8)
        dst_ap = xT[:, :, c0:c0 + 128]
        if evac_flip[0] % 2 == 0:
            emit(S_dve, lambda e: e.tensor_copy(out=dst_ap, in_=src_ap),
                 reads=[ps1_b], writes=[xT_b[s][0], xT_b[s][1]])
        else:
            emit(S_act, lambda e: e.copy(out=dst_ap, in_=src_ap),
                 reads=[ps1_b], writes=[xT_b[s][0], xT_b[s][1]])
        evac_flip[0] += 1

    def xT_reads(b):
        r = []
        for s in range(4 * b, 4 * b + 4):
            r += xT_b[s]
        return r

    def p_family(sbi, g, slot):
        w = WINDOWS[g]
        for b in range(NBLK):
            bc0 = HALO + b * BLK
            ab = (fam_ctr[0] * NBLK + b) % 2
            a_t, a_b, ah_b = abuf[ab], abuf_b[ab], abuf_hb[ab]
            psa = [ps_alloc(), ps_alloc()]

            def fn_a(e, psa=psa, bc0=bc0):
                last = None
                for c in range(2):
                    for k in range(8):
                        last = e.matmul(psa[c][0][:], lhsT=win[slot][:, k, c * 128:(c + 1) * 128],
                                        rhs=xT[:, k, bc0:bc0 + BLK], start=(k == 0), stop=(k == 7))
                return last
            emit(S_pe, fn_a, reads=[win_hb[slot][0]] + xT_reads(b), writes=[psa[0][1], psa[1][1]])
            for c in range(2):
                emit(S_act, lambda e, c=c, psa=psa, a_t=a_t: e.copy(out=a_t[:, c, HALO:HALO + BLK],
                                                                    in_=psa[c][0][:]),
                     reads=[psa[c][1]], fills=[a_b])
            if b == 0:
                psh, psh_b = ps_alloc()

                def fn_ah(e, psh=psh):
                    last = None
                    for c in range(2):
                        for k in range(8):
                            last = e.matmul(psh[:, c * HALO:(c + 1) * HALO],
                                            lhsT=win[slot][:, k, c * 128:(c + 1) * 128],
                                            rhs=xT[:, k, 0:HALO], start=(k == 0), stop=(k == 7))
                    return last
                emit(S_pe, fn_ah, reads=[win_hb[slot][0], xTh_b], writes=[psh_b])
                emit(S_act, lambda e, psh=psh, a_t=a_t: e.copy(
                    out=a_t[:, :, 0:HALO], in_=psh[:, 0:2 * HALO].rearrange("p (c t) -> p c t", c=2)),
                    reads=[psh_b, ah_b], writes=[ah_b])
            else:
                oth = abuf[1 - ab]
                emit(S_pool, lambda e, a_t=a_t, oth=oth: e.tensor_copy(out=a_t[:, :, 0:HALO],
                                                                       in_=oth[:, :, BLK:BLK + HALO]),
                     reads=[abuf_b[1 - ab], ah_b], writes=[ah_b])
            yield
            if len(pending) >= 2:
                pending.pop(0)()
            sA, sB = f4b[0], f4b[1]
            chain = [(a_t, 1, sA), (sA, 2, sB), (sB, 4, sA), (sA, 8, sB)]
            lo = 0
            src_b = [a_b, ah_b]
            fin, fin_b = None, None
            for step in range(g + 1):
                src, sh, dst = chain[step]
                lo_new = lo + sh
                dst_b = f4b_b[step % 2]
                emit(S_dve, lambda e, src=src, dst=dst, lo=lo, lo_new=lo_new, sh=sh: e.tensor_tensor(
                    out=dst[:, :, lo_new:528], in0=src[:, :, lo_new:528], in1=src[:, :, lo:528 - sh],
                    op=ALU.add),
                    reads=src_b, writes=[dst_b])
                src_b = [dst_b]
                lo = lo_new
                fin, fin_b = dst, dst_b
            pl, pl_b = pooled[b % 2], pooled_b[b % 2]
            pl3 = pl[:].rearrange("p (c t) -> p c t", c=2)
            emit(S_dve, lambda e, fin=fin, a_t=a_t, pl3=pl3: e.scalar_tensor_tensor(
                out=pl3, in0=fin[:, :, HALO:528], scalar=1.0 / w, in1=a_t[:, :, HALO:528],
                op0=ALU.mult, op1=ALU.subtract),
                reads=[fin_b, a_b], writes=[pl_b])
            if sbi == 0 and b == 0:
                emit(S_dve, lambda e, fin=fin: e.tensor_tensor(
                    out=fix_t[:], in0=fin[:, :, HALO:2 * HALO],
                    in1=icnt[:, g * 16:(g + 1) * 16].unsqueeze(1).to_broadcast([128, 2, 16]),
                    op=ALU.mult),
                    reads=[fin_b, icnt_buf, const_b], writes=[const_b])
                emit(S_dve, lambda e, a_t=a_t, pl3=pl3: e.tensor_tensor(
                    out=pl3[:, :, 0:HALO], in0=fix_t[:], in1=a_t[:, :, HALO:2 * HALO], op=ALU.subtract),
                    reads=[const_b, a_b, pl_b], writes=[pl_b])
            psz = [ps_alloc(), ps_alloc()]

            def fn_z(e, psz=psz, bc0=bc0):
                last = None
                for c in range(2):
                    for k in range(8):
                        last = e.matmul(psz[c][0][:], lhsT=win[slot][:, k, 256 + c * 128:256 + (c + 1) * 128],
                                        rhs=xT[:, k, bc0:bc0 + BLK], start=(k == 0), stop=(k == 7))
                return last
            emit(S_pe, fn_z, reads=[win_hb[slot][1]] + xT_reads(b), writes=[psz[0][1], psz[1][1]])
            sz_t, sz_b = szb[b % 2], szb_b[b % 2]
            for c in range(2):
                emit(S_act, lambda e, c=c, psz=psz, sz_t=sz_t: e.activation(
                    out=sz_t[:, c * BLK:(c + 1) * BLK], in_=psz[c][0][:], func=AF.Silu),
                    reads=[psz[c][1]], fills=[sz_b])

            def pw(pl=pl, pl_b=pl_b, sz_t=sz_t, sz_b=sz_b, b=b):
                psw = [ps_alloc(), ps_alloc()]

                def fn_pw(e):
                    last = None
                    for dc in range(2):
                        for kk in range(2):
                            last = e.matmul(psw[dc][0][:], lhsT=wpool[:, g, kk, dc * 128:(dc + 1) * 128],
                                            rhs=pl[:, kk * BLK:(kk + 1) * BLK], start=(kk == 0), stop=(kk == 1))
                    return last
                emit(S_pe, fn_pw, reads=[wpool_b, pl_b], writes=[psw[0][1], psw[1][1]])
                for dc in range(2):
                    ck = 2 * g + dc
                    emit(S_dve, lambda e, dc=dc, ck=ck: e.scalar_tensor_tensor(
                        out=yT[:, ck, b * BLK:(b + 1) * BLK], in0=psw[dc][0][:],
                        scalar=vecs[:, VEC_PS + ck:VEC_PS + ck + 1], in1=sz_t[:, dc * BLK:(dc + 1) * BLK],
                        op0=ALU.mult, op1=ALU.mult),
                        reads=[psw[dc][1], vecs_buf, sz_b], writes=[yT_b[ck][b]])
            pending.append(pw)
            yield

    z_ctr = [0]
    szq_b = [Buf() for _ in range(4)]

    def z_family(sbi, fz, slot):
        for b in range(NBLK):
            bc0 = HALO + b * BLK
            for c in range(4):
                ck = 8 + 4 * fz + c
                psz, psz_b = ps_alloc()

                def fn_z(e, psz=psz, c=c, bc0=bc0):
                    last = None
                    for k in range(8):
                        last = e.matmul(psz[:], lhsT=win[slot][:, k, c * 128:(c + 1) * 128],
                                        rhs=xT[:, k, bc0:bc0 + BLK], start=(k == 0), stop=(k == 7))
                    return last
                emit(S_pe, fn_z, reads=[win_hb[slot][c // 2]] + xT_reads(b), writes=[psz_b])
                zi = z_ctr[0] % 4
                z_ctr[0] += 1
                zt, zt_b = szb[zi // 2], szq_b[zi]
                emit(S_act, lambda e, psz=psz, zt=zt, zi=zi: e.activation(
                    out=zt[:, (zi % 2) * BLK:(zi % 2 + 1) * BLK], in_=psz[:], func=AF.Silu),
                    reads=[psz_b], writes=[zt_b], fills=[szb_b[zi // 2]])
                emit(S_dve, lambda e, zt=zt, zi=zi, ck=ck, b=b: e.tensor_tensor(
                    out=yT[:, ck, b * BLK:(b + 1) * BLK], in0=yT[:, ck, b * BLK:(b + 1) * BLK],
                    in1=zt[:, (zi % 2) * BLK:(zi % 2 + 1) * BLK], op=ALU.mult),
                    reads=[zt_b, szb_b[zi // 2], yT_b[ck][b]], writes=[yT_b[ck][b]])
                if c == 3:
                    flush_pending()
                yield

    def s_family(sbi, h, slot):
        for b in range(NBLK):
            bc0 = HALO + b * BLK
            it = (fam_ctr[0] * NBLK + b) % 2
            gv_t, gv_b = f4c[it], f4c_b[it]
            vn_t, vn_b = vnb[it], vnb_b[it]
            gu_t, gu_b = gub[it], gub_b[it]
            t1_t, t1_b = f4b[it], f4b_b[it]
            psv = [ps_alloc(), ps_alloc()]

            def fn_v(e, psv=psv, bc0=bc0):
                last = None
                for s in range(4):
                    for k in range(8):
                        last = e.matmul(psv[s // 2][0][:, (s % 2) * 256:(s % 2 + 1) * 256],
                                        lhsT=xT[:, k, bc0 + s * 128:bc0 + (s + 1) * 128],
                                        rhs=win[slot][:, k, 256:512], start=(k == 0), stop=(k == 7))
                return last
            emit(S_pe, fn_v, reads=[win_hb[slot][1]] + xT_reads(b), writes=[psv[0][1], psv[1][1]])
            for i in range(2):
                emit(S_act, lambda e, i=i, psv=psv, gv_t=gv_t: e.activation(
                    out=gv_t[:, i * 512:(i + 1) * 512], in_=psv[i][0][:], func=AF.Gelu_apprx_tanh),
                    reads=[psv[i][1]], fills=[gv_b])
            for s in range(4):
                emit(S_dve, lambda e, s=s, gv_t=gv_t, it=it: e.bn_stats(
                    out=st6[it][:, s, :], in_=gv_t[:, s * 256:(s + 1) * 256]),
                    reads=[gv_b], fills=[stat_b[it]])
            for s in range(4):
                emit(S_dve, lambda e, s=s, it=it: e.bn_aggr(out=mv[it][:, s, :], in_=st6[it][:, s, :]),
                     reads=[stat_b[it]], fills=[mv_b[it]])
            emit(S_pool, lambda e, it=it: e.tensor_scalar(
                out=ve[it][:], in0=mv[it][:, :, 1], scalar1=LN_EPS, scalar2=1.0, op0=ALU.add, op1=ALU.mult),
                reads=[mv_b[it]], writes=[ve_b[it]])
            emit(S_pool, lambda e, it=it: e.tensor_tensor(
                out=rstd[it][:], in0=ve[it][:], in1=negh[:, 0:4], op=ALU.pow),
                reads=[ve_b[it], negh_buf], writes=[rstd_b[it]])
            for s in range(4):
                emit(S_dve, lambda e, s=s, gv_t=gv_t, vn_t=vn_t, it=it: e.tensor_scalar(
                    out=vn_t[:, s * 256:(s + 1) * 256], in0=gv_t[:, s * 256:(s + 1) * 256],
                    scalar1=mv[it][:, s, 0:1], scalar2=rstd[it][:, s:s + 1],
                    op0=ALU.subtract, op1=ALU.mult),
                    reads=[gv_b, mv_b[it], rstd_b[it]], fills=[vn_b])
            yield
            psu = [ps_alloc(), ps_alloc()]

            def fn_u(e, psu=psu, bc0=bc0):
                last = None
                for c in range(2):
                    for k in range(8):
                        last = e.matmul(psu[c][0][:], lhsT=win[slot][:, k, c * 128:(c + 1) * 128],
                                        rhs=xT[:, k, bc0:bc0 + BLK], start=(k == 0), stop=(k == 7))
                return last
            emit(S_pe, fn_u, reads=[win_hb[slot][0]] + xT_reads(b), writes=[psu[0][1], psu[1][1]])
            for c in range(2):
                emit(S_act, lambda e, c=c, psu=psu, gu_t=gu_t: e.activation(
                    out=gu_t[:, c * BLK:(c + 1) * BLK], in_=psu[c][0][:], func=AF.Gelu_apprx_tanh),
                    reads=[psu[c][1]], fills=[gu_b])
            flush_pending()

            def sjob(vn_t=vn_t, vn_b=vn_b, gu_t=gu_t, gu_b=gu_b, t1_t=t1_t, t1_b=t1_b, b=b):
                pss = [ps_alloc(), ps_alloc()]

                def fn_s(e):
                    last = None
                    for c in range(2):
                        for s in range(4):
                            last = e.matmul(pss[c][0][:, s * 128:(s + 1) * 128],
                                            lhsT=vn_t[:, s * 256 + c * 128:s * 256 + (c + 1) * 128],
                                            rhs=wmT[:, h, :], start=True, stop=True)
                    return last
                emit(S_pe, fn_s, reads=[vn_b, wmT_buf], writes=[pss[0][1], pss[1][1]])
                for c in range(2):
                    k = 2 * h + c
                    emit(S_dve, lambda e, c=c, k=k: e.scalar_tensor_tensor(
                        out=t1_t[:, c, 0:BLK].rearrange("p (s i) -> p s i", s=4),
                        in0=pss[c][0][:].rearrange("p (s i) -> p s i", s=4),
                        scalar=vecs[:, VEC_SG + k:VEC_SG + k + 1],
                        in1=cst[:, k:k + 1, :].to_broadcast([128, 4, 128]),
                        op0=ALU.mult, op1=ALU.add),
                        reads=[pss[c][1], vecs_buf, cst_buf], fills=[t1_b])
                ck = 8 + 2 * h
                emit(S_dve, lambda e: e.tensor_tensor(
                    out=yT[:, ck:ck + 2, b * BLK:(b + 1) * BLK], in0=t1_t[:, :, 0:BLK],
                    in1=gu_t[:].rearrange("p (c t) -> p c t", c=2), op=ALU.mult),
                    reads=[t1_b, gu_b], writes=[yT_b[ck][b], yT_b[ck + 1][b]])
            pending.append(sjob)
            yield

    late_pieces = []
    for q in range(4):
        late_pieces.append(lambda q=q: misc_dma(S_pool, wout[:, 4 * q:4 * q + 4, :], wout_v[:, 4 * q:4 * q + 4, :],
                                                [wout_bs[q]]))
    for q in range(2):
        late_pieces.append(lambda q=q: misc_dma(S_pool, wg[:, 4 * q:4 * q + 4, :], wg_v[:, 4 * q:4 * q + 4, :],
                                                [wg_bs[q]]))
    late_pieces.append(lambda: misc_dma(S_pool, wp[:], wp_d.rearrange("(k p) d -> p k d", p=128), [wp_b]))

    def phase1(sbi, first_xt_rest):
        for fi, f in enumerate(fam_list):
            slot = fam_ctr[0] % 2
            kind, i = f
            if kind == "P":
                gen = p_family(sbi, i, slot)
            elif kind == "Z":
                gen = z_family(sbi, i, slot)
            else:
                gen = s_family(sbi, i, slot)
            last_fam = (fi == len(fam_list) - 1)
            nsteps = 0
            for _ in gen:
                nsteps += 1
                if fi == 0 and nsteps == 1 and first_xt_rest:
                    for s_ in first_xt_rest:
                        xt_job(sbi, s_)
                if sbi == 0 and fi == 1 and nsteps == 1:
                    emit_const_b()
                if sbi == 0 and fi == 1 and nsteps == 2:
                    emit_const_c()
                    emit_bg2()
                yield
            emit_fam_load(fam_ctr[0] + 2)
            if FLAG_LATE_PIECES:
                if late_pieces:
                    late_pieces.pop(0)()
            elif fi == 1 and sbi == 0:
                while late_pieces:
                    late_pieces.pop(0)()
            fam_ctr[0] += 1
            if fi == 0 and sbi == 0:
                emit_const_compute()
        flush_pending()
        yield

    NCH = SB // 128
    out_toks = []

    def phase2(sbi):
        st = {}

        def loads(j):
            t0 = sbi * SB + j * 128
            rs = row_alloc()
            ps_ = j % 2

            def fn_x(e, h):
                e.dma_start(out=row[rs][:, :], in_=x_d[t0:t0 + 128, :]).then_inc(h, 16)
            emit(S_sp, fn_x, writes=[row_b[rs]], dma_sem=row_sem[rs])

            def fn_p(e, h):
                e.dma_start(out=pin[ps_][:, :], in_=p_d[t0:t0 + 128, :]).then_inc(h, 16)
            emit(S_sp, fn_p, writes=[pin_b[ps_]], dma_sem=pin_sem[ps_])
            st[j] = {"row": rs, "pin": ps_}

        def stage_m(j):
            rs, pi = st[j]["row"], st[j]["pin"]
            b = (j * 128) // BLK
            tc0 = j * 128
            pst, pst_b = ps_alloc()

            def fn_pt(e):
                last = None
                for kk in range(2):
                    last = e.transpose(out=pst[:, kk * 128:(kk + 1) * 128],
                                       in_=pin[pi][:, kk * 128:(kk + 1) * 128], identity=ident_f[:])
                return last
            emit(S_pe, fn_pt, reads=[pin_b[pi], ident_buf], writes=[pst_b])
            pts = j % 3
            emit(S_act, lambda e: e.copy(out=pT[pts][:].rearrange("p k t -> p (k t)"), in_=pst[:, 0:256]),
                 reads=[pst_b, pT_b[pts]], writes=[pT_b[pts]])
            psm = [ps_alloc(), ps_alloc()]

            for hf_ in range(2):
                def fn_m(e, hf=hf_):
                    last = None
                    for k in range(16):
                        last = e.matmul(psm[hf][0][:], lhsT=yT[:, k, tc0:tc0 + 128],
                                        rhs=wout[:, k, hf * 512:(hf + 1) * 512], start=(k == 0), stop=(k == 15))
                    return last
                emit(S_pe, fn_m, reads=wout_bs + [yT_b[k][b] for k in range(16)], writes=[psm[hf_][1]])
            i2 = j % 2
            i3 = j % 3
            for hf in range(2):
                emit(S_dve, lambda e, hf=hf: e.scalar_tensor_tensor(
                    out=row[rs][:, hf * 512:(hf + 1) * 512], in0=row[rs][:, hf * 512:(hf + 1) * 512],
                    scalar=ALPHA, in1=psm[hf][0][:], op0=ALU.mult, op1=ALU.add),
                    reads=[psm[hf][1], row_b[rs]], fills=[row_b[rs]])
            for hf in range(2):
                emit(S_dve, lambda e, hf=hf: e.bn_stats(out=st6p[i3][:, hf, :],
                                                        in_=row[rs][:, hf * 512:(hf + 1) * 512]),
                     reads=[row_b[rs]], fills=[statp_b[i3]])
            emit(S_dve, lambda e: e.bn_aggr(out=mvp[i3][:], in_=st6p[i3][:].rearrange("p a b -> p (a b)")),
                 reads=[statp_b[i3]], writes=[mvp_b[i3]])
            emit(S_pool, lambda e: e.tensor_scalar(
                out=vep[i3][:], in0=mvp[i3][:, 1:2], scalar1=LN_EPS, scalar2=1.0, op0=ALU.add, op1=ALU.mult),
                reads=[mvp_b[i3]], writes=[vep_b[i3]])
            emit(S_pool, lambda e: e.tensor_tensor(
                out=rstdp[i3][:], in0=vep[i3][:], in1=negh[:, 0:1], op=ALU.pow),
                reads=[vep_b[i3], negh_buf], writes=[rstdp_b[i3]])
            emit(S_dve, lambda e: e.tensor_scalar(
                out=xhat[i2][:], in0=row[rs][:], scalar1=mvp[i3][:, 0:1], scalar2=rstdp[i3][:, 0:1],
                op0=ALU.subtract, op1=ALU.mult),
                reads=[row_b[rs], mvp_b[i3], rstdp_b[i3]], writes=[xhat_b[i2]])
            st[j]["pT"] = pts
            st[j]["i2"] = i2
            st[j]["i3"] = i3

        def stage_t(j):
            i2 = st[j]["i2"]
            pst_, psb_b = ps_alloc()
            psb = pst_[:].bitcast(BF16)

            def fn_t(e):
                last = None
                for k in range(8):
                    last = e.transpose(out=psb[:, k * 128:(k + 1) * 128], in_=xhat[i2][:, k * 128:(k + 1) * 128],
                                       identity=ident_b[:])
                return last
            emit(S_pe, fn_t, reads=[xhat_b[i2], identb_buf], writes=[psb_b])
            fast_tail = (sbi == NSB - 1 and j >= NCH - 1) or j == 0
            for k in range(8):
                if fast_tail and k >= 4:
                    emit(S_dve, lambda e, k=k: e.tensor_scalar(
                        out=xnT[i2][:, k * 128:(k + 1) * 128], in0=psb[:, k * 128:(k + 1) * 128],
                        scalar1=vecs[:, VEC_LG + k:VEC_LG + k + 1], scalar2=vecs[:, VEC_LB + k:VEC_LB + k + 1],
                        op0=ALU.mult, op1=ALU.add),
                        reads=[psb_b, vecs_buf], fills=[xnT_b[i2]])
                    continue
                emit(S_act, lambda e, k=k: e.activation(
                    out=xnT[i2][:, k * 128:(k + 1) * 128], in_=psb[:, k * 128:(k + 1) * 128], func=AF.Identity,
                    bias=vecs[:, VEC_LB + k:VEC_LB + k + 1], scale=vecs[:, VEC_LG + k:VEC_LG + k + 1]),
                    reads=[psb_b, vecs_buf], fills=[xnT_b[i2]])

        def stage_g(j):
            rs, i2, pts, i3 = st[j]["row"], st[j]["i2"], st[j]["pT"], st[j]["i3"]
            t0 = sbi * SB + j * 128
            g_t, g_b = f4c[i2], f4c_b[i2]
            psp = [ps_alloc(), ps_alloc()]

            def fn_pl(e):
                last = None
                for hf in range(2):
                    for kk in range(2):
                        last = e.matmul(psp[hf][0][:], lhsT=pT[pts][:, kk, :],
                                        rhs=wp[:, kk, hf * 512:(hf + 1) * 512], start=(kk == 0), stop=(kk == 1))
                return last
            emit(S_pe, fn_pl, reads=[pT_b[pts], wp_b], writes=[psp[0][1], psp[1][1]])
            psg = [ps_alloc(), ps_alloc()]

            for hf_ in range(2):
                def fn_g(e, hf=hf_):
                    for k in range(8):
                        e.matmul(psg[hf][0][:], lhsT=xnT[i2][:, k * 128:(k + 1) * 128],
                                 rhs=wg[:, k, hf * 512:(hf + 1) * 512], start=(k == 0), stop=False)
                    return e.matmul(psg[hf][0][:], lhsT=ones2[:, :], rhs=bg2[:, hf * 512:(hf + 1) * 512],
                                    start=False, stop=True)
                emit(S_pe, fn_g, reads=[xnT_b[i2], ones2_buf, bg2_buf] + wg_bs, writes=[psg[hf_][1]])
            for hf in range(2):
                emit(S_act, lambda e, hf=hf: e.activation(
                    out=g_t[:, hf * 512:(hf + 1) * 512], in_=psg[hf][0][:], func=AF.Sigmoid),
                    reads=[psg[hf][1]], fills=[g_b])
            emit(S_dve, lambda e: e.scalar_tensor_tensor(
                out=row[rs][:], in0=row[rs][:], scalar=mvp[i3][:, 0:1], in1=gbc[:],
                op0=ALU.subtract, op1=ALU.mult),
                reads=[row_b[rs], mvp_b[i3], gb_buf], writes=[row_b[rs]])
            emit(S_dve, lambda e: e.scalar_tensor_tensor(
                out=row[rs][:], in0=row[rs][:], scalar=rstdp[i3][:, 0:1], in1=bbc[:],
                op0=ALU.mult, op1=ALU.add),
                reads=[row_b[rs], rstdp_b[i3], gb_buf], writes=[row_b[rs]])
            for hf in range(2):
                emit(S_dve, lambda e, hf=hf: e.tensor_tensor(
                    out=g_t[:, hf * 512:(hf + 1) * 512], in0=g_t[:, hf * 512:(hf + 1) * 512],
                    in1=psp[hf][0][:], op=ALU.mult),
                    reads=[g_b, psp[hf][1]], fills=[g_b])
            fin_eng = S_dve
            emit(fin_eng, lambda e: e.tensor_tensor(out=row[rs][:], in0=row[rs][:], in1=g_t[:], op=ALU.add),
                 reads=[row_b[rs], g_b], writes=[row_b[rs]])

            def fn_st(e, h):
                e.dma_start(out=out_d[t0:t0 + 128, :], in_=row[rs][:, :]).then_inc(h, 16)
            out_toks.append(emit(S_sp, fn_st, reads=[row_b[rs]], dma_sem=row_sem[rs]))

        loads(0)
        for j in range(NCH + 1):
            xt_def = []
            if sbi + 1 < NSB and j <= 6:
                todo = {0: ["h", 0], 1: [1, 2]}.get(j, [j + 1])
                for s_ in todo:
                    xt_def.append(xt_job(sbi + 1, s_, defer=True))
            if j + 1 < NCH:
                loads(j + 1)
            if j == 0:
                stage_m(0)
                yield j
                stage_m(1)
                yield j
            elif 2 <= j < NCH:
                stage_m(j)
                yield j
            if 1 <= j:
                stage_g(j - 1)
                yield j
            if j < NCH:
                stage_t(j)
                yield j
            for f_ in xt_def:
                f_()

    warm_b = Buf()
    emit(S_pool, lambda e: e.memset(warm[:], 0.0), writes=[warm_b])
    emit(S_act, lambda e: e.activation(out=warm[:], in_=warm[:], func=AF.Silu), writes=[warm_b])
    emit_fam_load(0)
    defs_ = [xt_job(0, s_, defer=True) for s_ in ["h", 0, 1, 2, 3]]
    for f_ in defs_:
        f_()
    misc_dma(S_pool, wpool[:], poolw_d.rearrange("g (kk p) d -> p g kk d", p=128), [wpool_b])
    emit_fam_load(1)
    emit_small_const_loads()
    if not FLAG_CONST_LATE:
        emit_const_compute()
        emit_bg2()
    g1 = phase1(0, [4, 5, 6, 7])
    for _ in g1:
        pass
    for sbi in range(NSB):
        if dbg and sbi == 0:
            def fn_dbg(e, h):
                e.dma_start(out=dbg_y, in_=yT[:].rearrange("p k t -> p (k t)")).then_inc(h, 16)
            out_toks.append(emit(S_sp, fn_dbg, reads=[yT_b[k][b] for k in range(16) for b in range(NBLK)],
                                 dma_sem=misc_sem[misc_ctr[0]]))
            misc_ctr[0] += 1
        g2 = phase2(sbi)
        g1n = phase1(sbi + 1, None) if sbi + 1 < NSB else None
        budget = 0
        for j in g2:
            if g1n is not None and j >= NCH - 1 and budget < INTERLEAVE_STEPS:
                budget += 1
                if next(g1n, "done") == "done":
                    g1n = None
        if g1n is not None:
            for _ in g1n:
                pass

    fin = {}
    for tok in out_toks:
        _merge(fin, tok)
    for s, v in fin.items():
        S_sp.items.append(("wait", s, v))

    with nc.Block() as block:
        @block.tensor
        def _(e):
            replay(S_pe, e)

        @block.scalar
        def _(e):
            replay(S_act, e)

        @block.vector
        def _(e):
            replay(S_dve, e)

        @block.gpsimd
        def _(e):
            replay(S_pool, e)

        @block.sync
        def _(e):
            replay(S_sp, e)
    es.close()
    return nc


def _chunkT(v):
    return np.ascontiguousarray(np.asarray(v, dtype=np.float32).reshape(-1, 128).T)


def make_in_maps(x, p, w_in, pool_w, pool_scale, sgu_ln_g, sgu_ln_b, sgu_w, sgu_b,
                 w_out, ln_g, ln_b, ple_w, ple_gate_w, ple_gate_b):
    f = np.float32
    x = np.asarray(x, f)
    p = np.asarray(p, f)[0]
    vecs = np.concatenate([_chunkT(pool_scale[0]), _chunkT(sgu_ln_g[0]), _chunkT(sgu_ln_b[0]),
                           _chunkT(ln_g[0]), _chunkT(ln_b[0])], axis=1)
    shared = {
        "w_in": np.ascontiguousarray(np.asarray(w_in, f)[0]),
        "pool_w": np.ascontiguousarray(np.asarray(pool_w, f)[0]),
        "w_out": np.ascontiguousarray(np.asarray(w_out, f)[0]),
        "ple_w": np.ascontiguousarray(np.asarray(ple_w, f)[0]),
        "ple_gate_w": np.ascontiguousarray(np.asarray(ple_gate_w, f)[0]),
        "ple_gate_b": np.ascontiguousarray(np.asarray(ple_gate_b, f)[0].reshape(1, D)),
        "vecs": np.ascontiguousarray(vecs),
        "gbc": np.ascontiguousarray(np.broadcast_to(np.asarray(ln_g, f)[0][None, :], (128, D))),
        "bbc": np.ascontiguousarray(np.broadcast_to(np.asarray(ln_b, f)[0][None, :], (128, D))),
        "bsb": np.ascontiguousarray(np.broadcast_to(np.asarray(sgu_b, f)[0].reshape(1, 512), (128, 512))),
        "tril": np.tril(np.ones((128, 128), f)),
        "ident": np.eye(128, dtype=f),
        "sgw": np.ascontiguousarray(np.asarray(sgu_w, f)[0].transpose(1, 0, 2).reshape(128, 512)),
    }
    in_maps = []
    for c in range(N_CORES):
        b, q = divmod(c, 4)
        t0 = q * TOK
        xc = np.ascontiguousarray(x[b, t0:t0 + TOK])
        if q == 0:
            xh = np.zeros((HALO, D), f)
        else:
            xh = np.ascontiguousarray(x[b, t0 - HALO:t0])
        ic = np.zeros((4, 16), f)
        for g, w in enumerate(WINDOWS):
            for t in range(16):
                ic[g, t] = 1.0 / (min(t + 1, w) if q == 0 else w)
        m = dict(shared)
        m["x"] = xc
        m["xh"] = xh
        m["p"] = np.ascontiguousarray(p[b, t0:t0 + TOK])
        m["icnt"] = np.ascontiguousarray(np.broadcast_to(ic.reshape(1, 64), (128, 64)))
        in_maps.append(m)
    return in_maps


_NC_CACHE = {}


def kernel(**inputs):
    in_maps = make_in_maps(**inputs)
    if "nc" not in _NC_CACHE:
        _NC_CACHE["nc"] = build_program()
    nc = _NC_CACHE["nc"]
    res = run_bass_kernel_spmd(nc, in_maps, core_ids=list(range(N_CORES)))
    out = np.empty((2, 4 * TOK, D), np.float32)
    for c in range(N_CORES):
        b, q = divmod(c, 4)
        out[b, q * TOK:(q + 1) * TOK] = res.results[c]["out"]
    return out
```

```python
import numpy as np
from contextlib import ExitStack

import concourse.bass as bass
import concourse.mybir as mybir
from concourse.bass_utils import run_bass_kernel_spmd

F32 = mybir.dt.float32
BF16 = mybir.dt.bfloat16
AF = mybir.ActivationFunctionType
ALU = mybir.AluOpType

N_CORES = 8
D = 1024
TOK = 2048
SB = 1024
NSB = TOK // SB
BLK = 512
NBLK = SB // BLK
HALO = 16
XTW = HALO + SB
ALPHA = 2.0 ** 0.25
LN_EPS = 1e-5
WINDOWS = (2, 4, 8, 16)
INTERLEAVE_STEPS = 6
import os
FLAG_LATE_PIECES = os.environ.get('K_LATE', '1') == '1'
FLAG_CONST_LATE = os.environ.get('K_CONST', '1') == '1'
FLAG_XT_EARLY = os.environ.get('K_XT', '1') == '1'


class Sem:
    def __init__(self, handle, step):
        self.h = handle
        self.step = step
        self.val = 0

    def advance(self, n=1):
        self.val += self.step * n
        return (self, self.val)


class Buf:
    __slots__ = ("w", "r", "name", "psum")

    def __init__(self, name="", psum=False):
        self.w = {}
        self.r = {}
        self.name = name
        self.psum = psum


class Stream:
    def __init__(self, name, prog, is_pe=False):
        self.name = name
        self.prog = prog
        self.items = []
        self.waited = {}
        self.is_pe = is_pe


def _merge(d, tok):
    s, v = tok
    if d.get(s, 0) < v:
        d[s] = v


def emit(stream, fn, reads=(), writes=(), dma_sem=None, n_dma=1, fills=()):
    deps = {}
    for b in fills:
        for s, v in b.r.items():
            _merge(deps, (s, v))
        for s, v in b.w.items():
            if s is not stream.prog:
                _merge(deps, (s, v))
    for b in reads:
        for s, v in b.w.items():
            _merge(deps, (s, v))
        if b.psum:
            for s, v in b.r.items():
                if s is not stream.prog:
                    _merge(deps, (s, v))
    for b in writes:
        for s, v in b.w.items():
            _merge(deps, (s, v))
        for s, v in b.r.items():
            _merge(deps, (s, v))
    for s, v in deps.items():
        if stream.is_pe and s is stream.prog:
            continue
        if stream.waited.get(s, 0) >= v:
            continue
        stream.waited[s] = v
        stream.items.append(("wait", s, v))
    if dma_sem is not None:
        tok = dma_sem.advance(n_dma)
        stream.items.append(("dma", fn, dma_sem))
    else:
        tok = stream.prog.advance()
        stream.items.append(("op", fn, stream.prog))
    for b in reads:
        _merge(b.r, tok)
    for b in writes:
        b.w = {tok[0]: tok[1]}
        b.r = {}
    for b in fills:
        _merge(b.w, tok)
    return tok


def replay(stream, eng):
    for it in stream.items:
        if it[0] == "wait":
            eng.wait_ge(it[1].h, it[2])
        elif it[0] == "op":
            inst = it[1](eng)
            inst.then_inc(it[2].h, 1)
        else:
            it[1](eng, it[2].h)


def build_program(dbg=False):
    nc = bass.Bass("TRN2", target_bir_lowering=False)
    es = ExitStack()

    def dram_in(name, shape):
        return nc.dram_tensor(name, list(shape), F32, kind="ExternalInput").ap()

    x_d = dram_in("x", [TOK, D])
    xh_d = dram_in("xh", [HALO, D])
    p_d = dram_in("p", [TOK, 256])
    icnt_d = dram_in("icnt", [128, 4 * 16])
    win_d = dram_in("w_in", [D, 5120])
    poolw_d = dram_in("pool_w", [4, 256, 256])
    wout_d = dram_in("w_out", [2048, D])
    wp_d = dram_in("ple_w", [256, D])
    wg_d = dram_in("ple_gate_w", [D, D])
    bg_d = dram_in("ple_gate_b", [1, D])
    vecs_d = dram_in("vecs", [128, 40])
    gbc_d = dram_in("gbc", [128, D])
    bbc_d = dram_in("bbc", [128, D])
    bsb_d = dram_in("bsb", [128, 512])
    tril_d = dram_in("tril", [128, 128])
    ident_d = dram_in("ident", [128, 128])
    sgw_d = dram_in("sgw", [128, 512])
    out_d = nc.dram_tensor("out", [TOK, D], F32, kind="ExternalOutput").ap()
    if dbg:
        dbg_y = nc.dram_tensor("dbg_y", [128, 16 * SB], BF16, kind="ExternalOutput").ap()

    def sb_t(name, shape, dt):
        return es.enter_context(nc.sbuf_tensor("s_" + name, list(shape), dt))

    def ps_t(name, shape, dt):
        return es.enter_context(nc.psum_tensor(name, list(shape), dt))

    def new_sem(name, step):
        return Sem(es.enter_context(nc.semaphore(name)), step)

    S_pe = Stream("pe", new_sem("p_pe", 1), is_pe=True)
    S_act = Stream("act", new_sem("p_act", 1))
    S_dve = Stream("dve", new_sem("p_dve", 1))
    S_pool = Stream("pool", new_sem("p_pool", 1))
    S_sp = Stream("sp", new_sem("p_sp", 1))

    win = [sb_t(f"win{i}", [128, 8, 512], BF16) for i in range(2)]
    win_hb = [[Buf(f"win{i}a"), Buf(f"win{i}b")] for i in range(2)]
    win_sem = [[new_sem(f"d_win{i}a", 16), new_sem(f"d_win{i}b", 16)] for i in range(2)]
    wpool = sb_t("wpool", [128, 4, 2, 256], BF16)
    wout = sb_t("wout", [128, 16, D], BF16)
    wg = sb_t("wg", [128, 8, D], BF16)
    wp = sb_t("wp", [128, 2, D], BF16)
    wmT = sb_t("wmT", [128, 4, 128], BF16)
    xT = sb_t("xT", [128, 8, XTW], BF16)
    yT = sb_t("yT", [128, 16, SB], BF16)
    NROW = 4
    row = [sb_t(f"row{i}", [128, D], F32) for i in range(NROW)]
    row_b = [Buf(f"row{i}") for i in range(NROW)]
    row_sem = [new_sem(f"d_row{i}", 16) for i in range(NROW)]
    abuf = [sb_t(f"abuf{i}", [128, 2, 528], F32) for i in range(2)]
    abuf_b = [Buf() for _ in range(2)]
    abuf_hb = [Buf() for _ in range(2)]
    abuf_sem = [new_sem(f"d_abuf{i}", 16) for i in range(2)]
    f4b = [sb_t(f"f4b{i}", [128, 2, 528], F32) for i in range(2)]
    f4b_b = [Buf() for _ in range(2)]
    f4c = [sb_t(f"f4c{i}", [128, D], F32) for i in range(2)]
    f4c_b = [Buf() for _ in range(2)]
    b2 = [sb_t(f"b2_{i}", [128, D], BF16) for i in range(8)]
    b2_b = [Buf() for _ in range(8)]
    pooled, pooled_b = b2[0:2], b2_b[0:2]
    szb, szb_b = b2[2:4], b2_b[2:4]
    gub, gub_b = b2[4:6], b2_b[4:6]
    vnb, vnb_b = b2[6:8], b2_b[6:8]
    xhat, xhat_b = b2[4:6], b2_b[4:6]
    xnT, xnT_b = b2[6:8], b2_b[6:8]
    NXB = 2
    xb = [sb_t(f"xb{i}", [128, D], BF16) for i in range(NXB)]
    xb_b = [Buf() for _ in range(NXB)]
    xb_sem = [new_sem(f"d_xb{i}", 16) for i in range(NXB)]
    pin = [sb_t(f"pin{i}", [128, 256], F32) for i in range(2)]
    pin_b = [Buf() for _ in range(2)]
    pin_sem = [new_sem(f"d_pin{i}", 16) for i in range(2)]
    pT = [sb_t(f"pT{i}", [128, 2, 128], BF16) for i in range(3)]
    pT_b = [Buf() for _ in range(3)]
    ident_f = sb_t("ident_f", [128, 128], F32)
    ident_b = sb_t("ident_b", [128, 128], BF16)
    vecs = sb_t("vecs", [128, 40], F32)
    cst = sb_t("cst", [128, 8, 128], F32)
    gbc = sb_t("gbc", [128, D], F32)
    bbc = sb_t("bbc", [128, D], F32)
    bsb = sb_t("bsb", [128, 512], F32)
    tril = sb_t("tril", [128, 128], F32)
    icnt = sb_t("icnt", [128, 64], F32)
    ones_f = sb_t("ones_f", [128, 128], F32)
    ones2 = sb_t("ones2", [128, 128], BF16)
    bg2 = sb_t("bg2", [128, D], BF16)
    bgf = f4c[0][0:1, :]
    bgt = f4c[1][0:1, :]
    bghi = b2[6][0:1, :]
    bglo = b2[7][0:1, :]
    wmTf2 = b2[5][:].bitcast(F32)
    sgw2 = b2[4][:].bitcast(F32)
    negh = sb_t("negh", [128, 8], F32)
    warm = sb_t("warm", [128, 2], F32)
    fix_t = sb_t("fix_t", [128, 2, 16], F32)
    st6 = [sb_t(f"st6_{i}", [128, 4, 6], F32) for i in range(2)]
    mv = [sb_t(f"mv{i}", [128, 4, 2], F32) for i in range(2)]
    ve = [sb_t(f"ve{i}", [128, 4], F32) for i in range(2)]
    rstd = [sb_t(f"rstd{i}", [128, 4], F32) for i in range(2)]
    stat_b = [Buf() for _ in range(2)]
    mv_b = [Buf() for _ in range(2)]
    ve_b = [Buf() for _ in range(2)]
    rstd_b = [Buf() for _ in range(2)]
    st6p = [sb_t(f"st6p{i}", [128, 2, 6], F32) for i in range(3)]
    mvp = [sb_t(f"mvp{i}", [128, 2], F32) for i in range(3)]
    vep = [sb_t(f"vep{i}", [128, 1], F32) for i in range(3)]
    rstdp = [sb_t(f"rstdp{i}", [128, 1], F32) for i in range(3)]
    statp_b = [Buf() for _ in range(3)]
    mvp_b = [Buf() for _ in range(3)]
    vep_b = [Buf() for _ in range(3)]
    rstdp_b = [Buf() for _ in range(3)]

    VEC_PS, VEC_SG, VEC_SB, VEC_LG, VEC_LB = 0, 8, 16, 24, 32

    NPS = 8
    psf = [ps_t(f"psf{i}", [128, 512], F32) for i in range(NPS)]
    psf_b = [Buf(f"psf{i}", psum=True) for i in range(NPS)]
    ps_ctr = [0]

    def ps_alloc():
        i = ps_ctr[0] % NPS
        ps_ctr[0] += 1
        return psf[i], psf_b[i]

    const_b = Buf("consts")
    xT_b = [[Buf() for _ in range(2)] for _ in range(8)]
    xTh_b = Buf()
    yT_b = [[Buf() for _ in range(NBLK)] for _ in range(16)]
    wpool_b, wp_b = Buf(), Buf()
    misc_sem = [new_sem(f"d_misc{i}", 16) for i in range(26)]
    misc_ctr = [0]

    def misc_dma(stream, out_ap, in_ap, writes, reads=()):
        sem = misc_sem[misc_ctr[0]]
        misc_ctr[0] += 1

        def fn(e, h):
            e.dma_start(out=out_ap, in_=in_ap).then_inc(h, 16)
        return emit(stream, fn, reads=reads, writes=writes, dma_sem=sem)

    win_v = win_d.rearrange("(k p) e -> p k e", p=128)
    fam_list = [("P", 0), ("P", 1), ("P", 2), ("P", 3),
                ("S", 0), ("S", 1), ("S", 2), ("S", 3), ("Z", 0), ("Z", 1)]
    all_fams = [(sbi, f) for sbi in range(NSB) for f in fam_list]

    def fam_cols(f):
        kind, i = f
        if kind == "P":
            return [(i * 256, 256), (3072 + i * 256, 256)]
        if kind == "Z":
            return [(3072 + 1024 + i * 512, 256), (3072 + 1024 + i * 512 + 256, 256)]
        return [(1024 + i * 256, 256), (2048 + i * 256, 256)]

    def emit_fam_load(fidx):
        if fidx >= len(all_fams):
            return
        slot = fidx % 2
        cols = fam_cols(all_fams[fidx][1])

        for hf, (c0, n) in enumerate(cols):
            def fn(e, h, hf=hf, c0=c0, n=n):
                e.dma_start(out=win[slot][:, :, hf * 256:hf * 256 + n], in_=win_v[:, :, c0:c0 + n]).then_inc(h, 16)
            extra = [win_hb[0][0]] if (fidx == 0 and hf == 1) else []
            emit(S_pool, fn, reads=extra, writes=[win_hb[slot][hf]], dma_sem=win_sem[slot][hf])

    ident_buf, vecs_buf, tril_buf, bsb_buf, icnt_buf, gb_buf = (Buf() for _ in range(6))
    sgw_buf = b2_b[4]
    wmTf_buf = b2_b[5]
    bgf_buf, bgt_buf, bghi_buf, bglo_buf = f4c_b[0], f4c_b[1], b2_b[6], b2_b[7]
    identb_buf = Buf()
    misc_dma(S_sp, ident_f[:], ident_d, [ident_buf])
    misc_dma(S_pool, ident_b[:], ident_d, [identb_buf])

    def emit_small_const_loads():
        misc_dma(S_sp, vecs[:], vecs_d, [vecs_buf])
        misc_dma(S_sp, icnt[:], icnt_d, [icnt_buf])

    ones_buf, negh_buf, ones2_buf, wmT_buf, cst_buf, bg2_buf = (Buf() for _ in range(6))

    def emit_const_compute():
        misc_dma(S_sp, tril[:], tril_d, [tril_buf])
        misc_dma(S_sp, sgw2, sgw_d, [sgw_buf])
        misc_dma(S_sp, bsb[:], bsb_d, [bsb_buf])
        misc_dma(S_sp, bgf, bg_d, [bgf_buf])
        misc_dma(S_sp, gbc[:], gbc_d, [gb_buf])
        misc_dma(S_sp, bbc[:], bbc_d, [gb_buf], reads=[gb_buf])
        emit(S_dve, lambda e: e.memset(ones_f[:], 1.0), writes=[ones_buf])
        emit(S_dve, lambda e: e.memset(negh[:], -0.5), writes=[negh_buf])
        emit(S_dve, lambda e: e.memset(ones2[:], 0.0), writes=[ones2_buf])
        emit(S_dve, lambda e: e.memset(ones2[0:2, :], 1.0), reads=[ones2_buf], writes=[ones2_buf])
        emit(S_pool, lambda e: e.memset(bg2[:], 0.0), writes=[bg2_buf])

        sgw3 = sgw2.rearrange("p (h j) -> p h j", h=4)
        emit(S_dve, lambda e: e.tensor_tensor(out=sgw3, in0=sgw3,
                                              in1=tril[:].unsqueeze(1).to_broadcast([128, 4, 128]),
                                              op=ALU.mult),
             reads=[tril_buf], writes=[sgw_buf])

    ps_w_box = []

    def emit_const_b():
        ps_w, ps_w_b = ps_alloc()
        ps_w_box.append((ps_w, ps_w_b))

        def fn_wmT(e):
            last = None
            for h in range(4):
                last = e.transpose(out=ps_w[:, h * 128:(h + 1) * 128], in_=sgw2[:, h * 128:(h + 1) * 128],
                                   identity=ident_f[:])
            return last
        emit(S_pe, fn_wmT, reads=[sgw_buf, ident_buf], writes=[ps_w_b])
        emit(S_act, lambda e: e.copy(out=wmT[:].rearrange("p h i -> p (h i)"), in_=ps_w[:]),
             reads=[ps_w_b], writes=[wmT_buf])
        emit(S_dve, lambda e: e.tensor_copy(out=wmTf2, in_=ps_w[:]),
             reads=[ps_w_b], writes=[wmTf_buf])

    def emit_const_c():
        ps_r, ps_r_b = ps_alloc()
        emit(S_pe, lambda e: e.matmul(ps_r[:], lhsT=ones_f[:], rhs=wmTf2, start=True, stop=True),
             reads=[ones_buf, wmTf_buf], writes=[ps_r_b])
        for k in range(8):
            h = k // 2
            emit(S_dve, lambda e, k=k, h=h: e.scalar_tensor_tensor(
                out=cst[:, k, :], in0=ps_r[:, h * 128:(h + 1) * 128],
                scalar=vecs[:, VEC_SB + k:VEC_SB + k + 1], in1=bsb[:, h * 128:(h + 1) * 128],
                op0=ALU.mult, op1=ALU.add),
                reads=[ps_r_b, vecs_buf, bsb_buf], fills=[cst_buf])
        emit(S_dve, lambda e: e.tensor_copy(out=bghi, in_=bgf), reads=[bgf_buf], writes=[bghi_buf])
        emit(S_dve, lambda e: e.tensor_copy(out=bgt, in_=bghi), reads=[bghi_buf], writes=[bgt_buf])
        emit(S_dve, lambda e: e.tensor_tensor(out=bgt, in0=bgf, in1=bgt, op=ALU.subtract),
             reads=[bgf_buf, bgt_buf], writes=[bgt_buf])
        emit(S_dve, lambda e: e.tensor_copy(out=bglo, in_=bgt), reads=[bgt_buf], writes=[bglo_buf])

    def emit_bg2():
        misc_dma(S_sp, bg2[0:1, :], bghi, [bg2_buf], reads=[bghi_buf, bg2_buf])
        misc_dma(S_sp, bg2[1:2, :], bglo, [bg2_buf], reads=[bglo_buf, bg2_buf])

    wout_bs = [Buf() for _ in range(4)]
    wg_bs = [Buf() for _ in range(2)]
    wout_v = wout_d.rearrange("(k p) d -> p k d", p=128)
    wg_v = wg_d.rearrange("(k p) d -> p k d", p=128)

    row_ctr = [0]

    def row_alloc():
        i = row_ctr[0] % NROW
        row_ctr[0] += 1
        return i

    pending = []

    def flush_pending():
        while pending:
            pending.pop(0)()

    fam_ctr = [0]
    evac_flip = [0]

    xb_ctr = [0]

    stage_ctr = [0]

    def xt_job(sbi, s, defer=False):
        if sbi == 0:
            pool_ = [(xb[0], xb_b[0]), (xb[1], xb_b[1]), (b2[4], b2_b[4]), (b2[5], b2_b[5]),
                     (b2[6], b2_b[6]), (b2[7], b2_b[7])]
        else:
            pool_ = [(xb[0], xb_b[0]), (xb[1], xb_b[1])]
        xbt, xbb = pool_[xb_ctr[0] % len(pool_)]
        xb_ctr[0] += 1
        si = stage_ctr[0]
        stage_ctr[0] += 1
        if sbi == 0:
            sidx = si % NROW
            stg, stg_bufs, stg_sem = row[sidx][:, :], [row_b[sidx]], row_sem[sidx]
        else:
            sidx = si % 2
            stg = abuf[sidx][:].rearrange("p c t -> p (c t)")[:, 0:D]
            stg_bufs, stg_sem = [abuf_b[sidx], abuf_hb[sidx]], abuf_sem[sidx]
        if s == "h":
            src = xh_d if sbi == 0 else x_d[sbi * SB - HALO:sbi * SB, :]
            np_ = HALO
        else:
            t0 = sbi * SB + s * 128
            src = x_d[t0:t0 + 128, :]
            np_ = 128

        def fn_ld(e, h):
            e.dma_start(out=stg[0:np_, :], in_=src).then_inc(h, 16)
        gate = [win_hb[0][0]] if (sbi == 0 and s in (4, 5, 6, 7)) else []
        emit(S_sp, fn_ld, reads=gate, writes=stg_bufs, dma_sem=stg_sem)
        if sbi == 0:
            emit(S_dve, lambda e: e.tensor_copy(out=xbt[0:np_, :], in_=stg[0:np_, :]),
                 reads=stg_bufs, writes=[xbb])
        else:
            emit(S_act, lambda e: e.copy(out=xbt[0:np_, :], in_=stg[0:np_, :]),
                 reads=stg_bufs, writes=[xbb])
        if defer:
            return lambda: xt_compute(s, xbt, xbb)
        xt_compute(s, xbt, xbb)

    def xt_compute(s, xbt, xbb):
        ps1, ps1_b = ps_alloc()
        pv = ps1[:].bitcast(BF16)
        if s == "h":
            def fn_t(e):
                last = None
                for k in range(8):
                    last = e.transpose(out=pv[:, k * HALO:(k + 1) * HALO],
                                       in_=xbt[0:HALO, k * 128:(k + 1) * 128],
                                       identity=ident_b[0:HALO, 0:HALO])
                return last
            emit(S_pe, fn_t, reads=[xbb, identb_buf], writes=[ps1_b])
            emit(S_dve, lambda e: e.tensor_copy(
                out=xT[:, :, 0:HALO], in_=pv[:, 0:8 * HALO].rearrange("p (k t) -> p k t", k=8)),
                reads=[ps1_b], writes=[xTh_b])
            return

        def fn_t(e):
            last = None
            for k in range(8):
                last = e.transpose(out=pv[:, k * 128:(k + 1) * 128],
                                   in_=xbt[:, k * 128:(k + 1) * 128], identity=ident_b[:])
            return last
        emit(S_pe, fn_t, reads=[xbb, identb_buf], writes=[ps1_b])
        c0 = HALO + s * 128
        src_ap = pv.rearrange("p (k t) -> p k t", k=8)
        dst_ap = xT[:, :, c0:c0 + 128]
        if evac_flip[0] % 2 == 0:
            emit(S_dve, lambda e: e.tensor_copy(out=dst_ap, in_=src_ap),
                 reads=[ps1_b], writes=[xT_b[s][0], xT_b[s][1]])
        else:
            emit(S_act, lambda e: e.copy(out=dst_ap, in_=src_ap),
                 reads=[ps1_b], writes=[xT_b[s][0], xT_b[s][1]])
        evac_flip[0] += 1

    def xT_reads(b):
        r = []
        for s in range(4 * b, 4 * b + 4):
            r += xT_b[s]
        return r

    def p_family(sbi, g, slot):
        w = WINDOWS[g]
        for b in range(NBLK):
            bc0 = HALO + b * BLK
            ab = (fam_ctr[0] * NBLK + b) % 2
            a_t, a_b, ah_b = abuf[ab], abuf_b[ab], abuf_hb[ab]
            psa = [ps_alloc(), ps_alloc()]

            def fn_a(e, psa=psa, bc0=bc0):
                last = None
                for c in range(2):
                    for k in range(8):
                        last = e.matmul(psa[c][0][:], lhsT=win[slot][:, k, c * 128:(c + 1) * 128],
                                        rhs=xT[:, k, bc0:bc0 + BLK], start=(k == 0), stop=(k == 7))
                return last
            emit(S_pe, fn_a, reads=[win_hb[slot][0]] + xT_reads(b), writes=[psa[0][1], psa[1][1]])
            for c in range(2):
                emit(S_act, lambda e, c=c, psa=psa, a_t=a_t: e.copy(out=a_t[:, c, HALO:HALO + BLK],
                                                                    in_=psa[c][0][:]),
                     reads=[psa[c][1]], fills=[a_b])
            if b == 0:
                psh, psh_b = ps_alloc()

                def fn_ah(e, psh=psh):
                    last = None
                    for c in range(2):
                        for k in range(8):
                            last = e.matmul(psh[:, c * HALO:(c + 1) * HALO],
                                            lhsT=win[slot][:, k, c * 128:(c + 1) * 128],
                                            rhs=xT[:, k, 0:HALO], start=(k == 0), stop=(k == 7))
                    return last
                emit(S_pe, fn_ah, reads=[win_hb[slot][0], xTh_b], writes=[psh_b])
                emit(S_act, lambda e, psh=psh, a_t=a_t: e.copy(
                    out=a_t[:, :, 0:HALO], in_=psh[:, 0:2 * HALO].rearrange("p (c t) -> p c t", c=2)),
                    reads=[psh_b, ah_b], writes=[ah_b])
            else:
                oth = abuf[1 - ab]
                emit(S_pool, lambda e, a_t=a_t, oth=oth: e.tensor_copy(out=a_t[:, :, 0:HALO],
                                                                       in_=oth[:, :, BLK:BLK + HALO]),
                     reads=[abuf_b[1 - ab], ah_b], writes=[ah_b])
            yield
            if len(pending) >= 2:
                pending.pop(0)()
            sA, sB = f4b[0], f4b[1]
            chain = [(a_t, 1, sA), (sA, 2, sB), (sB, 4, sA), (sA, 8, sB)]
            lo = 0
            src_b = [a_b, ah_b]
            fin, fin_b = None, None
            for step in range(g + 1):
                src, sh, dst = chain[step]
                lo_new = lo + sh
                dst_b = f4b_b[step % 2]
                emit(S_dve, lambda e, src=src, dst=dst, lo=lo, lo_new=lo_new, sh=sh: e.tensor_tensor(
                    out=dst[:, :, lo_new:528], in0=src[:, :, lo_new:528], in1=src[:, :, lo:528 - sh],
                    op=ALU.add),
                    reads=src_b, writes=[dst_b])
                src_b = [dst_b]
                lo = lo_new
                fin, fin_b = dst, dst_b
            pl, pl_b = pooled[b % 2], pooled_b[b % 2]
            pl3 = pl[:].rearrange("p (c t) -> p c t", c=2)
            emit(S_dve, lambda e, fin=fin, a_t=a_t, pl3=pl3: e.scalar_tensor_tensor(
                out=pl3, in0=fin[:, :, HALO:528], scalar=1.0 / w, in1=a_t[:, :, HALO:528],
                op0=ALU.mult, op1=ALU.subtract),
                reads=[fin_b, a_b], writes=[pl_b])
            if sbi == 0 and b == 0:
                emit(S_dve, lambda e, fin=fin: e.tensor_tensor(
                    out=fix_t[:], in0=fin[:, :, HALO:2 * HALO],
                    in1=icnt[:, g * 16:(g + 1) * 16].unsqueeze(1).to_broadcast([128, 2, 16]),
                    op=ALU.mult),
                    reads=[fin_b, icnt_buf, const_b], writes=[const_b])
                emit(S_dve, lambda e, a_t=a_t, pl3=pl3: e.tensor_tensor(
                    out=pl3[:, :, 0:HALO], in0=fix_t[:], in1=a_t[:, :, HALO:2 * HALO], op=ALU.subtract),
                    reads=[const_b, a_b, pl_b], writes=[pl_b])
            psz = [ps_alloc(), ps_alloc()]

            def fn_z(e, psz=psz, bc0=bc0):
                last = None
                for c in range(2):
                    for k in range(8):
                        last = e.matmul(psz[c][0][:], lhsT=win[slot][:, k, 256 + c * 128:256 + (c + 1) * 128],
                                        rhs=xT[:, k, bc0:bc0 + BLK], start=(k == 0), stop=(k == 7))
                return last
            emit(S_pe, fn_z, reads=[win_hb[slot][1]] + xT_reads(b), writes=[psz[0][1], psz[1][1]])
            sz_t, sz_b = szb[b % 2], szb_b[b % 2]
            for c in range(2):
                emit(S_act, lambda e, c=c, psz=psz, sz_t=sz_t: e.activation(
                    out=sz_t[:, c * BLK:(c + 1) * BLK], in_=psz[c][0][:], func=AF.Silu),
                    reads=[psz[c][1]], fills=[sz_b])

            def pw(pl=pl, pl_b=pl_b, sz_t=sz_t, sz_b=sz_b, b=b):
                psw = [ps_alloc(), ps_alloc()]

                def fn_pw(e):
                    last = None
                    for dc in range(2):
                        for kk in range(2):
                            last = e.matmul(psw[dc][0][:], lhsT=wpool[:, g, kk, dc * 128:(dc + 1) * 128],
                                            rhs=pl[:, kk * BLK:(kk + 1) * BLK], start=(kk == 0), stop=(kk == 1))
                    return last
                emit(S_pe, fn_pw, reads=[wpool_b, pl_b], writes=[psw[0][1], psw[1][1]])
                for dc in range(2):
                    ck = 2 * g + dc
                    emit(S_dve, lambda e, dc=dc, ck=ck: e.scalar_tensor_tensor(
                        out=yT[:, ck, b * BLK:(b + 1) * BLK], in0=psw[dc][0][:],
                        scalar=vecs[:, VEC_PS + ck:VEC_PS + ck + 1], in1=sz_t[:, dc * BLK:(dc + 1) * BLK],
                        op0=ALU.mult, op1=ALU.mult),
                        reads=[psw[dc][1], vecs_buf, sz_b], writes=[yT_b[ck][b]])
            pending.append(pw)
            yield

    z_ctr = [0]
    szq_b = [Buf() for _ in range(4)]

    def z_family(sbi, fz, slot):
        for b in range(NBLK):
            bc0 = HALO + b * BLK
            for c in range(4):
                ck = 8 + 4 * fz + c
                psz, psz_b = ps_alloc()

                def fn_z(e, psz=psz, c=c, bc0=bc0):
                    last = None
                    for k in range(8):
                        last = e.matmul(psz[:], lhsT=win[slot][:, k, c * 128:(c + 1) * 128],
                                        rhs=xT[:, k, bc0:bc0 + BLK], start=(k == 0), stop=(k == 7))
                    return last
                emit(S_pe, fn_z, reads=[win_hb[slot][c // 2]] + xT_reads(b), writes=[psz_b])
                zi = z_ctr[0] % 4
                z_ctr[0] += 1
                zt, zt_b = szb[zi // 2], szq_b[zi]
                emit(S_act, lambda e, psz=psz, zt=zt, zi=zi: e.activation(
                    out=zt[:, (zi % 2) * BLK:(zi % 2 + 1) * BLK], in_=psz[:], func=AF.Silu),
                    reads=[psz_b], writes=[zt_b], fills=[szb_b[zi // 2]])
                emit(S_dve, lambda e, zt=zt, zi=zi, ck=ck, b=b: e.tensor_tensor(
                    out=yT[:, ck, b * BLK:(b + 1) * BLK], in0=yT[:, ck, b * BLK:(b + 1) * BLK],
                    in1=zt[:, (zi % 2) * BLK:(zi % 2 + 1) * BLK], op=ALU.mult),
                    reads=[zt_b, szb_b[zi // 2], yT_b[ck][b]], writes=[yT_b[ck][b]])
                if c == 3:
                    flush_pending()
                yield

    def s_family(sbi, h, slot):
        for b in range(NBLK):
            bc0 = HALO + b * BLK
            it = (fam_ctr[0] * NBLK + b) % 2
            gv_t, gv_b = f4c[it], f4c_b[it]
            vn_t, vn_b = vnb[it], vnb_b[it]
            gu_t, gu_b = gub[it], gub_b[it]
            t1_t, t1_b = f4b[it], f4b_b[it]
            psv = [ps_alloc(), ps_alloc()]

            def fn_v(e, psv=psv, bc0=bc0):
                last = None
                for s in range(4):
                    for k in range(8):
                        last = e.matmul(psv[s // 2][0][:, (s % 2) * 256:(s % 2 + 1) * 256],
                                        lhsT=xT[:, k, bc0 + s * 128:bc0 + (s + 1) * 128],
                                        rhs=win[slot][:, k, 256:512], start=(k == 0), stop=(k == 7))
                return last
            emit(S_pe, fn_v, reads=[win_hb[slot][1]] + xT_reads(b), writes=[psv[0][1], psv[1][1]])
            for i in range(2):
                emit(S_act, lambda e, i=i, psv=psv, gv_t=gv_t: e.activation(
                    out=gv_t[:, i * 512:(i + 1) * 512], in_=psv[i][0][:], func=AF.Gelu_apprx_tanh),
                    reads=[psv[i][1]], fills=[gv_b])
            for s in range(4):
                emit(S_dve, lambda e, s=s, gv_t=gv_t, it=it: e.bn_stats(
                    out=st6[it][:, s, :], in_=gv_t[:, s * 256:(s + 1) * 256]),
                    reads=[gv_b], fills=[stat_b[it]])
            for s in range(4):
                emit(S_dve, lambda e, s=s, it=it: e.bn_aggr(out=mv[it][:, s, :], in_=st6[it][:, s, :]),
                     reads=[stat_b[it]], fills=[mv_b[it]])
            emit(S_pool, lambda e, it=it: e.tensor_scalar(
                out=ve[it][:], in0=mv[it][:, :, 1], scalar1=LN_EPS, scalar2=1.0, op0=ALU.add, op1=ALU.mult),
                reads=[mv_b[it]], writes=[ve_b[it]])
            emit(S_pool, lambda e, it=it: e.tensor_tensor(
                out=rstd[it][:], in0=ve[it][:], in1=negh[:, 0:4], op=ALU.pow),
                reads=[ve_b[it], negh_buf], writes=[rstd_b[it]])
            for s in range(4):
                emit(S_dve, lambda e, s=s, gv_t=gv_t, vn_t=vn_t, it=it: e.tensor_scalar(
                    out=vn_t[:, s * 256:(s + 1) * 256], in0=gv_t[:, s * 256:(s + 1) * 256],
                    scalar1=mv[it][:, s, 0:1], scalar2=rstd[it][:, s:s + 1],
                    op0=ALU.subtract, op1=ALU.mult),
                    reads=[gv_b, mv_b[it], rstd_b[it]], fills=[vn_b])
            yield
            psu = [ps_alloc(), ps_alloc()]

            def fn_u(e, psu=psu, bc0=bc0):
                last = None
                for c in range(2):
                    for k in range(8):
                        last = e.matmul(psu[c][0][:], lhsT=win[slot][:, k, c * 128:(c + 1) * 128],
                                        rhs=xT[:, k, bc0:bc0 + BLK], start=(k == 0), stop=(k == 7))
                return last
            emit(S_pe, fn_u, reads=[win_hb[slot][0]] + xT_reads(b), writes=[psu[0][1], psu[1][1]])
            for c in range(2):
                emit(S_act, lambda e, c=c, psu=psu, gu_t=gu_t: e.activation(
                    out=gu_t[:, c * BLK:(c + 1) * BLK], in_=psu[c][0][:], func=AF.Gelu_apprx_tanh),
                    reads=[psu[c][1]], fills=[gu_b])
            flush_pending()

            def sjob(vn_t=vn_t, vn_b=vn_b, gu_t=gu_t, gu_b=gu_b, t1_t=t1_t, t1_b=t1_b, b=b):
                pss = [ps_alloc(), ps_alloc()]

                def fn_s(e):
                    last = None
                    for c in range(2):
                        for s in range(4):
                            last = e.matmul(pss[c][0][:, s * 128:(s + 1) * 128],
                                            lhsT=vn_t[:, s * 256 + c * 128:s * 256 + (c + 1) * 128],
                                            rhs=wmT[:, h, :], start=True, stop=True)
                    return last
                emit(S_pe, fn_s, reads=[vn_b, wmT_buf], writes=[pss[0][1], pss[1][1]])
                for c in range(2):
                    k = 2 * h + c
                    emit(S_dve, lambda e, c=c, k=k: e.scalar_tensor_tensor(
                        out=t1_t[:, c, 0:BLK].rearrange("p (s i) -> p s i", s=4),
                        in0=pss[c][0][:].rearrange("p (s i) -> p s i", s=4),
                        scalar=vecs[:, VEC_SG + k:VEC_SG + k + 1],
                        in1=cst[:, k:k + 1, :].to_broadcast([128, 4, 128]),
                        op0=ALU.mult, op1=ALU.add),
                        reads=[pss[c][1], vecs_buf, cst_buf], fills=[t1_b])
                ck = 8 + 2 * h
                emit(S_dve, lambda e: e.tensor_tensor(
                    out=yT[:, ck:ck + 2, b * BLK:(b + 1) * BLK], in0=t1_t[:, :, 0:BLK],
                    in1=gu_t[:].rearrange("p (c t) -> p c t", c=2), op=ALU.mult),
                    reads=[t1_b, gu_b], writes=[yT_b[ck][b], yT_b[ck + 1][b]])
            pending.append(sjob)
            yield

    late_pieces = []
    for q in range(4):
        late_pieces.append(lambda q=q: misc_dma(S_pool, wout[:, 4 * q:4 * q + 4, :], wout_v[:, 4 * q:4 * q + 4, :],
                                                [wout_bs[q]]))
    for q in range(2):
        late_pieces.append(lambda q=q: misc_dma(S_pool, wg[:, 4 * q:4 * q + 4, :], wg_v[:, 4 * q:4 * q + 4, :],
                                                [wg_bs[q]]))
    late_pieces.append(lambda: misc_dma(S_pool, wp[:], wp_d.rearrange("(k p) d -> p k d", p=128), [wp_b]))

    def phase1(sbi, first_xt_rest):
        for fi, f in enumerate(fam_list):
            slot = fam_ctr[0] % 2
            kind, i = f
            if kind == "P":
                gen = p_family(sbi, i, slot)
            elif kind == "Z":
                gen = z_family(sbi, i, slot)
            else:
                gen = s_family(sbi, i, slot)
            last_fam = (fi == len(fam_list) - 1)
            nsteps = 0
            for _ in gen:
                nsteps += 1
                if fi == 0 and nsteps == 1 and first_xt_rest:
                    for s_ in first_xt_rest:
                        xt_job(sbi, s_)
                if sbi == 0 and fi == 1 and nsteps == 1:
                    emit_const_b()
                if sbi == 0 and fi == 1 and nsteps == 2:
                    emit_const_c()
                    emit_bg2()
                yield
            emit_fam_load(fam_ctr[0] + 2)
            if FLAG_LATE_PIECES:
                if late_pieces:
                    late_pieces.pop(0)()
            elif fi == 1 and sbi == 0:
                while late_pieces:
                    late_pieces.pop(0)()
            fam_ctr[0] += 1
            if fi == 0 and sbi == 0:
                emit_const_compute()
        flush_pending()
        yield

    NCH = SB // 128
    out_toks = []

    def phase2(sbi):
        st = {}

        def loads(j):
            t0 = sbi * SB + j * 128
            rs = row_alloc()
            ps_ = j % 2

            def fn_x(e, h):
                e.dma_start(out=row[rs][:, :], in_=x_d[t0:t0 + 128, :]).then_inc(h, 16)
            emit(S_sp, fn_x, writes=[row_b[rs]], dma_sem=row_sem[rs])

            def fn_p(e, h):
                e.dma_start(out=pin[ps_][:, :], in_=p_d[t0:t0 + 128, :]).then_inc(h, 16)
            emit(S_sp, fn_p, writes=[pin_b[ps_]], dma_sem=pin_sem[ps_])
            st[j] = {"row": rs, "pin": ps_}

        def stage_m(j):
            rs, pi = st[j]["row"], st[j]["pin"]
            b = (j * 128) // BLK
            tc0 = j * 128
            pst, pst_b = ps_alloc()

            def fn_pt(e):
                last = None
                for kk in range(2):
                    last = e.transpose(out=pst[:, kk * 128:(kk + 1) * 128],
                                       in_=pin[pi][:, kk * 128:(kk + 1) * 128], identity=ident_f[:])
                return last
            emit(S_pe, fn_pt, reads=[pin_b[pi], ident_buf], writes=[pst_b])
            pts = j % 3
            emit(S_act, lambda e: e.copy(out=pT[pts][:].rearrange("p k t -> p (k t)"), in_=pst[:, 0:256]),
                 reads=[pst_b, pT_b[pts]], writes=[pT_b[pts]])
            psm = [ps_alloc(), ps_alloc()]

            for hf_ in range(2):
                def fn_m(e, hf=hf_):
                    last = None
                    for k in range(16):
                        last = e.matmul(psm[hf][0][:], lhsT=yT[:, k, tc0:tc0 + 128],
                                        rhs=wout[:, k, hf * 512:(hf + 1) * 512], start=(k == 0), stop=(k == 15))
                    return last
                emit(S_pe, fn_m, reads=wout_bs + [yT_b[k][b] for k in range(16)], writes=[psm[hf_][1]])
            i2 = j % 2
            i3 = j % 3
            for hf in range(2):
                emit(S_dve, lambda e, hf=hf: e.scalar_tensor_tensor(
                    out=row[rs][:, hf * 512:(hf + 1) * 512], in0=row[rs][:, hf * 512:(hf + 1) * 512],
                    scalar=ALPHA, in1=psm[hf][0][:], op0=ALU.mult, op1=ALU.add),
                    reads=[psm[hf][1], row_b[rs]], fills=[row_b[rs]])
            for hf in range(2):
                emit(S_dve, lambda e, hf=hf: e.bn_stats(out=st6p[i3][:, hf, :],
                                                        in_=row[rs][:, hf * 512:(hf + 1) * 512]),
                     reads=[row_b[rs]], fills=[statp_b[i3]])
            emit(S_dve, lambda e: e.bn_aggr(out=mvp[i3][:], in_=st6p[i3][:].rearrange("p a b -> p (a b)")),
                 reads=[statp_b[i3]], writes=[mvp_b[i3]])
            emit(S_pool, lambda e: e.tensor_scalar(
                out=vep[i3][:], in0=mvp[i3][:, 1:2], scalar1=LN_EPS, scalar2=1.0, op0=ALU.add, op1=ALU.mult),
                reads=[mvp_b[i3]], writes=[vep_b[i3]])
            emit(S_pool, lambda e: e.tensor_tensor(
                out=rstdp[i3][:], in0=vep[i3][:], in1=negh[:, 0:1], op=ALU.pow),
                reads=[vep_b[i3], negh_buf], writes=[rstdp_b[i3]])
            emit(S_dve, lambda e: e.tensor_scalar(
                out=xhat[i2][:], in0=row[rs][:], scalar1=mvp[i3][:, 0:1], scalar2=rstdp[i3][:, 0:1],
                op0=ALU.subtract, op1=ALU.mult),
                reads=[row_b[rs], mvp_b[i3], rstdp_b[i3]], writes=[xhat_b[i2]])
            st[j]["pT"] = pts
            st[j]["i2"] = i2
            st[j]["i3"] = i3

        def stage_t(j):
            i2 = st[j]["i2"]
            pst_, psb_b = ps_alloc()
            psb = pst_[:].bitcast(BF16)

            def fn_t(e):
                last = None
                for k in range(8):
                    last = e.transpose(out=psb[:, k * 128:(k + 1) * 128], in_=xhat[i2][:, k * 128:(k + 1) * 128],
                                       identity=ident_b[:])
                return last
            emit(S_pe, fn_t, reads=[xhat_b[i2], identb_buf], writes=[psb_b])
            fast_tail = (sbi == NSB - 1 and j >= NCH - 1)
            for k in range(8):
                if fast_tail and k >= 4:
                    emit(S_dve, lambda e, k=k: e.tensor_scalar(
                        out=xnT[i2][:, k * 128:(k + 1) * 128], in0=psb[:, k * 128:(k + 1) * 128],
                        scalar1=vecs[:, VEC_LG + k:VEC_LG + k + 1], scalar2=vecs[:, VEC_LB + k:VEC_LB + k + 1],
                        op0=ALU.mult, op1=ALU.add),
                        reads=[psb_b, vecs_buf], fills=[xnT_b[i2]])
                    continue
                emit(S_act, lambda e, k=k: e.activation(
                    out=xnT[i2][:, k * 128:(k + 1) * 128], in_=psb[:, k * 128:(k + 1) * 128], func=AF.Identity,
                    bias=vecs[:, VEC_LB + k:VEC_LB + k + 1], scale=vecs[:, VEC_LG + k:VEC_LG + k + 1]),
                    reads=[psb_b, vecs_buf], fills=[xnT_b[i2]])

        def stage_g(j):
            rs, i2, pts, i3 = st[j]["row"], st[j]["i2"], st[j]["pT"], st[j]["i3"]
            t0 = sbi * SB + j * 128
            g_t, g_b = f4c[i2], f4c_b[i2]
            psp = [ps_alloc(), ps_alloc()]

            def fn_pl(e):
                last = None
                for hf in range(2):
                    for kk in range(2):
                        last = e.matmul(psp[hf][0][:], lhsT=pT[pts][:, kk, :],
                                        rhs=wp[:, kk, hf * 512:(hf + 1) * 512], start=(kk == 0), stop=(kk == 1))
                return last
            emit(S_pe, fn_pl, reads=[pT_b[pts], wp_b], writes=[psp[0][1], psp[1][1]])
            psg = [ps_alloc(), ps_alloc()]

            for hf_ in range(2):
                def fn_g(e, hf=hf_):
                    for k in range(8):
                        e.matmul(psg[hf][0][:], lhsT=xnT[i2][:, k * 128:(k + 1) * 128],
                                 rhs=wg[:, k, hf * 512:(hf + 1) * 512], start=(k == 0), stop=False)
                    return e.matmul(psg[hf][0][:], lhsT=ones2[:, :], rhs=bg2[:, hf * 512:(hf + 1) * 512],
                                    start=False, stop=True)
                emit(S_pe, fn_g, reads=[xnT_b[i2], ones2_buf, bg2_buf] + wg_bs, writes=[psg[hf_][1]])
            for hf in range(2):
                emit(S_act, lambda e, hf=hf: e.activation(
                    out=g_t[:, hf * 512:(hf + 1) * 512], in_=psg[hf][0][:], func=AF.Sigmoid),
                    reads=[psg[hf][1]], fills=[g_b])
            emit(S_dve, lambda e: e.scalar_tensor_tensor(
                out=row[rs][:], in0=row[rs][:], scalar=mvp[i3][:, 0:1], in1=gbc[:],
                op0=ALU.subtract, op1=ALU.mult),
                reads=[row_b[rs], mvp_b[i3], gb_buf], writes=[row_b[rs]])
            emit(S_dve, lambda e: e.scalar_tensor_tensor(
                out=row[rs][:], in0=row[rs][:], scalar=rstdp[i3][:, 0:1], in1=bbc[:],
                op0=ALU.mult, op1=ALU.add),
                reads=[row_b[rs], rstdp_b[i3], gb_buf], writes=[row_b[rs]])
            for hf in range(2):
                emit(S_dve, lambda e, hf=hf: e.tensor_tensor(
                    out=g_t[:, hf * 512:(hf + 1) * 512], in0=g_t[:, hf * 512:(hf + 1) * 512],
                    in1=psp[hf][0][:], op=ALU.mult),
                    reads=[g_b, psp[hf][1]], fills=[g_b])
            fin_eng = S_dve
            emit(fin_eng, lambda e: e.tensor_tensor(out=row[rs][:], in0=row[rs][:], in1=g_t[:], op=ALU.add),
                 reads=[row_b[rs], g_b], writes=[row_b[rs]])

            def fn_st(e, h):
                e.dma_start(out=out_d[t0:t0 + 128, :], in_=row[rs][:, :]).then_inc(h, 16)
            out_toks.append(emit(S_sp, fn_st, reads=[row_b[rs]], dma_sem=row_sem[rs]))

        loads(0)
        for j in range(NCH + 1):
            xt_def = []
            if sbi + 1 < NSB and j <= 6:
                todo = {0: ["h", 0], 1: [1, 2]}.get(j, [j + 1])
                for s_ in todo:
                    xt_def.append(xt_job(sbi + 1, s_, defer=True))
            if j + 1 < NCH:
                loads(j + 1)
            if j == 0:
                stage_m(0)
                yield j
                stage_m(1)
                yield j
            elif 2 <= j < NCH:
                stage_m(j)
                yield j
            if 1 <= j:
                stage_g(j - 1)
                yield j
            if j < NCH:
                stage_t(j)
                yield j
            for f_ in xt_def:
                f_()

    warm_b = Buf()
    emit(S_pool, lambda e: e.memset(warm[:], 0.0), writes=[warm_b])
    emit(S_act, lambda e: e.activation(out=warm[:], in_=warm[:], func=AF.Silu), writes=[warm_b])
    emit_fam_load(0)
    defs_ = [xt_job(0, s_, defer=True) for s_ in ["h", 0, 1, 2, 3]]
    for f_ in defs_:
        f_()
    misc_dma(S_pool, wpool[:], poolw_d.rearrange("g (kk p) d -> p g kk d", p=128), [wpool_b])
    emit_fam_load(1)
    emit_small_const_loads()
    if not FLAG_CONST_LATE:
        emit_const_compute()
        emit_bg2()
    g1 = phase1(0, [4, 5, 6, 7])
    for _ in g1:
        pass
    for sbi in range(NSB):
        if dbg and sbi == 0:
            def fn_dbg(e, h):
                e.dma_start(out=dbg_y, in_=yT[:].rearrange("p k t -> p (k t)")).then_inc(h, 16)
            out_toks.append(emit(S_sp, fn_dbg, reads=[yT_b[k][b] for k in range(16) for b in range(NBLK)],
                                 dma_sem=misc_sem[misc_ctr[0]]))
            misc_ctr[0] += 1
        g2 = phase2(sbi)
        g1n = phase1(sbi + 1, None) if sbi + 1 < NSB else None
        budget = 0
        for j in g2:
            if g1n is not None and j >= NCH - 1 and budget < INTERLEAVE_STEPS:
                budget += 1
                if next(g1n, "done") == "done":
                    g1n = None
        if g1n is not None:
            for _ in g1n:
                pass

    fin = {}
    for tok in out_toks:
        _merge(fin, tok)
    for s, v in fin.items():
        S_sp.items.append(("wait", s, v))

    with nc.Block() as block:
        @block.tensor
        def _(e):
            replay(S_pe, e)

        @block.scalar
        def _(e):
            replay(S_act, e)

        @block.vector
        def _(e):
            replay(S_dve, e)

        @block.gpsimd
        def _(e):
            replay(S_pool, e)

        @block.sync
        def _(e):
            replay(S_sp, e)
    es.close()
    return nc


def _chunkT(v):
    return np.ascontiguousarray(np.asarray(v, dtype=np.float32).reshape(-1, 128).T)


def make_in_maps(x, p, w_in, pool_w, pool_scale, sgu_ln_g, sgu_ln_b, sgu_w, sgu_b,
                 w_out, ln_g, ln_b, ple_w, ple_gate_w, ple_gate_b):
    f = np.float32
    x = np.asarray(x, f)
    p = np.asarray(p, f)[0]
    vecs = np.concatenate([_chunkT(pool_scale[0]), _chunkT(sgu_ln_g[0]), _chunkT(sgu_ln_b[0]),
                           _chunkT(ln_g[0]), _chunkT(ln_b[0])], axis=1)
    shared = {
        "w_in": np.ascontiguousarray(np.asarray(w_in, f)[0]),
        "pool_w": np.ascontiguousarray(np.asarray(pool_w, f)[0]),
        "w_out": np.ascontiguousarray(np.asarray(w_out, f)[0]),
        "ple_w": np.ascontiguousarray(np.asarray(ple_w, f)[0]),
        "ple_gate_w": np.ascontiguousarray(np.asarray(ple_gate_w, f)[0]),
        "ple_gate_b": np.ascontiguousarray(np.asarray(ple_gate_b, f)[0].reshape(1, D)),
        "vecs": np.ascontiguousarray(vecs),
        "gbc": np.ascontiguousarray(np.broadcast_to(np.asarray(ln_g, f)[0][None, :], (128, D))),
        "bbc": np.ascontiguousarray(np.broadcast_to(np.asarray(ln_b, f)[0][None, :], (128, D))),
        "bsb": np.ascontiguousarray(np.broadcast_to(np.asarray(sgu_b, f)[0].reshape(1, 512), (128, 512))),
        "tril": np.tril(np.ones((128, 128), f)),
        "ident": np.eye(128, dtype=f),
        "sgw": np.ascontiguousarray(np.asarray(sgu_w, f)[0].transpose(1, 0, 2).reshape(128, 512)),
    }
    in_maps = []
    for c in range(N_CORES):
        b, q = divmod(c, 4)
        t0 = q * TOK
        xc = np.ascontiguousarray(x[b, t0:t0 + TOK])
        if q == 0:
            xh = np.zeros((HALO, D), f)
        else:
            xh = np.ascontiguousarray(x[b, t0 - HALO:t0])
        ic = np.zeros((4, 16), f)
        for g, w in enumerate(WINDOWS):
            for t in range(16):
                ic[g, t] = 1.0 / (min(t + 1, w) if q == 0 else w)
        m = dict(shared)
        m["x"] = xc
        m["xh"] = xh
        m["p"] = np.ascontiguousarray(p[b, t0:t0 + TOK])
        m["icnt"] = np.ascontiguousarray(np.broadcast_to(ic.reshape(1, 64), (128, 64)))
        in_maps.append(m)
    return in_maps


_NC_CACHE = {}


def kernel(**inputs):
    in_maps = make_in_maps(**inputs)
    if "nc" not in _NC_CACHE:
        _NC_CACHE["nc"] = build_program()
    nc = _NC_CACHE["nc"]
    res = run_bass_kernel_spmd(nc, in_maps, core_ids=list(range(N_CORES)))
    out = np.empty((2, 4 * TOK, D), np.float32)
    for c in range(N_CORES):
        b, q = divmod(c, 4)
        out[b, q * TOK:(q + 1) * TOK] = res.results[c]["out"]
    return out
```

```python
import numpy as np
from contextlib import ExitStack

import concourse.bass as bass
import concourse.mybir as mybir
from concourse.bass_utils import run_bass_kernel_spmd

F32 = mybir.dt.float32
BF16 = mybir.dt.bfloat16
AF = mybir.ActivationFunctionType
ALU = mybir.AluOpType

N_CORES = 8
D = 1024
TOK = 2048
SB = 1024
NSB = TOK // SB
BLK = 512
NBLK = SB // BLK
HALO = 16
XTW = HALO + SB
ALPHA = 2.0 ** 0.25
LN_EPS = 1e-5
WINDOWS = (2, 4, 8, 16)
INTERLEAVE_STEPS = 6
import os
FLAG_LATE_PIECES = os.environ.get('K_LATE', '1') == '1'
FLAG_CONST_LATE = os.environ.get('K_CONST', '1') == '1'
FLAG_XT_EARLY = os.environ.get('K_XT', '1') == '1'


class Sem:
    def __init__(self, handle, step):
        self.h = handle
        self.step = step
        self.val = 0

    def advance(self, n=1):
        self.val += self.step * n
        return (self, self.val)


class Buf:
    __slots__ = ("w", "r", "name", "psum")

    def __init__(self, name="", psum=False):
        self.w = {}
        self.r = {}
        self.name = name
        self.psum = psum


class Stream:
    def __init__(self, name, prog, is_pe=False):
        self.name = name
        self.prog = prog
        self.items = []
        self.waited = {}
        self.is_pe = is_pe


def _merge(d, tok):
    s, v = tok
    if d.get(s, 0) < v:
        d[s] = v


def emit(stream, fn, reads=(), writes=(), dma_sem=None, n_dma=1, fills=()):
    deps = {}
    for b in fills:
        for s, v in b.r.items():
            _merge(deps, (s, v))
        for s, v in b.w.items():
            if s is not stream.prog:
                _merge(deps, (s, v))
    for b in reads:
        for s, v in b.w.items():
            _merge(deps, (s, v))
        if b.psum:
            for s, v in b.r.items():
                if s is not stream.prog:
                    _merge(deps, (s, v))
    for b in writes:
        for s, v in b.w.items():
            _merge(deps, (s, v))
        for s, v in b.r.items():
            _merge(deps, (s, v))
    for s, v in deps.items():
        if stream.is_pe and s is stream.prog:
            continue
        if stream.waited.get(s, 0) >= v:
            continue
        stream.waited[s] = v
        stream.items.append(("wait", s, v))
    if dma_sem is not None:
        tok = dma_sem.advance(n_dma)
        stream.items.append(("dma", fn, dma_sem))
    else:
        tok = stream.prog.advance()
        stream.items.append(("op", fn, stream.prog))
    for b in reads:
        _merge(b.r, tok)
    for b in writes:
        b.w = {tok[0]: tok[1]}
        b.r = {}
    for b in fills:
        _merge(b.w, tok)
    return tok


def replay(stream, eng):
    for it in stream.items:
        if it[0] == "wait":
            eng.wait_ge(it[1].h, it[2])
        elif it[0] == "op":
            inst = it[1](eng)
            inst.then_inc(it[2].h, 1)
        else:
            it[1](eng, it[2].h)


def build_program(dbg=False):
    nc = bass.Bass("TRN2", target_bir_lowering=False)
    es = ExitStack()

    def dram_in(name, shape):
        return nc.dram_tensor(name, list(shape), F32, kind="ExternalInput").ap()

    x_d = dram_in("x", [TOK, D])
    xh_d = dram_in("xh", [HALO, D])
    p_d = dram_in("p", [TOK, 256])
    icnt_d = dram_in("icnt", [128, 4 * 16])
    win_d = dram_in("w_in", [D, 5120])
    poolw_d = dram_in("pool_w", [4, 256, 256])
    wout_d = dram_in("w_out", [2048, D])
    wp_d = dram_in("ple_w", [256, D])
    wg_d = dram_in("ple_gate_w", [D, D])
    bg_d = dram_in("ple_gate_b", [1, D])
    vecs_d = dram_in("vecs", [128, 40])
    gbc_d = dram_in("gbc", [128, D])
    bbc_d = dram_in("bbc", [128, D])
    bsb_d = dram_in("bsb", [128, 512])
    tril_d = dram_in("tril", [128, 128])
    ident_d = dram_in("ident", [128, 128])
    sgw_d = dram_in("sgw", [128, 512])
    out_d = nc.dram_tensor("out", [TOK, D], F32, kind="ExternalOutput").ap()
    if dbg:
        dbg_y = nc.dram_tensor("dbg_y", [128, 16 * SB], BF16, kind="ExternalOutput").ap()

    def sb_t(name, shape, dt):
        return es.enter_context(nc.sbuf_tensor("s_" + name, list(shape), dt))

    def ps_t(name, shape, dt):
        return es.enter_context(nc.psum_tensor(name, list(shape), dt))

    def new_sem(name, step):
        return Sem(es.enter_context(nc.semaphore(name)), step)

    S_pe = Stream("pe", new_sem("p_pe", 1), is_pe=True)
    S_act = Stream("act", new_sem("p_act", 1))
    S_dve = Stream("dve", new_sem("p_dve", 1))
    S_pool = Stream("pool", new_sem("p_pool", 1))
    S_sp = Stream("sp", new_sem("p_sp", 1))

    win = [sb_t(f"win{i}", [128, 8, 512], BF16) for i in range(2)]
    win_hb = [[Buf(f"win{i}a"), Buf(f"win{i}b")] for i in range(2)]
    win_sem = [[new_sem(f"d_win{i}a", 16), new_sem(f"d_win{i}b", 16)] for i in range(2)]
    wpool = sb_t("wpool", [128, 4, 2, 256], BF16)
    wout = sb_t("wout", [128, 16, D], BF16)
    wg = sb_t("wg", [128, 8, D], BF16)
    wp = sb_t("wp", [128, 2, D], BF16)
    wmT = sb_t("wmT", [128, 4, 128], BF16)
    xT = sb_t("xT", [128, 8, XTW], BF16)
    yT = sb_t("yT", [128, 16, SB], BF16)
    NROW = 4
    row = [sb_t(f"row{i}", [128, D], F32) for i in range(NROW)]
    row_b = [Buf(f"row{i}") for i in range(NROW)]
    row_sem = [new_sem(f"d_row{i}", 16) for i in range(NROW)]
    abuf = [sb_t(f"abuf{i}", [128, 2, 528], F32) for i in range(2)]
    abuf_b = [Buf() for _ in range(2)]
    abuf_hb = [Buf() for _ in range(2)]
    abuf_sem = [new_sem(f"d_abuf{i}", 16) for i in range(2)]
    f4b = [sb_t(f"f4b{i}", [128, 2, 528], F32) for i in range(2)]
    f4b_b = [Buf() for _ in range(2)]
    f4c = [sb_t(f"f4c{i}", [128, D], F32) for i in range(2)]
    f4c_b = [Buf() for _ in range(2)]
    b2 = [sb_t(f"b2_{i}", [128, D], BF16) for i in range(8)]
    b2_b = [Buf() for _ in range(8)]
    pooled, pooled_b = b2[0:2], b2_b[0:2]
    szb, szb_b = b2[2:4], b2_b[2:4]
    gub, gub_b = b2[4:6], b2_b[4:6]
    vnb, vnb_b = b2[6:8], b2_b[6:8]
    xhat, xhat_b = b2[4:6], b2_b[4:6]
    xnT, xnT_b = b2[6:8], b2_b[6:8]
    NXB = 2
    xb = [sb_t(f"xb{i}", [128, D], BF16) for i in range(NXB)]
    xb_b = [Buf() for _ in range(NXB)]
    xb_sem = [new_sem(f"d_xb{i}", 16) for i in range(NXB)]
    pin = [sb_t(f"pin{i}", [128, 256], F32) for i in range(2)]
    pin_b = [Buf() for _ in range(2)]
    pin_sem = [new_sem(f"d_pin{i}", 16) for i in range(2)]
    pT = [sb_t(f"pT{i}", [128, 2, 128], BF16) for i in range(3)]
    pT_b = [Buf() for _ in range(3)]
    ident_f = sb_t("ident_f", [128, 128], F32)
    ident_b = sb_t("ident_b", [128, 128], BF16)
    vecs = sb_t("vecs", [128, 40], F32)
    cst = sb_t("cst", [128, 8, 128], F32)
    gbc = sb_t("gbc", [128, D], F32)
    bbc = sb_t("bbc", [128, D], F32)
    bsb = sb_t("bsb", [128, 512], F32)
    tril = sb_t("tril", [128, 128], F32)
    icnt = sb_t("icnt", [128, 64], F32)
    ones_f = sb_t("ones_f", [128, 128], F32)
    ones2 = sb_t("ones2", [128, 128], BF16)
    bg2 = sb_t("bg2", [128, D], BF16)
    bgf = f4c[0][0:1, :]
    bgt = f4c[1][0:1, :]
    bghi = b2[6][0:1, :]
    bglo = b2[7][0:1, :]
    wmTf2 = b2[5][:].bitcast(F32)
    sgw2 = b2[4][:].bitcast(F32)
    negh = sb_t("negh", [128, 8], F32)
    warm = sb_t("warm", [128, 2], F32)
    fix_t = sb_t("fix_t", [128, 2, 16], F32)
    st6 = [sb_t(f"st6_{i}", [128, 4, 6], F32) for i in range(2)]
    mv = [sb_t(f"mv{i}", [128, 4, 2], F32) for i in range(2)]
    ve = [sb_t(f"ve{i}", [128, 4], F32) for i in range(2)]
    rstd = [sb_t(f"rstd{i}", [128, 4], F32) for i in range(2)]
    stat_b = [Buf() for _ in range(2)]
    mv_b = [Buf() for _ in range(2)]
    ve_b = [Buf() for _ in range(2)]
    rstd_b = [Buf() for _ in range(2)]
    st6p = [sb_t(f"st6p{i}", [128, 2, 6], F32) for i in range(3)]
    mvp = [sb_t(f"mvp{i}", [128, 2], F32) for i in range(3)]
    vep = [sb_t(f"vep{i}", [128, 1], F32) for i in range(3)]
    rstdp = [sb_t(f"rstdp{i}", [128, 1], F32) for i in range(3)]
    statp_b = [Buf() for _ in range(3)]
    mvp_b = [Buf() for _ in range(3)]
    vep_b = [Buf() for _ in range(3)]
    rstdp_b = [Buf() for _ in range(3)]

    VEC_PS, VEC_SG, VEC_SB, VEC_LG, VEC_LB = 0, 8, 16, 24, 32

    NPS = 8
    psf = [ps_t(f"psf{i}", [128, 512], F32) for i in range(NPS)]
    psf_b = [Buf(f"psf{i}", psum=True) for i in range(NPS)]
    ps_ctr = [0]

    def ps_alloc():
        i = ps_ctr[0] % NPS
        ps_ctr[0] += 1
        return psf[i], psf_b[i]

    const_b = Buf("consts")
    xT_b = [[Buf() for _ in range(2)] for _ in range(8)]
    xTh_b = Buf()
    yT_b = [[Buf() for _ in range(NBLK)] for _ in range(16)]
    wpool_b, wp_b = Buf(), Buf()
    misc_sem = [new_sem(f"d_misc{i}", 16) for i in range(26)]
    misc_ctr = [0]

    def misc_dma(stream, out_ap, in_ap, writes, reads=()):
        sem = misc_sem[misc_ctr[0]]
        misc_ctr[0] += 1

        def fn(e, h):
            e.dma_start(out=out_ap, in_=in_ap).then_inc(h, 16)
        return emit(stream, fn, reads=reads, writes=writes, dma_sem=sem)

    win_v = win_d.rearrange("(k p) e -> p k e", p=128)
    fam_list = [("P", 0), ("P", 1), ("P", 2), ("P", 3),
                ("S", 0), ("S", 1), ("S", 2), ("S", 3), ("Z", 0), ("Z", 1)]
    all_fams = [(sbi, f) for sbi in range(NSB) for f in fam_list]

    def fam_cols(f):
        kind, i = f
        if kind == "P":
            return [(i * 256, 256), (3072 + i * 256, 256)]
        if kind == "Z":
            return [(3072 + 1024 + i * 512, 256), (3072 + 1024 + i * 512 + 256, 256)]
        return [(1024 + i * 256, 256), (2048 + i * 256, 256)]

    def emit_fam_load(fidx):
        if fidx >= len(all_fams):
            return
        slot = fidx % 2
        cols = fam_cols(all_fams[fidx][1])

        for hf, (c0, n) in enumerate(cols):
            def fn(e, h, hf=hf, c0=c0, n=n):
                e.dma_start(out=win[slot][:, :, hf * 256:hf * 256 + n], in_=win_v[:, :, c0:c0 + n]).then_inc(h, 16)
            extra = [win_hb[0][0]] if (fidx == 0 and hf == 1) else []
            emit(S_pool, fn, reads=extra, writes=[win_hb[slot][hf]], dma_sem=win_sem[slot][hf])

    ident_buf, vecs_buf, tril_buf, bsb_buf, icnt_buf, gb_buf = (Buf() for _ in range(6))
    sgw_buf = b2_b[4]
    wmTf_buf = b2_b[5]
    bgf_buf, bgt_buf, bghi_buf, bglo_buf = f4c_b[0], f4c_b[1], b2_b[6], b2_b[7]
    identb_buf = Buf()
    misc_dma(S_sp, ident_f[:], ident_d, [ident_buf])
    misc_dma(S_pool, ident_b[:], ident_d, [identb_buf])

    def emit_small_const_loads():
        misc_dma(S_sp, vecs[:], vecs_d, [vecs_buf])
        misc_dma(S_sp, icnt[:], icnt_d, [icnt_buf])

    ones_buf, negh_buf, ones2_buf, wmT_buf, cst_buf, bg2_buf = (Buf() for _ in range(6))

    def emit_const_compute():
        misc_dma(S_sp, tril[:], tril_d, [tril_buf])
        misc_dma(S_sp, sgw2, sgw_d, [sgw_buf])
        misc_dma(S_sp, bsb[:], bsb_d, [bsb_buf])
        misc_dma(S_sp, bgf, bg_d, [bgf_buf])
        misc_dma(S_sp, gbc[:], gbc_d, [gb_buf])
        misc_dma(S_sp, bbc[:], bbc_d, [gb_buf], reads=[gb_buf])
        emit(S_dve, lambda e: e.memset(ones_f[:], 1.0), writes=[ones_buf])
        emit(S_dve, lambda e: e.memset(negh[:], -0.5), writes=[negh_buf])
        emit(S_dve, lambda e: e.memset(ones2[:], 0.0), writes=[ones2_buf])
        emit(S_dve, lambda e: e.memset(ones2[0:2, :], 1.0), reads=[ones2_buf], writes=[ones2_buf])
        emit(S_pool, lambda e: e.memset(bg2[:], 0.0), writes=[bg2_buf])

        sgw3 = sgw2.rearrange("p (h j) -> p h j", h=4)
        emit(S_dve, lambda e: e.tensor_tensor(out=sgw3, in0=sgw3,
                                              in1=tril[:].unsqueeze(1).to_broadcast([128, 4, 128]),
                                              op=ALU.mult),
             reads=[tril_buf], writes=[sgw_buf])

    ps_w_box = []

    def emit_const_b():
        ps_w, ps_w_b = ps_alloc()
        ps_w_box.append((ps_w, ps_w_b))

        def fn_wmT(e):
            last = None
            for h in range(4):
                last = e.transpose(out=ps_w[:, h * 128:(h + 1) * 128], in_=sgw2[:, h * 128:(h + 1) * 128],
                                   identity=ident_f[:])
            return last
        emit(S_pe, fn_wmT, reads=[sgw_buf, ident_buf], writes=[ps_w_b])
        emit(S_act, lambda e: e.copy(out=wmT[:].rearrange("p h i -> p (h i)"), in_=ps_w[:]),
             reads=[ps_w_b], writes=[wmT_buf])
        emit(S_dve, lambda e: e.tensor_copy(out=wmTf2, in_=ps_w[:]),
             reads=[ps_w_b], writes=[wmTf_buf])

    def emit_const_c():
        ps_r, ps_r_b = ps_alloc()
        emit(S_pe, lambda e: e.matmul(ps_r[:], lhsT=ones_f[:], rhs=wmTf2, start=True, stop=True),
             reads=[ones_buf, wmTf_buf], writes=[ps_r_b])
        for k in range(8):
            h = k // 2
            emit(S_dve, lambda e, k=k, h=h: e.scalar_tensor_tensor(
                out=cst[:, k, :], in0=ps_r[:, h * 128:(h + 1) * 128],
                scalar=vecs[:, VEC_SB + k:VEC_SB + k + 1], in1=bsb[:, h * 128:(h + 1) * 128],
                op0=ALU.mult, op1=ALU.add),
                reads=[ps_r_b, vecs_buf, bsb_buf], fills=[cst_buf])
        emit(S_dve, lambda e: e.tensor_copy(out=bghi, in_=bgf), reads=[bgf_buf], writes=[bghi_buf])
        emit(S_dve, lambda e: e.tensor_copy(out=bgt, in_=bghi), reads=[bghi_buf], writes=[bgt_buf])
        emit(S_dve, lambda e: e.tensor_tensor(out=bgt, in0=bgf, in1=bgt, op=ALU.subtract),
             reads=[bgf_buf, bgt_buf], writes=[bgt_buf])
        emit(S_dve, lambda e: e.tensor_copy(out=bglo, in_=bgt), reads=[bgt_buf], writes=[bglo_buf])

    def emit_bg2():
        misc_dma(S_sp, bg2[0:1, :], bghi, [bg2_buf], reads=[bghi_buf, bg2_buf])
        misc_dma(S_sp, bg2[1:2, :], bglo, [bg2_buf], reads=[bglo_buf, bg2_buf])

    wout_bs = [Buf() for _ in range(4)]
    wg_bs = [Buf() for _ in range(2)]
    wout_v = wout_d.rearrange("(k p) d -> p k d", p=128)
    wg_v = wg_d.rearrange("(k p) d -> p k d", p=128)

    row_ctr = [0]

    def row_alloc():
        i = row_ctr[0] % NROW
        row_ctr[0] += 1
        return i

    pending = []

    def flush_pending():
        while pending:
            pending.pop(0)()

    fam_ctr = [0]
    evac_flip = [0]

    xb_ctr = [0]

    stage_ctr = [0]

    def xt_job(sbi, s, defer=False):
        if sbi == 0:
            pool_ = [(xb[0], xb_b[0]), (xb[1], xb_b[1]), (b2[4], b2_b[4]), (b2[5], b2_b[5]),
                     (b2[6], b2_b[6]), (b2[7], b2_b[7])]
        else:
            pool_ = [(xb[0], xb_b[0]), (xb[1], xb_b[1])]
        xbt, xbb = pool_[xb_ctr[0] % len(pool_)]
        xb_ctr[0] += 1
        si = stage_ctr[0]
        stage_ctr[0] += 1
        if sbi == 0:
            sidx = si % NROW
            stg, stg_bufs, stg_sem = row[sidx][:, :], [row_b[sidx]], row_sem[sidx]
        else:
            sidx = si % 2
            stg = abuf[sidx][:].rearrange("p c t -> p (c t)")[:, 0:D]
            stg_bufs, stg_sem = [abuf_b[sidx], abuf_hb[sidx]], abuf_sem[sidx]
        if s == "h":
            src = xh_d if sbi == 0 else x_d[sbi * SB - HALO:sbi * SB, :]
            np_ = HALO
        else:
            t0 = sbi * SB + s * 128
            src = x_d[t0:t0 + 128, :]
            np_ = 128

        def fn_ld(e, h):
            e.dma_start(out=stg[0:np_, :], in_=src).then_inc(h, 16)
        gate = [win_hb[0][0]] if (sbi == 0 and s in (4, 5, 6, 7)) else []
        emit(S_sp, fn_ld, reads=gate, writes=stg_bufs, dma_sem=stg_sem)
        if sbi == 0:
            emit(S_dve, lambda e: e.tensor_copy(out=xbt[0:np_, :], in_=stg[0:np_, :]),
                 reads=stg_bufs, writes=[xbb])
        else:
            emit(S_act, lambda e: e.copy(out=xbt[0:np_, :], in_=stg[0:np_, :]),
                 reads=stg_bufs, writes=[xbb])
        if defer:
            return lambda: xt_compute(s, xbt, xbb)
        xt_compute(s, xbt, xbb)

    def xt_compute(s, xbt, xbb):
        ps1, ps1_b = ps_alloc()
        pv = ps1[:].bitcast(BF16)
        if s == "h":
            def fn_t(e):
                last = None
                for k in range(8):
                    last = e.transpose(out=pv[:, k * HALO:(k + 1) * HALO],
                                       in_=xbt[0:HALO, k * 128:(k + 1) * 128],
                                       identity=ident_b[0:HALO, 0:HALO])
                return last
            emit(S_pe, fn_t, reads=[xbb, identb_buf], writes=[ps1_b])
            emit(S_dve, lambda e: e.tensor_copy(
                out=xT[:, :, 0:HALO], in_=pv[:, 0:8 * HALO].rearrange("p (k t) -> p k t", k=8)),
                reads=[ps1_b], writes=[xTh_b])
            return

        def fn_t(e):
            last = None
            for k in range(8):
                last = e.transpose(out=pv[:, k * 128:(k + 1) * 128],
                                   in_=xbt[:, k * 128:(k + 1) * 128], identity=ident_b[:])
            return last
        emit(S_pe, fn_t, reads=[xbb, identb_buf], writes=[ps1_b])
        c0 = HALO + s * 128
        src_ap = pv.rearrange("p (k t) -> p k t", k=8)
        dst_ap = xT[:, :, c0:c0 + 128]
        if evac_flip[0] % 2 == 0:
            emit(S_dve, lambda e: e.tensor_copy(out=dst_ap, in_=src_ap),
                 reads=[ps1_b], writes=[xT_b[s][0], xT_b[s][1]])
        else:
            emit(S_act, lambda e: e.copy(out=dst_ap, in_=src_ap),
                 reads=[ps1_b], writes=[xT_b[s][0], xT_b[s][1]])
        evac_flip[0] += 1

    def xT_reads(b):
        r = []
        for s in range(4 * b, 4 * b + 4):
            r += xT_b[s]
        return r

    def p_family(sbi, g, slot):
        w = WINDOWS[g]
        for b in range(NBLK):
            bc0 = HALO + b * BLK
            ab = (fam_ctr[0] * NBLK + b) % 2
            a_t, a_b, ah_b = abuf[ab], abuf_b[ab], abuf_hb[ab]
            psa = [ps_alloc(), ps_alloc()]

            def fn_a(e, psa=psa, bc0=bc0):
                last = None
                for c in range(2):
                    for k in range(8):
                        last = e.matmul(psa[c][0][:], lhsT=win[slot][:, k, c * 128:(c + 1) * 128],
                                        rhs=xT[:, k, bc0:bc0 + BLK], start=(k == 0), stop=(k == 7))
                return last
            emit(S_pe, fn_a, reads=[win_hb[slot][0]] + xT_reads(b), writes=[psa[0][1], psa[1][1]])
            for c in range(2):
                emit(S_act, lambda e, c=c, psa=psa, a_t=a_t: e.copy(out=a_t[:, c, HALO:HALO + BLK],
                                                                    in_=psa[c][0][:]),
                     reads=[psa[c][1]], fills=[a_b])
            if b == 0:
                psh, psh_b = ps_alloc()

                def fn_ah(e, psh=psh):
                    last = None
                    for c in range(2):
                        for k in range(8):
                            last = e.matmul(psh[:, c * HALO:(c + 1) * HALO],
                                            lhsT=win[slot][:, k, c * 128:(c + 1) * 128],
                                            rhs=xT[:, k, 0:HALO], start=(k == 0), stop=(k == 7))
                    return last
                emit(S_pe, fn_ah, reads=[win_hb[slot][0], xTh_b], writes=[psh_b])
                emit(S_act, lambda e, psh=psh, a_t=a_t: e.copy(
                    out=a_t[:, :, 0:HALO], in_=psh[:, 0:2 * HALO].rearrange("p (c t) -> p c t", c=2)),
                    reads=[psh_b, ah_b], writes=[ah_b])
            else:
                oth = abuf[1 - ab]
                emit(S_pool, lambda e, a_t=a_t, oth=oth: e.tensor_copy(out=a_t[:, :, 0:HALO],
                                                                       in_=oth[:, :, BLK:BLK + HALO]),
                     reads=[abuf_b[1 - ab], ah_b], writes=[ah_b])
            yield
            if len(pending) >= 2:
                pending.pop(0)()
            sA, sB = f4b[0], f4b[1]
            chain = [(a_t, 1, sA), (sA, 2, sB), (sB, 4, sA), (sA, 8, sB)]
            lo = 0
            src_b = [a_b, ah_b]
            fin, fin_b = None, None
            for step in range(g + 1):
                src, sh, dst = chain[step]
                lo_new = lo + sh
                dst_b = f4b_b[step % 2]
                emit(S_dve, lambda e, src=src, dst=dst, lo=lo, lo_new=lo_new, sh=sh: e.tensor_tensor(
                    out=dst[:, :, lo_new:528], in0=src[:, :, lo_new:528], in1=src[:, :, lo:528 - sh],
                    op=ALU.add),
                    reads=src_b, writes=[dst_b])
                src_b = [dst_b]
                lo = lo_new
                fin, fin_b = dst, dst_b
            pl, pl_b = pooled[b % 2], pooled_b[b % 2]
            pl3 = pl[:].rearrange("p (c t) -> p c t", c=2)
            emit(S_dve, lambda e, fin=fin, a_t=a_t, pl3=pl3: e.scalar_tensor_tensor(
                out=pl3, in0=fin[:, :, HALO:528], scalar=1.0 / w, in1=a_t[:, :, HALO:528],
                op0=ALU.mult, op1=ALU.subtract),
                reads=[fin_b, a_b], writes=[pl_b])
            if sbi == 0 and b == 0:
                emit(S_dve, lambda e, fin=fin: e.tensor_tensor(
                    out=fix_t[:], in0=fin[:, :, HALO:2 * HALO],
                    in1=icnt[:, g * 16:(g + 1) * 16].unsqueeze(1).to_broadcast([128, 2, 16]),
                    op=ALU.mult),
                    reads=[fin_b, icnt_buf, const_b], writes=[const_b])
                emit(S_dve, lambda e, a_t=a_t, pl3=pl3: e.tensor_tensor(
                    out=pl3[:, :, 0:HALO], in0=fix_t[:], in1=a_t[:, :, HALO:2 * HALO], op=ALU.subtract),
                    reads=[const_b, a_b, pl_b], writes=[pl_b])
            psz = [ps_alloc(), ps_alloc()]

            def fn_z(e, psz=psz, bc0=bc0):
                last = None
                for c in range(2):
                    for k in range(8):
                        last = e.matmul(psz[c][0][:], lhsT=win[slot][:, k, 256 + c * 128:256 + (c + 1) * 128],
                                        rhs=xT[:, k, bc0:bc0 + BLK], start=(k == 0), stop=(k == 7))
                return last
            emit(S_pe, fn_z, reads=[win_hb[slot][1]] + xT_reads(b), writes=[psz[0][1], psz[1][1]])
            sz_t, sz_b = szb[b % 2], szb_b[b % 2]
            for c in range(2):
                emit(S_act, lambda e, c=c, psz=psz, sz_t=sz_t: e.activation(
                    out=sz_t[:, c * BLK:(c + 1) * BLK], in_=psz[c][0][:], func=AF.Silu),
                    reads=[psz[c][1]], fills=[sz_b])

            def pw(pl=pl, pl_b=pl_b, sz_t=sz_t, sz_b=sz_b, b=b):
                psw = [ps_alloc(), ps_alloc()]

                def fn_pw(e):
                    last = None
                    for dc in range(2):
                        for kk in range(2):
                            last = e.matmul(psw[dc][0][:], lhsT=wpool[:, g, kk, dc * 128:(dc + 1) * 128],
                                            rhs=pl[:, kk * BLK:(kk + 1) * BLK], start=(kk == 0), stop=(kk == 1))
                    return last
                emit(S_pe, fn_pw, reads=[wpool_b, pl_b], writes=[psw[0][1], psw[1][1]])
                for dc in range(2):
                    ck = 2 * g + dc
                    emit(S_dve, lambda e, dc=dc, ck=ck: e.scalar_tensor_tensor(
                        out=yT[:, ck, b * BLK:(b + 1) * BLK], in0=psw[dc][0][:],
                        scalar=vecs[:, VEC_PS + ck:VEC_PS + ck + 1], in1=sz_t[:, dc * BLK:(dc + 1) * BLK],
                        op0=ALU.mult, op1=ALU.mult),
                        reads=[psw[dc][1], vecs_buf, sz_b], writes=[yT_b[ck][b]])
            pending.append(pw)
            yield

    z_ctr = [0]
    szq_b = [Buf() for _ in range(4)]

    def z_family(sbi, fz, slot):
        for b in range(NBLK):
            bc0 = HALO + b * BLK
            for c in range(4):
                ck = 8 + 4 * fz + c
                psz, psz_b = ps_alloc()

                def fn_z(e, psz=psz, c=c, bc0=bc0):
                    last = None
                    for k in range(8):
                        last = e.matmul(psz[:], lhsT=win[slot][:, k, c * 128:(c + 1) * 128],
                                        rhs=xT[:, k, bc0:bc0 + BLK], start=(k == 0), stop=(k == 7))
                    return last
                emit(S_pe, fn_z, reads=[win_hb[slot][c // 2]] + xT_reads(b), writes=[psz_b])
                zi = z_ctr[0] % 4
                z_ctr[0] += 1
                zt, zt_b = szb[zi // 2], szq_b[zi]
                emit(S_act, lambda e, psz=psz, zt=zt, zi=zi: e.activation(
                    out=zt[:, (zi % 2) * BLK:(zi % 2 + 1) * BLK], in_=psz[:], func=AF.Silu),
                    reads=[psz_b], writes=[zt_b], fills=[szb_b[zi // 2]])
                emit(S_dve, lambda e, zt=zt, zi=zi, ck=ck, b=b: e.tensor_tensor(
                    out=yT[:, ck, b * BLK:(b + 1) * BLK], in0=yT[:, ck, b * BLK:(b + 1) * BLK],
                    in1=zt[:, (zi % 2) * BLK:(zi % 2 + 1) * BLK], op=ALU.mult),
                    reads=[zt_b, szb_b[zi // 2], yT_b[ck][b]], writes=[yT_b[ck][b]])
                if c == 3:
                    flush_pending()
                yield

    def s_family(sbi, h, slot):
        for b in range(NBLK):
            bc0 = HALO + b * BLK
            it = (fam_ctr[0] * NBLK + b) % 2
            gv_t, gv_b = f4c[it], f4c_b[it]
            vn_t, vn_b = vnb[it], vnb_b[it]
            gu_t, gu_b = gub[it], gub_b[it]
            t1_t, t1_b = f4b[it], f4b_b[it]
            psv = [ps_alloc(), ps_alloc()]

            def fn_v(e, psv=psv, bc0=bc0):
                last = None
                for s in range(4):
                    for k in range(8):
                        last = e.matmul(psv[s // 2][0][:, (s % 2) * 256:(s % 2 + 1) * 256],
                                        lhsT=xT[:, k, bc0 + s * 128:bc0 + (s + 1) * 128],
                                        rhs=win[slot][:, k, 256:512], start=(k == 0), stop=(k == 7))
                return last
            emit(S_pe, fn_v, reads=[win_hb[slot][1]] + xT_reads(b), writes=[psv[0][1], psv[1][1]])
            for i in range(2):
                emit(S_act, lambda e, i=i, psv=psv, gv_t=gv_t: e.activation(
                    out=gv_t[:, i * 512:(i + 1) * 512], in_=psv[i][0][:], func=AF.Gelu_apprx_tanh),
                    reads=[psv[i][1]], fills=[gv_b])
            for s in range(4):
                emit(S_dve, lambda e, s=s, gv_t=gv_t, it=it: e.bn_stats(
                    out=st6[it][:, s, :], in_=gv_t[:, s * 256:(s + 1) * 256]),
                    reads=[gv_b], fills=[stat_b[it]])
            for s in range(4):
                emit(S_dve, lambda e, s=s, it=it: e.bn_aggr(out=mv[it][:, s, :], in_=st6[it][:, s, :]),
                     reads=[stat_b[it]], fills=[mv_b[it]])
            emit(S_pool, lambda e, it=it: e.tensor_scalar(
                out=ve[it][:], in0=mv[it][:, :, 1], scalar1=LN_EPS, scalar2=1.0, op0=ALU.add, op1=ALU.mult),
                reads=[mv_b[it]], writes=[ve_b[it]])
            emit(S_pool, lambda e, it=it: e.tensor_tensor(
                out=rstd[it][:], in0=ve[it][:], in1=negh[:, 0:4], op=ALU.pow),
                reads=[ve_b[it], negh_buf], writes=[rstd_b[it]])
            for s in range(4):
                emit(S_dve, lambda e, s=s, gv_t=gv_t, vn_t=vn_t, it=it: e.tensor_scalar(
                    out=vn_t[:, s * 256:(s + 1) * 256], in0=gv_t[:, s * 256:(s + 1) * 256],
                    scalar1=mv[it][:, s, 0:1], scalar2=rstd[it][:, s:s + 1],
                    op0=ALU.subtract, op1=ALU.mult),
                    reads=[gv_b, mv_b[it], rstd_b[it]], fills=[vn_b])
            yield
            psu = [ps_alloc(), ps_alloc()]

            def fn_u(e, psu=psu, bc0=bc0):
                last = None
                for c in range(2):
                    for k in range(8):
                        last = e.matmul(psu[c][0][:], lhsT=win[slot][:, k, c * 128:(c + 1) * 128],
                                        rhs=xT[:, k, bc0:bc0 + BLK], start=(k == 0), stop=(k == 7))
                return last
            emit(S_pe, fn_u, reads=[win_hb[slot][0]] + xT_reads(b), writes=[psu[0][1], psu[1][1]])
            for c in range(2):
                emit(S_act, lambda e, c=c, psu=psu, gu_t=gu_t: e.activation(
                    out=gu_t[:, c * BLK:(c + 1) * BLK], in_=psu[c][0][:], func=AF.Gelu_apprx_tanh),
                    reads=[psu[c][1]], fills=[gu_b])
            flush_pending()

            def sjob(vn_t=vn_t, vn_b=vn_b, gu_t=gu_t, gu_b=gu_b, t1_t=t1_t, t1_b=t1_b, b=b):
                pss = [ps_alloc(), ps_alloc()]

                def fn_s(e):
                    last = None
                    for c in range(2):
                        for s in range(4):
                            last = e.matmul(pss[c][0][:, s * 128:(s + 1) * 128],
                                            lhsT=vn_t[:, s * 256 + c * 128:s * 256 + (c + 1) * 128],
                                            rhs=wmT[:, h, :], start=True, stop=True)
                    return last
                emit(S_pe, fn_s, reads=[vn_b, wmT_buf], writes=[pss[0][1], pss[1][1]])
                for c in range(2):
                    k = 2 * h + c
                    emit(S_dve, lambda e, c=c, k=k: e.scalar_tensor_tensor(
                        out=t1_t[:, c, 0:BLK].rearrange("p (s i) -> p s i", s=4),
                        in0=pss[c][0][:].rearrange("p (s i) -> p s i", s=4),
                        scalar=vecs[:, VEC_SG + k:VEC_SG + k + 1],
                        in1=cst[:, k:k + 1, :].to_broadcast([128, 4, 128]),
                        op0=ALU.mult, op1=ALU.add),
                        reads=[pss[c][1], vecs_buf, cst_buf], fills=[t1_b])
                ck = 8 + 2 * h
                emit(S_dve, lambda e: e.tensor_tensor(
                    out=yT[:, ck:ck + 2, b * BLK:(b + 1) * BLK], in0=t1_t[:, :, 0:BLK],
                    in1=gu_t[:].rearrange("p (c t) -> p c t", c=2), op=ALU.mult),
                    reads=[t1_b, gu_b], writes=[yT_b[ck][b], yT_b[ck + 1][b]])
            pending.append(sjob)
            yield

    late_pieces = []
    for q in range(4):
        late_pieces.append(lambda q=q: misc_dma(S_pool, wout[:, 4 * q:4 * q + 4, :], wout_v[:, 4 * q:4 * q + 4, :],
                                                [wout_bs[q]]))
    for q in range(2):
        late_pieces.append(lambda q=q: misc_dma(S_pool, wg[:, 4 * q:4 * q + 4, :], wg_v[:, 4 * q:4 * q + 4, :],
                                                [wg_bs[q]]))
    late_pieces.append(lambda: misc_dma(S_pool, wp[:], wp_d.rearrange("(k p) d -> p k d", p=128), [wp_b]))

    def phase1(sbi, first_xt_rest):
        for fi, f in enumerate(fam_list):
            slot = fam_ctr[0] % 2
            kind, i = f
            if kind == "P":
                gen = p_family(sbi, i, slot)
            elif kind == "Z":
                gen = z_family(sbi, i, slot)
            else:
                gen = s_family(sbi, i, slot)
            last_fam = (fi == len(fam_list) - 1)
            nsteps = 0
            for _ in gen:
                nsteps += 1
                if fi == 0 and nsteps == 1 and first_xt_rest:
                    for s_ in first_xt_rest:
                        xt_job(sbi, s_)
                if sbi == 0 and fi == 1 and nsteps == 1:
                    emit_const_b()
                if sbi == 0 and fi == 1 and nsteps == 2:
                    emit_const_c()
                    emit_bg2()
                yield
            emit_fam_load(fam_ctr[0] + 2)
            if FLAG_LATE_PIECES:
                if late_pieces:
                    late_pieces.pop(0)()
            elif fi == 1 and sbi == 0:
                while late_pieces:
                    late_pieces.pop(0)()
            fam_ctr[0] += 1
            if fi == 0 and sbi == 0:
                emit_const_compute()
        flush_pending()
        yield

    NCH = SB // 128
    out_toks = []

    def phase2(sbi):
        st = {}

        def loads(j):
            t0 = sbi * SB + j * 128
            rs = row_alloc()
            ps_ = j % 2

            def fn_x(e, h):
                e.dma_start(out=row[rs][:, :], in_=x_d[t0:t0 + 128, :]).then_inc(h, 16)
            emit(S_sp, fn_x, writes=[row_b[rs]], dma_sem=row_sem[rs])

            def fn_p(e, h):
                e.dma_start(out=pin[ps_][:, :], in_=p_d[t0:t0 + 128, :]).then_inc(h, 16)
            emit(S_sp, fn_p, writes=[pin_b[ps_]], dma_sem=pin_sem[ps_])
            st[j] = {"row": rs, "pin": ps_}

        def stage_m(j):
            rs, pi = st[j]["row"], st[j]["pin"]
            b = (j * 128) // BLK
            tc0 = j * 128
            pst, pst_b = ps_alloc()

            def fn_pt(e):
                last = None
                for kk in range(2):
                    last = e.transpose(out=pst[:, kk * 128:(kk + 1) * 128],
                                       in_=pin[pi][:, kk * 128:(kk + 1) * 128], identity=ident_f[:])
                return last
            emit(S_pe, fn_pt, reads=[pin_b[pi], ident_buf], writes=[pst_b])
            pts = j % 3
            emit(S_act, lambda e: e.copy(out=pT[pts][:].rearrange("p k t -> p (k t)"), in_=pst[:, 0:256]),
                 reads=[pst_b, pT_b[pts]], writes=[pT_b[pts]])
            psm = [ps_alloc(), ps_alloc()]

            for hf_ in range(2):
                def fn_m(e, hf=hf_):
                    last = None
                    for k in range(16):
                        last = e.matmul(psm[hf][0][:], lhsT=yT[:, k, tc0:tc0 + 128],
                                        rhs=wout[:, k, hf * 512:(hf + 1) * 512], start=(k == 0), stop=(k == 15))
                    return last
                emit(S_pe, fn_m, reads=wout_bs + [yT_b[k][b] for k in range(16)], writes=[psm[hf_][1]])
            i2 = j % 2
            i3 = j % 3
            for hf in range(2):
                emit(S_dve, lambda e, hf=hf: e.scalar_tensor_tensor(
                    out=row[rs][:, hf * 512:(hf + 1) * 512], in0=row[rs][:, hf * 512:(hf + 1) * 512],
                    scalar=ALPHA, in1=psm[hf][0][:], op0=ALU.mult, op1=ALU.add),
                    reads=[psm[hf][1], row_b[rs]], fills=[row_b[rs]])
            for hf in range(2):
                emit(S_dve, lambda e, hf=hf: e.bn_stats(out=st6p[i3][:, hf, :],
                                                        in_=row[rs][:, hf * 512:(hf + 1) * 512]),
                     reads=[row_b[rs]], fills=[statp_b[i3]])
            emit(S_dve, lambda e: e.bn_aggr(out=mvp[i3][:], in_=st6p[i3][:].rearrange("p a b -> p (a b)")),
                 reads=[statp_b[i3]], writes=[mvp_b[i3]])
            emit(S_pool, lambda e: e.tensor_scalar(
                out=vep[i3][:], in0=mvp[i3][:, 1:2], scalar1=LN_EPS, scalar2=1.0, op0=ALU.add, op1=ALU.mult),
                reads=[mvp_b[i3]], writes=[vep_b[i3]])
            emit(S_pool, lambda e: e.tensor_tensor(
                out=rstdp[i3][:], in0=vep[i3][:], in1=negh[:, 0:1], op=ALU.pow),
                reads=[vep_b[i3], negh_buf], writes=[rstdp_b[i3]])
            emit(S_dve, lambda e: e.tensor_scalar(
                out=xhat[i2][:], in0=row[rs][:], scalar1=mvp[i3][:, 0:1], scalar2=rstdp[i3][:, 0:1],
                op0=ALU.subtract, op1=ALU.mult),
                reads=[row_b[rs], mvp_b[i3], rstdp_b[i3]], writes=[xhat_b[i2]])
            st[j]["pT"] = pts
            st[j]["i2"] = i2
            st[j]["i3"] = i3

        def stage_t(j):
            i2 = st[j]["i2"]
            pst_, psb_b = ps_alloc()
            psb = pst_[:].bitcast(BF16)

            def fn_t(e):
                last = None
                for k in range(8):
                    last = e.transpose(out=psb[:, k * 128:(k + 1) * 128], in_=xhat[i2][:, k * 128:(k + 1) * 128],
                                       identity=ident_b[:])
                return last
            emit(S_pe, fn_t, reads=[xhat_b[i2], identb_buf], writes=[psb_b])
            fast_tail = False
            for k in range(8):
                if fast_tail and k >= 4:
                    emit(S_dve, lambda e, k=k: e.tensor_scalar(
                        out=xnT[i2][:, k * 128:(k + 1) * 128], in0=psb[:, k * 128:(k + 1) * 128],
                        scalar1=vecs[:, VEC_LG + k:VEC_LG + k + 1], scalar2=vecs[:, VEC_LB + k:VEC_LB + k + 1],
                        op0=ALU.mult, op1=ALU.add),
                        reads=[psb_b, vecs_buf], fills=[xnT_b[i2]])
                    continue
                emit(S_act, lambda e, k=k: e.activation(
                    out=xnT[i2][:, k * 128:(k + 1) * 128], in_=psb[:, k * 128:(k + 1) * 128], func=AF.Identity,
                    bias=vecs[:, VEC_LB + k:VEC_LB + k + 1], scale=vecs[:, VEC_LG + k:VEC_LG + k + 1]),
                    reads=[psb_b, vecs_buf], fills=[xnT_b[i2]])

        def stage_g(j):
            rs, i2, pts, i3 = st[j]["row"], st[j]["i2"], st[j]["pT"], st[j]["i3"]
            t0 = sbi * SB + j * 128
            g_t, g_b = f4c[i2], f4c_b[i2]
            psp = [ps_alloc(), ps_alloc()]

            def fn_pl(e):
                last = None
                for hf in range(2):
                    for kk in range(2):
                        last = e.matmul(psp[hf][0][:], lhsT=pT[pts][:, kk, :],
                                        rhs=wp[:, kk, hf * 512:(hf + 1) * 512], start=(kk == 0), stop=(kk == 1))
                return last
            emit(S_pe, fn_pl, reads=[pT_b[pts], wp_b], writes=[psp[0][1], psp[1][1]])
            psg = [ps_alloc(), ps_alloc()]

            for hf_ in range(2):
                def fn_g(e, hf=hf_):
                    for k in range(8):
                        e.matmul(psg[hf][0][:], lhsT=xnT[i2][:, k * 128:(k + 1) * 128],
                                 rhs=wg[:, k, hf * 512:(hf + 1) * 512], start=(k == 0), stop=False)
                    return e.matmul(psg[hf][0][:], lhsT=ones2[:, :], rhs=bg2[:, hf * 512:(hf + 1) * 512],
                                    start=False, stop=True)
                emit(S_pe, fn_g, reads=[xnT_b[i2], ones2_buf, bg2_buf] + wg_bs, writes=[psg[hf_][1]])
            for hf in range(2):
                emit(S_act, lambda e, hf=hf: e.activation(
                    out=g_t[:, hf * 512:(hf + 1) * 512], in_=psg[hf][0][:], func=AF.Sigmoid),
                    reads=[psg[hf][1]], fills=[g_b])
            emit(S_dve, lambda e: e.scalar_tensor_tensor(
                out=row[rs][:], in0=row[rs][:], scalar=mvp[i3][:, 0:1], in1=gbc[:],
                op0=ALU.subtract, op1=ALU.mult),
                reads=[row_b[rs], mvp_b[i3], gb_buf], writes=[row_b[rs]])
            emit(S_dve, lambda e: e.scalar_tensor_tensor(
                out=row[rs][:], in0=row[rs][:], scalar=rstdp[i3][:, 0:1], in1=bbc[:],
                op0=ALU.mult, op1=ALU.add),
                reads=[row_b[rs], rstdp_b[i3], gb_buf], writes=[row_b[rs]])
            for hf in range(2):
                emit(S_dve, lambda e, hf=hf: e.tensor_tensor(
                    out=g_t[:, hf * 512:(hf + 1) * 512], in0=g_t[:, hf * 512:(hf + 1) * 512],
                    in1=psp[hf][0][:], op=ALU.mult),
                    reads=[g_b, psp[hf][1]], fills=[g_b])
            fin_eng = S_dve
            emit(fin_eng, lambda e: e.tensor_tensor(out=row[rs][:], in0=row[rs][:], in1=g_t[:], op=ALU.add),
                 reads=[row_b[rs], g_b], writes=[row_b[rs]])

            def fn_st(e, h):
                e.dma_start(out=out_d[t0:t0 + 128, :], in_=row[rs][:, :]).then_inc(h, 16)
            out_toks.append(emit(S_sp, fn_st, reads=[row_b[rs]], dma_sem=row_sem[rs]))

        loads(0)
        for j in range(NCH + 1):
            xt_def = []
            if sbi + 1 < NSB and j <= 6:
                todo = {0: ["h", 0], 1: [1, 2]}.get(j, [j + 1])
                for s_ in todo:
                    xt_def.append(xt_job(sbi + 1, s_, defer=True))
            if j + 1 < NCH:
                loads(j + 1)
            if j == 0:
                stage_m(0)
                yield j
                stage_m(1)
                yield j
            elif 2 <= j < NCH:
                stage_m(j)
                yield j
            if 1 <= j:
                stage_g(j - 1)
                yield j
            if j < NCH:
                stage_t(j)
                yield j
            for f_ in xt_def:
                f_()

    warm_b = Buf()
    emit(S_pool, lambda e: e.memset(warm[:], 0.0), writes=[warm_b])
    emit(S_act, lambda e: e.activation(out=warm[:], in_=warm[:], func=AF.Silu), writes=[warm_b])
    emit_fam_load(0)
    defs_ = [xt_job(0, s_, defer=True) for s_ in ["h", 0, 1, 2, 3]]
    for f_ in defs_:
        f_()
    misc_dma(S_pool, wpool[:], poolw_d.rearrange("g (kk p) d -> p g kk d", p=128), [wpool_b])
    emit_fam_load(1)
    emit_small_const_loads()
    if not FLAG_CONST_LATE:
        emit_const_compute()
        emit_bg2()
    g1 = phase1(0, [4, 5, 6, 7])
    for _ in g1:
        pass
    for sbi in range(NSB):
        if dbg and sbi == 0:
            def fn_dbg(e, h):
                e.dma_start(out=dbg_y, in_=yT[:].rearrange("p k t -> p (k t)")).then_inc(h, 16)
            out_toks.append(emit(S_sp, fn_dbg, reads=[yT_b[k][b] for k in range(16) for b in range(NBLK)],
                                 dma_sem=misc_sem[misc_ctr[0]]))
            misc_ctr[0] += 1
        g2 = phase2(sbi)
        g1n = phase1(sbi + 1, None) if sbi + 1 < NSB else None
        budget = 0
        for j in g2:
            if g1n is not None and j >= NCH - 1 and budget < INTERLEAVE_STEPS:
                budget += 1
                if next(g1n, "done") == "done":
                    g1n = None
        if g1n is not None:
            for _ in g1n:
                pass

    fin = {}
    for tok in out_toks:
        _merge(fin, tok)
    for s, v in fin.items():
        S_sp.items.append(("wait", s, v))

    with nc.Block() as block:
        @block.tensor
        def _(e):
            replay(S_pe, e)

        @block.scalar
        def _(e):
            replay(S_act, e)

        @block.vector
        def _(e):
            replay(S_dve, e)

        @block.gpsimd
        def _(e):
            replay(S_pool, e)

        @block.sync
        def _(e):
            replay(S_sp, e)
    es.close()
    return nc


def _chunkT(v):
    return np.ascontiguousarray(np.asarray(v, dtype=np.float32).reshape(-1, 128).T)


def make_in_maps(x, p, w_in, pool_w, pool_scale, sgu_ln_g, sgu_ln_b, sgu_w, sgu_b,
                 w_out, ln_g, ln_b, ple_w, ple_gate_w, ple_gate_b):
    f = np.float32
    x = np.asarray(x, f)
    p = np.asarray(p, f)[0]
    vecs = np.concatenate([_chunkT(pool_scale[0]), _chunkT(sgu_ln_g[0]), _chunkT(sgu_ln_b[0]),
                           _chunkT(ln_g[0]), _chunkT(ln_b[0])], axis=1)
    shared = {
        "w_in": np.ascontiguousarray(np.asarray(w_in, f)[0]),
        "pool_w": np.ascontiguousarray(np.asarray(pool_w, f)[0]),
        "w_out": np.ascontiguousarray(np.asarray(w_out, f)[0]),
        "ple_w": np.ascontiguousarray(np.asarray(ple_w, f)[0]),
        "ple_gate_w": np.ascontiguousarray(np.asarray(ple_gate_w, f)[0]),
        "ple_gate_b": np.ascontiguousarray(np.asarray(ple_gate_b, f)[0].reshape(1, D)),
        "vecs": np.ascontiguousarray(vecs),
        "gbc": np.ascontiguousarray(np.broadcast_to(np.asarray(ln_g, f)[0][None, :], (128, D))),
        "bbc": np.ascontiguousarray(np.broadcast_to(np.asarray(ln_b, f)[0][None, :], (128, D))),
        "bsb": np.ascontiguousarray(np.broadcast_to(np.asarray(sgu_b, f)[0].reshape(1, 512), (128, 512))),
        "tril": np.tril(np.ones((128, 128), f)),
        "ident": np.eye(128, dtype=f),
        "sgw": np.ascontiguousarray(np.asarray(sgu_w, f)[0].transpose(1, 0, 2).reshape(128, 512)),
    }
    in_maps = []
    for c in range(N_CORES):
        b, q = divmod(c, 4)
        t0 = q * TOK
        xc = np.ascontiguousarray(x[b, t0:t0 + TOK])
        if q == 0:
            xh = np.zeros((HALO, D), f)
        else:
            xh = np.ascontiguousarray(x[b, t0 - HALO:t0])
        ic = np.zeros((4, 16), f)
        for g, w in enumerate(WINDOWS):
            for t in range(16):
                ic[g, t] = 1.0 / (min(t + 1, w) if q == 0 else w)
        m = dict(shared)
        m["x"] = xc
        m["xh"] = xh
        m["p"] = np.ascontiguousarray(p[b, t0:t0 + TOK])
        m["icnt"] = np.ascontiguousarray(np.broadcast_to(ic.reshape(1, 64), (128, 64)))
        in_maps.append(m)
    return in_maps


_NC_CACHE = {}


def kernel(**inputs):
    in_maps = make_in_maps(**inputs)
    if "nc" not in _NC_CACHE:
        _NC_CACHE["nc"] = build_program()
    nc = _NC_CACHE["nc"]
    res = run_bass_kernel_spmd(nc, in_maps, core_ids=list(range(N_CORES)))
    out = np.empty((2, 4 * TOK, D), np.float32)
    for c in range(N_CORES):
        b, q = divmod(c, 4)
        out[b, q * TOK:(q + 1) * TOK] = res.results[c]["out"]
    return out
```

```python
import numpy as np
from contextlib import ExitStack

import concourse.bass as bass
import concourse.mybir as mybir
from concourse.bass_utils import run_bass_kernel_spmd

F32 = mybir.dt.float32
BF16 = mybir.dt.bfloat16
AF = mybir.ActivationFunctionType
ALU = mybir.AluOpType

N_CORES = 8
D = 1024
TOK = 2048
SB = 1024
NSB = TOK // SB
BLK = 512
NBLK = SB // BLK
HALO = 16
XTW = HALO + SB
ALPHA = 2.0 ** 0.25
LN_EPS = 1e-5
WINDOWS = (2, 4, 8, 16)
INTERLEAVE_STEPS = 6
import os
FLAG_LATE_PIECES = os.environ.get('K_LATE', '1') == '1'
FLAG_CONST_LATE = os.environ.get('K_CONST', '1') == '1'
FLAG_XT_EARLY = os.environ.get('K_XT', '1') == '1'


class Sem:
    def __init__(self, handle, step):
        self.h = handle
        self.step = step
        self.val = 0

    def advance(self, n=1):
        self.val += self.step * n
        return (self, self.val)


class Buf:
    __slots__ = ("w", "r", "name", "psum")

    def __init__(self, name="", psum=False):
        self.w = {}
        self.r = {}
        self.name = name
        self.psum = psum


class Stream:
    def __init__(self, name, prog, is_pe=False):
        self.name = name
        self.prog = prog
        self.items = []
        self.waited = {}
        self.is_pe = is_pe


def _merge(d, tok):
    s, v = tok
    if d.get(s, 0) < v:
        d[s] = v


def emit(stream, fn, reads=(), writes=(), dma_sem=None, n_dma=1, fills=()):
    deps = {}
    for b in fills:
        for s, v in b.r.items():
            _merge(deps, (s, v))
        for s, v in b.w.items():
            if s is not stream.prog:
                _merge(deps, (s, v))
    for b in reads:
        for s, v in b.w.items():
            _merge(deps, (s, v))
        if b.psum:
            for s, v in b.r.items():
                if s is not stream.prog:
                    _merge(deps, (s, v))
    for b in writes:
        for s, v in b.w.items():
            _merge(deps, (s, v))
        for s, v in b.r.items():
            _merge(deps, (s, v))
    for s, v in deps.items():
        if stream.is_pe and s is stream.prog:
            continue
        if stream.waited.get(s, 0) >= v:
            continue
        stream.waited[s] = v
        stream.items.append(("wait", s, v))
    if dma_sem is not None:
        tok = dma_sem.advance(n_dma)
        stream.items.append(("dma", fn, dma_sem))
    else:
        tok = stream.prog.advance()
        stream.items.append(("op", fn, stream.prog))
    for b in reads:
        _merge(b.r, tok)
    for b in writes:
        b.w = {tok[0]: tok[1]}
        b.r = {}
    for b in fills:
        _merge(b.w, tok)
    return tok


def replay(stream, eng):
    for it in stream.items:
        if it[0] == "wait":
            eng.wait_ge(it[1].h, it[2])
        elif it[0] == "op":
            inst = it[1](eng)
            inst.then_inc(it[2].h, 1)
        else:
            it[1](eng, it[2].h)


def build_program(dbg=False):
    nc = bass.Bass("TRN2", target_bir_lowering=False)
    es = ExitStack()

    def dram_in(name, shape):
        return nc.dram_tensor(name, list(shape), F32, kind="ExternalInput").ap()

    x_d = dram_in("x", [TOK, D])
    xh_d = dram_in("xh", [HALO, D])
    p_d = dram_in("p", [TOK, 256])
    icnt_d = dram_in("icnt", [128, 4 * 16])
    win_d = dram_in("w_in", [D, 5120])
    poolw_d = dram_in("pool_w", [4, 256, 256])
    wout_d = dram_in("w_out", [2048, D])
    wp_d = dram_in("ple_w", [256, D])
    wg_d = dram_in("ple_gate_w", [D, D])
    bg_d = dram_in("ple_gate_b", [1, D])
    vecs_d = dram_in("vecs", [128, 40])
    gbc_d = dram_in("gbc", [128, D])
    bbc_d = dram_in("bbc", [128, D])
    bsb_d = dram_in("bsb", [128, 512])
    tril_d = dram_in("tril", [128, 128])
    ident_d = dram_in("ident", [128, 128])
    sgw_d = dram_in("sgw", [128, 512])
    out_d = nc.dram_tensor("out", [TOK, D], F32, kind="ExternalOutput").ap()
    if dbg:
        dbg_y = nc.dram_tensor("dbg_y", [128, 16 * SB], BF16, kind="ExternalOutput").ap()

    def sb_t(name, shape, dt):
        return es.enter_context(nc.sbuf_tensor("s_" + name, list(shape), dt))

    def ps_t(name, shape, dt):
        return es.enter_context(nc.psum_tensor(name, list(shape), dt))

    def new_sem(name, step):
        return Sem(es.enter_context(nc.semaphore(name)), step)

    S_pe = Stream("pe", new_sem("p_pe", 1), is_pe=True)
    S_act = Stream("act", new_sem("p_act", 1))
    S_dve = Stream("dve", new_sem("p_dve", 1))
    S_pool = Stream("pool", new_sem("p_pool", 1))
    S_sp = Stream("sp", new_sem("p_sp", 1))

    win = [sb_t(f"win{i}", [128, 8, 512], BF16) for i in range(2)]
    win_hb = [[Buf(f"win{i}a"), Buf(f"win{i}b")] for i in range(2)]
    win_sem = [[new_sem(f"d_win{i}a", 16), new_sem(f"d_win{i}b", 16)] for i in range(2)]
    wpool = sb_t("wpool", [128, 4, 2, 256], BF16)
    wout = sb_t("wout", [128, 16, D], BF16)
    wg = sb_t("wg", [128, 8, D], BF16)
    wp = sb_t("wp", [128, 2, D], BF16)
    wmT = sb_t("wmT", [128, 4, 128], BF16)
    xT = sb_t("xT", [128, 8, XTW], BF16)
    yT = sb_t("yT", [128, 16, SB], BF16)
    NROW = 4
    row = [sb_t(f"row{i}", [128, D], F32) for i in range(NROW)]
    row_b = [Buf(f"row{i}") for i in range(NROW)]
    row_sem = [new_sem(f"d_row{i}", 16) for i in range(NROW)]
    abuf = [sb_t(f"abuf{i}", [128, 2, 528], F32) for i in range(2)]
    abuf_b = [Buf() for _ in range(2)]
    abuf_hb = [Buf() for _ in range(2)]
    abuf_sem = [new_sem(f"d_abuf{i}", 16) for i in range(2)]
    f4b = [sb_t(f"f4b{i}", [128, 2, 528], F32) for i in range(2)]
    f4b_b = [Buf() for _ in range(2)]
    f4c = [sb_t(f"f4c{i}", [128, D], F32) for i in range(2)]
    f4c_b = [Buf() for _ in range(2)]
    b2 = [sb_t(f"b2_{i}", [128, D], BF16) for i in range(8)]
    b2_b = [Buf() for _ in range(8)]
    pooled, pooled_b = b2[0:2], b2_b[0:2]
    szb, szb_b = b2[2:4], b2_b[2:4]
    gub, gub_b = b2[4:6], b2_b[4:6]
    vnb, vnb_b = b2[6:8], b2_b[6:8]
    xhat, xhat_b = b2[4:6], b2_b[4:6]
    xnT, xnT_b = b2[6:8], b2_b[6:8]
    NXB = 2
    xb = [sb_t(f"xb{i}", [128, D], BF16) for i in range(NXB)]
    xb_b = [Buf() for _ in range(NXB)]
    xb_sem = [new_sem(f"d_xb{i}", 16) for i in range(NXB)]
    pin = [sb_t(f"pin{i}", [128, 256], F32) for i in range(2)]
    pin_b = [Buf() for _ in range(2)]
    pin_sem = [new_sem(f"d_pin{i}", 16) for i in range(2)]
    pT = [sb_t(f"pT{i}", [128, 2, 128], BF16) for i in range(3)]
    pT_b = [Buf() for _ in range(3)]
    ident_f = sb_t("ident_f", [128, 128], F32)
    ident_b = sb_t("ident_b", [128, 128], BF16)
    vecs = sb_t("vecs", [128, 40], F32)
    cst = sb_t("cst", [128, 8, 128], F32)
    gbc = sb_t("gbc", [128, D], F32)
    bbc = sb_t("bbc", [128, D], F32)
    bsb = sb_t("bsb", [128, 512], F32)
    tril = sb_t("tril", [128, 128], F32)
    icnt = sb_t("icnt", [128, 64], F32)
    ones_f = sb_t("ones_f", [128, 128], F32)
    ones2 = sb_t("ones2", [128, 128], BF16)
    bg2 = sb_t("bg2", [128, D], BF16)
    bgf = f4c[0][0:1, :]
    bgt = f4c[1][0:1, :]
    bghi = b2[6][0:1, :]
    bglo = b2[7][0:1, :]
    wmTf2 = b2[5][:].bitcast(F32)
    sgw2 = b2[4][:].bitcast(F32)
    negh = sb_t("negh", [128, 8], F32)
    warm = sb_t("warm", [128, 2], F32)
    fix_t = sb_t("fix_t", [128, 2, 16], F32)
    st6 = [sb_t(f"st6_{i}", [128, 4, 6], F32) for i in range(2)]
    mv = [sb_t(f"mv{i}", [128, 4, 2], F32) for i in range(2)]
    ve = [sb_t(f"ve{i}", [128, 4], F32) for i in range(2)]
    rstd = [sb_t(f"rstd{i}", [128, 4], F32) for i in range(2)]
    stat_b = [Buf() for _ in range(2)]
    mv_b = [Buf() for _ in range(2)]
    ve_b = [Buf() for _ in range(2)]
    rstd_b = [Buf() for _ in range(2)]
    st6p = [sb_t(f"st6p{i}", [128, 2, 6], F32) for i in range(3)]
    mvp = [sb_t(f"mvp{i}", [128, 2], F32) for i in range(3)]
    vep = [sb_t(f"vep{i}", [128, 1], F32) for i in range(3)]
    rstdp = [sb_t(f"rstdp{i}", [128, 1], F32) for i in range(3)]
    statp_b = [Buf() for _ in range(3)]
    mvp_b = [Buf() for _ in range(3)]
    vep_b = [Buf() for _ in range(3)]
    rstdp_b = [Buf() for _ in range(3)]

    VEC_PS, VEC_SG, VEC_SB, VEC_LG, VEC_LB = 0, 8, 16, 24, 32

    NPS = 8
    psf = [ps_t(f"psf{i}", [128, 512], F32) for i in range(NPS)]
    psf_b = [Buf(f"psf{i}", psum=True) for i in range(NPS)]
    ps_ctr = [0]

    def ps_alloc():
        i = ps_ctr[0] % NPS
        ps_ctr[0] += 1
        return psf[i], psf_b[i]

    const_b = Buf("consts")
    xT_b = [[Buf() for _ in range(2)] for _ in range(8)]
    xTh_b = Buf()
    yT_b = [[Buf() for _ in range(NBLK)] for _ in range(16)]
    wpool_b, wp_b = Buf(), Buf()
    misc_sem = [new_sem(f"d_misc{i}", 16) for i in range(26)]
    misc_ctr = [0]

    def misc_dma(stream, out_ap, in_ap, writes, reads=()):
        sem = misc_sem[misc_ctr[0]]
        misc_ctr[0] += 1

        def fn(e, h):
            e.dma_start(out=out_ap, in_=in_ap).then_inc(h, 16)
        return emit(stream, fn, reads=reads, writes=writes, dma_sem=sem)

    win_v = win_d.rearrange("(k p) e -> p k e", p=128)
    fam_list = [("P", 0), ("P", 1), ("P", 2), ("P", 3),
                ("S", 0), ("S", 1), ("S", 2), ("S", 3), ("Z", 0), ("Z", 1)]
    all_fams = [(sbi, f) for sbi in range(NSB) for f in fam_list]

    def fam_cols(f):
        kind, i = f
        if kind == "P":
            return [(i * 256, 256), (3072 + i * 256, 256)]
        if kind == "Z":
            return [(3072 + 1024 + i * 512, 256), (3072 + 1024 + i * 512 + 256, 256)]
        return [(1024 + i * 256, 256), (2048 + i * 256, 256)]

    def emit_fam_load(fidx):
        if fidx >= len(all_fams):
            return
        slot = fidx % 2
        cols = fam_cols(all_fams[fidx][1])

        for hf, (c0, n) in enumerate(cols):
            def fn(e, h, hf=hf, c0=c0, n=n):
                e.dma_start(out=win[slot][:, :, hf * 256:hf * 256 + n], in_=win_v[:, :, c0:c0 + n]).then_inc(h, 16)
            extra = [win_hb[0][0]] if (fidx == 0 and hf == 1) else []
            emit(S_pool, fn, reads=extra, writes=[win_hb[slot][hf]], dma_sem=win_sem[slot][hf])

    ident_buf, vecs_buf, tril_buf, bsb_buf, icnt_buf, gb_buf = (Buf() for _ in range(6))
    sgw_buf = b2_b[4]
    wmTf_buf = b2_b[5]
    bgf_buf, bgt_buf, bghi_buf, bglo_buf = f4c_b[0], f4c_b[1], b2_b[6], b2_b[7]
    identb_buf = Buf()
    misc_dma(S_sp, ident_f[:], ident_d, [ident_buf])
    misc_dma(S_pool, ident_b[:], ident_d, [identb_buf])

    def emit_small_const_loads():
        misc_dma(S_sp, vecs[:], vecs_d, [vecs_buf])
        misc_dma(S_sp, icnt[:], icnt_d, [icnt_buf])

    ones_buf, negh_buf, ones2_buf, wmT_buf, cst_buf, bg2_buf = (Buf() for _ in range(6))

    def emit_const_compute():
        misc_dma(S_sp, tril[:], tril_d, [tril_buf])
        misc_dma(S_sp, sgw2, sgw_d, [sgw_buf])
        misc_dma(S_sp, bsb[:], bsb_d, [bsb_buf])
        misc_dma(S_sp, bgf, bg_d, [bgf_buf])
        misc_dma(S_sp, gbc[:], gbc_d, [gb_buf])
        misc_dma(S_sp, bbc[:], bbc_d, [gb_buf], reads=[gb_buf])
        emit(S_dve, lambda e: e.memset(ones_f[:], 1.0), writes=[ones_buf])
        emit(S_dve, lambda e: e.memset(negh[:], -0.5), writes=[negh_buf])
        emit(S_dve, lambda e: e.memset(ones2[:], 0.0), writes=[ones2_buf])
        emit(S_dve, lambda e: e.memset(ones2[0:2, :], 1.0), reads=[ones2_buf], writes=[ones2_buf])
        emit(S_pool, lambda e: e.memset(bg2[:], 0.0), writes=[bg2_buf])

        sgw3 = sgw2.rearrange("p (h j) -> p h j", h=4)
        emit(S_dve, lambda e: e.tensor_tensor(out=sgw3, in0=sgw3,
                                              in1=tril[:].unsqueeze(1).to_broadcast([128, 4, 128]),
                                              op=ALU.mult),
             reads=[tril_buf], writes=[sgw_buf])

    ps_w_box = []

    def emit_const_b():
        ps_w, ps_w_b = ps_alloc()
        ps_w_box.append((ps_w, ps_w_b))

        def fn_wmT(e):
            last = None
            for h in range(4):
                last = e.transpose(out=ps_w[:, h * 128:(h + 1) * 128], in_=sgw2[:, h * 128:(h + 1) * 128],
                                   identity=ident_f[:])
            return last
        emit(S_pe, fn_wmT, reads=[sgw_buf, ident_buf], writes=[ps_w_b])
        emit(S_act, lambda e: e.copy(out=wmT[:].rearrange("p h i -> p (h i)"), in_=ps_w[:]),
             reads=[ps_w_b], writes=[wmT_buf])
        emit(S_dve, lambda e: e.tensor_copy(out=wmTf2, in_=ps_w[:]),
             reads=[ps_w_b], writes=[wmTf_buf])

    def emit_const_c():
        ps_r, ps_r_b = ps_alloc()
        emit(S_pe, lambda e: e.matmul(ps_r[:], lhsT=ones_f[:], rhs=wmTf2, start=True, stop=True),
             reads=[ones_buf, wmTf_buf], writes=[ps_r_b])
        for k in range(8):
            h = k // 2
            emit(S_dve, lambda e, k=k, h=h: e.scalar_tensor_tensor(
                out=cst[:, k, :], in0=ps_r[:, h * 128:(h + 1) * 128],
                scalar=vecs[:, VEC_SB + k:VEC_SB + k + 1], in1=bsb[:, h * 128:(h + 1) * 128],
                op0=ALU.mult, op1=ALU.add),
                reads=[ps_r_b, vecs_buf, bsb_buf], fills=[cst_buf])
        emit(S_dve, lambda e: e.tensor_copy(out=bghi, in_=bgf), reads=[bgf_buf], writes=[bghi_buf])
        emit(S_dve, lambda e: e.tensor_copy(out=bgt, in_=bghi), reads=[bghi_buf], writes=[bgt_buf])
        emit(S_dve, lambda e: e.tensor_tensor(out=bgt, in0=bgf, in1=bgt, op=ALU.subtract),
             reads=[bgf_buf, bgt_buf], writes=[bgt_buf])
        emit(S_dve, lambda e: e.tensor_copy(out=bglo, in_=bgt), reads=[bgt_buf], writes=[bglo_buf])

    def emit_bg2():
        misc_dma(S_sp, bg2[0:1, :], bghi, [bg2_buf], reads=[bghi_buf, bg2_buf])
        misc_dma(S_sp, bg2[1:2, :], bglo, [bg2_buf], reads=[bglo_buf, bg2_buf])

    wout_bs = [Buf() for _ in range(4)]
    wg_bs = [Buf() for _ in range(2)]
    wout_v = wout_d.rearrange("(k p) d -> p k d", p=128)
    wg_v = wg_d.rearrange("(k p) d -> p k d", p=128)

    row_ctr = [0]

    def row_alloc():
        i = row_ctr[0] % NROW
        row_ctr[0] += 1
        return i

    pending = []

    def flush_pending():
        while pending:
            pending.pop(0)()

    fam_ctr = [0]
    evac_flip = [0]

    xb_ctr = [0]

    stage_ctr = [0]

    def xt_job(sbi, s, defer=False):
        if sbi == 0:
            pool_ = [(xb[0], xb_b[0]), (xb[1], xb_b[1]), (b2[4], b2_b[4]), (b2[5], b2_b[5]),
                     (b2[6], b2_b[6]), (b2[7], b2_b[7])]
        else:
            pool_ = [(xb[0], xb_b[0]), (xb[1], xb_b[1])]
        xbt, xbb = pool_[xb_ctr[0] % len(pool_)]
        xb_ctr[0] += 1
        si = stage_ctr[0]
        stage_ctr[0] += 1
        if sbi == 0:
            sidx = si % NROW
            stg, stg_bufs, stg_sem = row[sidx][:, :], [row_b[sidx]], row_sem[sidx]
        else:
            sidx = si % 2
            stg = abuf[sidx][:].rearrange("p c t -> p (c t)")[:, 0:D]
            stg_bufs, stg_sem = [abuf_b[sidx], abuf_hb[sidx]], abuf_sem[sidx]
        if s == "h":
            src = xh_d if sbi == 0 else x_d[sbi * SB - HALO:sbi * SB, :]
            np_ = HALO
        else:
            t0 = sbi * SB + s * 128
            src = x_d[t0:t0 + 128, :]
            np_ = 128

        def fn_ld(e, h):
            e.dma_start(out=stg[0:np_, :], in_=src).then_inc(h, 16)
        gate = [win_hb[0][0]] if (sbi == 0 and s in (4, 5, 6, 7)) else []
        emit(S_sp, fn_ld, reads=gate, writes=stg_bufs, dma_sem=stg_sem)
        if sbi == 0:
            emit(S_dve, lambda e: e.tensor_copy(out=xbt[0:np_, :], in_=stg[0:np_, :]),
                 reads=stg_bufs, writes=[xbb])
        else:
            emit(S_act, lambda e: e.copy(out=xbt[0:np_, :], in_=stg[0:np_, :]),
                 reads=stg_bufs, writes=[xbb])
        if defer:
            return lambda: xt_compute(s, xbt, xbb)
        xt_compute(s, xbt, xbb)

    def xt_compute(s, xbt, xbb):
        ps1, ps1_b = ps_alloc()
        pv = ps1[:].bitcast(BF16)
        if s == "h":
            def fn_t(e):
                last = None
                for k in range(8):
                    last = e.transpose(out=pv[:, k * HALO:(k + 1) * HALO],
                                       in_=xbt[0:HALO, k * 128:(k + 1) * 128],
                                       identity=ident_b[0:HALO, 0:HALO])
                return last
            emit(S_pe, fn_t, reads=[xbb, identb_buf], writes=[ps1_b])
            emit(S_dve, lambda e: e.tensor_copy(
                out=xT[:, :, 0:HALO], in_=pv[:, 0:8 * HALO].rearrange("p (k t) -> p k t", k=8)),
                reads=[ps1_b], writes=[xTh_b])
            return

        def fn_t(e):
            last = None
            for k in range(8):
                last = e.transpose(out=pv[:, k * 128:(k + 1) * 128],
                                   in_=xbt[:, k * 128:(k + 1) * 128], identity=ident_b[:])
            return last
        emit(S_pe, fn_t, reads=[xbb, identb_buf], writes=[ps1_b])
        c0 = HALO + s * 128
        src_ap = pv.rearrange("p (k t) -> p k t", k=8)
        dst_ap = xT[:, :, c0:c0 + 128]
        if evac_flip[0] % 2 == 0:
            emit(S_dve, lambda e: e.tensor_copy(out=dst_ap, in_=src_ap),
                 reads=[ps1_b], writes=[xT_b[s][0], xT_b[s][1]])
        else:
            emit(S_act, lambda e: e.copy(out=dst_ap, in_=src_ap),
                 reads=[ps1_b], writes=[xT_b[s][0], xT_b[s][1]])
        evac_flip[0] += 1

    def xT_reads(b):
        r = []
        for s in range(4 * b, 4 * b + 4):
            r += xT_b[s]
        return r

    def p_family(sbi, g, slot):
        w = WINDOWS[g]
        for b in range(NBLK):
            bc0 = HALO + b * BLK
            ab = (fam_ctr[0] * NBLK + b) % 2
            a_t, a_b, ah_b = abuf[ab], abuf_b[ab], abuf_hb[ab]
            psa = [ps_alloc(), ps_alloc()]

            def fn_a(e, psa=psa, bc0=bc0):
                last = None
                for c in range(2):
                    for k in range(8):
                        last = e.matmul(psa[c][0][:], lhsT=win[slot][:, k, c * 128:(c + 1) * 128],
                                        rhs=xT[:, k, bc0:bc0 + BLK], start=(k == 0), stop=(k == 7))
                return last
            emit(S_pe, fn_a, reads=[win_hb[slot][0]] + xT_reads(b), writes=[psa[0][1], psa[1][1]])
            for c in range(2):
                emit(S_act, lambda e, c=c, psa=psa, a_t=a_t: e.copy(out=a_t[:, c, HALO:HALO + BLK],
                                                                    in_=psa[c][0][:]),
                     reads=[psa[c][1]], fills=[a_b])
            if b == 0:
                psh, psh_b = ps_alloc()

                def fn_ah(e, psh=psh):
                    last = None
                    for c in range(2):
                        for k in range(8):
                            last = e.matmul(psh[:, c * HALO:(c + 1) * HALO],
                                            lhsT=win[slot][:, k, c * 128:(c + 1) * 128],
                                            rhs=xT[:, k, 0:HALO], start=(k == 0), stop=(k == 7))
                    return last
                emit(S_pe, fn_ah, reads=[win_hb[slot][0], xTh_b], writes=[psh_b])
                emit(S_act, lambda e, psh=psh, a_t=a_t: e.copy(
                    out=a_t[:, :, 0:HALO], in_=psh[:, 0:2 * HALO].rearrange("p (c t) -> p c t", c=2)),
                    reads=[psh_b, ah_b], writes=[ah_b])
            else:
                oth = abuf[1 - ab]
                emit(S_pool, lambda e, a_t=a_t, oth=oth: e.tensor_copy(out=a_t[:, :, 0:HALO],
                                                                       in_=oth[:, :, BLK:BLK + HALO]),
                     reads=[abuf_b[1 - ab], ah_b], writes=[ah_b])
            yield
            if len(pending) >= 2:
                pending.pop(0)()
            sA, sB = f4b[0], f4b[1]
            chain = [(a_t, 1, sA), (sA, 2, sB), (sB, 4, sA), (sA, 8, sB)]
            lo = 0
            src_b = [a_b, ah_b]
            fin, fin_b = None, None
            for step in range(g + 1):
                src, sh, dst = chain[step]
                lo_new = lo + sh
                dst_b = f4b_b[step % 2]
                emit(S_dve, lambda e, src=src, dst=dst, lo=lo, lo_new=lo_new, sh=sh: e.tensor_tensor(
                    out=dst[:, :, lo_new:528], in0=src[:, :, lo_new:528], in1=src[:, :, lo:528 - sh],
                    op=ALU.add),
                    reads=src_b, writes=[dst_b])
                src_b = [dst_b]
                lo = lo_new
                fin, fin_b = dst, dst_b
            pl, pl_b = pooled[b % 2], pooled_b[b % 2]
            pl3 = pl[:].rearrange("p (c t) -> p c t", c=2)
            emit(S_dve, lambda e, fin=fin, a_t=a_t, pl3=pl3: e.scalar_tensor_tensor(
                out=pl3, in0=fin[:, :, HALO:528], scalar=1.0 / w, in1=a_t[:, :, HALO:528],
                op0=ALU.mult, op1=ALU.subtract),
                reads=[fin_b, a_b], writes=[pl_b])
            if sbi == 0 and b == 0:
                emit(S_dve, lambda e, fin=fin: e.tensor_tensor(
                    out=fix_t[:], in0=fin[:, :, HALO:2 * HALO],
                    in1=icnt[:, g * 16:(g + 1) * 16].unsqueeze(1).to_broadcast([128, 2, 16]),
                    op=ALU.mult),
                    reads=[fin_b, icnt_buf, const_b], writes=[const_b])
                emit(S_dve, lambda e, a_t=a_t, pl3=pl3: e.tensor_tensor(
                    out=pl3[:, :, 0:HALO], in0=fix_t[:], in1=a_t[:, :, HALO:2 * HALO], op=ALU.subtract),
                    reads=[const_b, a_b, pl_b], writes=[pl_b])
            psz = [ps_alloc(), ps_alloc()]

            def fn_z(e, psz=psz, bc0=bc0):
                last = None
                for c in range(2):
                    for k in range(8):
                        last = e.matmul(psz[c][0][:], lhsT=win[slot][:, k, 256 + c * 128:256 + (c + 1) * 128],
                                        rhs=xT[:, k, bc0:bc0 + BLK], start=(k == 0), stop=(k == 7))
                return last
            emit(S_pe, fn_z, reads=[win_hb[slot][1]] + xT_reads(b), writes=[psz[0][1], psz[1][1]])
            sz_t, sz_b = szb[b % 2], szb_b[b % 2]
            for c in range(2):
                emit(S_act, lambda e, c=c, psz=psz, sz_t=sz_t: e.activation(
                    out=sz_t[:, c * BLK:(c + 1) * BLK], in_=psz[c][0][:], func=AF.Silu),
                    reads=[psz[c][1]], fills=[sz_b])

            def pw(pl=pl, pl_b=pl_b, sz_t=sz_t, sz_b=sz_b, b=b):
                psw = [ps_alloc(), ps_alloc()]

                def fn_pw(e):
                    last = None
                    for dc in range(2):
                        for kk in range(2):
                            last = e.matmul(psw[dc][0][:], lhsT=wpool[:, g, kk, dc * 128:(dc + 1) * 128],
                                            rhs=pl[:, kk * BLK:(kk + 1) * BLK], start=(kk == 0), stop=(kk == 1))
                    return last
                emit(S_pe, fn_pw, reads=[wpool_b, pl_b], writes=[psw[0][1], psw[1][1]])
                for dc in range(2):
                    ck = 2 * g + dc
                    emit(S_dve, lambda e, dc=dc, ck=ck: e.scalar_tensor_tensor(
                        out=yT[:, ck, b * BLK:(b + 1) * BLK], in0=psw[dc][0][:],
                        scalar=vecs[:, VEC_PS + ck:VEC_PS + ck + 1], in1=sz_t[:, dc * BLK:(dc + 1) * BLK],
                        op0=ALU.mult, op1=ALU.mult),
                        reads=[psw[dc][1], vecs_buf, sz_b], writes=[yT_b[ck][b]])
            pending.append(pw)
            yield

    z_ctr = [0]
    szq_b = [Buf() for _ in range(4)]

    def z_family(sbi, fz, slot):
        for b in range(NBLK):
            bc0 = HALO + b * BLK
            for c in range(4):
                ck = 8 + 4 * fz + c
                psz, psz_b = ps_alloc()

                def fn_z(e, psz=psz, c=c, bc0=bc0):
                    last = None
                    for k in range(8):
                        last = e.matmul(psz[:], lhsT=win[slot][:, k, c * 128:(c + 1) * 128],
                                        rhs=xT[:, k, bc0:bc0 + BLK], start=(k == 0), stop=(k == 7))
                    return last
                emit(S_pe, fn_z, reads=[win_hb[slot][c // 2]] + xT_reads(b), writes=[psz_b])
                zi = z_ctr[0] % 4
                z_ctr[0] += 1
                zt, zt_b = szb[zi // 2], szq_b[zi]
                emit(S_act, lambda e, psz=psz, zt=zt, zi=zi: e.activation(
                    out=zt[:, (zi % 2) * BLK:(zi % 2 + 1) * BLK], in_=psz[:], func=AF.Silu),
                    reads=[psz_b], writes=[zt_b], fills=[szb_b[zi // 2]])
                emit(S_dve, lambda e, zt=zt, zi=zi, ck=ck, b=b: e.tensor_tensor(
                    out=yT[:, ck, b * BLK:(b + 1) * BLK], in0=yT[:, ck, b * BLK:(b + 1) * BLK],
                    in1=zt[:, (zi % 2) * BLK:(zi % 2 + 1) * BLK], op=ALU.mult),
                    reads=[zt_b, szb_b[zi // 2], yT_b[ck][b]], writes=[yT_b[ck][b]])
                if c == 3:
                    flush_pending()
                yield

    def s_family(sbi, h, slot):
        for b in range(NBLK):
            bc0 = HALO + b * BLK
            it = (fam_ctr[0] * NBLK + b) % 2
            gv_t, gv_b = f4c[it], f4c_b[it]
            vn_t, vn_b = vnb[it], vnb_b[it]
            gu_t, gu_b = gub[it], gub_b[it]
            t1_t, t1_b = f4b[it], f4b_b[it]
            psv = [ps_alloc(), ps_alloc()]

            def fn_v(e, psv=psv, bc0=bc0):
                last = None
                for s in range(4):
                    for k in range(8):
                        last = e.matmul(psv[s // 2][0][:, (s % 2) * 256:(s % 2 + 1) * 256],
                                        lhsT=xT[:, k, bc0 + s * 128:bc0 + (s + 1) * 128],
                                        rhs=win[slot][:, k, 256:512], start=(k == 0), stop=(k == 7))
                return last
            emit(S_pe, fn_v, reads=[win_hb[slot][1]] + xT_reads(b), writes=[psv[0][1], psv[1][1]])
            for i in range(2):
                emit(S_act, lambda e, i=i, psv=psv, gv_t=gv_t: e.activation(
                    out=gv_t[:, i * 512:(i + 1) * 512], in_=psv[i][0][:], func=AF.Gelu_apprx_tanh),
                    reads=[psv[i][1]], fills=[gv_b])
            for s in range(4):
                emit(S_dve, lambda e, s=s, gv_t=gv_t, it=it: e.bn_stats(
                    out=st6[it][:, s, :], in_=gv_t[:, s * 256:(s + 1) * 256]),
                    reads=[gv_b], fills=[stat_b[it]])
            for s in range(4):
                emit(S_dve, lambda e, s=s, it=it: e.bn_aggr(out=mv[it][:, s, :], in_=st6[it][:, s, :]),
                     reads=[stat_b[it]], fills=[mv_b[it]])
            emit(S_pool, lambda e, it=it: e.tensor_scalar(
                out=ve[it][:], in0=mv[it][:, :, 1], scalar1=LN_EPS, scalar2=1.0, op0=ALU.add, op1=ALU.mult),
                reads=[mv_b[it]], writes=[ve_b[it]])
            emit(S_pool, lambda e, it=it: e.tensor_tensor(
                out=rstd[it][:], in0=ve[it][:], in1=negh[:, 0:4], op=ALU.pow),
                reads=[ve_b[it], negh_buf], writes=[rstd_b[it]])
            for s in range(4):
                emit(S_dve, lambda e, s=s, gv_t=gv_t, vn_t=vn_t, it=it: e.tensor_scalar(
                    out=vn_t[:, s * 256:(s + 1) * 256], in0=gv_t[:, s * 256:(s + 1) * 256],
                    scalar1=mv[it][:, s, 0:1], scalar2=rstd[it][:, s:s + 1],
                    op0=ALU.subtract, op1=ALU.mult),
                    reads=[gv_b, mv_b[it], rstd_b[it]], fills=[vn_b])
            yield
            psu = [ps_alloc(), ps_alloc()]

            def fn_u(e, psu=psu, bc0=bc0):
                last = None
                for c in range(2):
                    for k in range(8):
                        last = e.matmul(psu[c][0][:], lhsT=win[slot][:, k, c * 128:(c + 1) * 128],
                                        rhs=xT[:, k, bc0:bc0 + BLK], start=(k == 0), stop=(k == 7))
                return last
            emit(S_pe, fn_u, reads=[win_hb[slot][0]] + xT_reads(b), writes=[psu[0][1], psu[1][1]])
            for c in range(2):
                emit(S_act, lambda e, c=c, psu=psu, gu_t=gu_t: e.activation(
                    out=gu_t[:, c * BLK:(c + 1) * BLK], in_=psu[c][0][:], func=AF.Gelu_apprx_tanh),
                    reads=[psu[c][1]], fills=[gu_b])
            flush_pending()

            def sjob(vn_t=vn_t, vn_b=vn_b, gu_t=gu_t, gu_b=gu_b, t1_t=t1_t, t1_b=t1_b, b=b):
                pss = [ps_alloc(), ps_alloc()]

                def fn_s(e):
                    last = None
                    for c in range(2):
                        for s in range(4):
                            last = e.matmul(pss[c][0][:, s * 128:(s + 1) * 128],
                                            lhsT=vn_t[:, s * 256 + c * 128:s * 256 + (c + 1) * 128],
                                            rhs=wmT[:, h, :], start=True, stop=True)
                    return last
                emit(S_pe, fn_s, reads=[vn_b, wmT_buf], writes=[pss[0][1], pss[1][1]])
                for c in range(2):
                    k = 2 * h + c
                    emit(S_dve, lambda e, c=c, k=k: e.scalar_tensor_tensor(
                        out=t1_t[:, c, 0:BLK].rearrange("p (s i) -> p s i", s=4),
                        in0=pss[c][0][:].rearrange("p (s i) -> p s i", s=4),
                        scalar=vecs[:, VEC_SG + k:VEC_SG + k + 1],
                        in1=cst[:, k:k + 1, :].to_broadcast([128, 4, 128]),
                        op0=ALU.mult, op1=ALU.add),
                        reads=[pss[c][1], vecs_buf, cst_buf], fills=[t1_b])
                ck = 8 + 2 * h
                emit(S_dve, lambda e: e.tensor_tensor(
                    out=yT[:, ck:ck + 2, b * BLK:(b + 1) * BLK], in0=t1_t[:, :, 0:BLK],
                    in1=gu_t[:].rearrange("p (c t) -> p c t", c=2), op=ALU.mult),
                    reads=[t1_b, gu_b], writes=[yT_b[ck][b], yT_b[ck + 1][b]])
            pending.append(sjob)
            yield

    late_pieces = []
    for q in range(4):
        late_pieces.append(lambda q=q: misc_dma(S_pool, wout[:, 4 * q:4 * q + 4, :], wout_v[:, 4 * q:4 * q + 4, :],
                                                [wout_bs[q]]))
    for q in range(2):
        late_pieces.append(lambda q=q: misc_dma(S_pool, wg[:, 4 * q:4 * q + 4, :], wg_v[:, 4 * q:4 * q + 4, :],
                                                [wg_bs[q]]))
    late_pieces.append(lambda: misc_dma(S_pool, wp[:], wp_d.rearrange("(k p) d -> p k d", p=128), [wp_b]))

    def phase1(sbi, first_xt_rest):
        for fi, f in enumerate(fam_list):
            slot = fam_ctr[0] % 2
            kind, i = f
            if kind == "P":
                gen = p_family(sbi, i, slot)
            elif kind == "Z":
                gen = z_family(sbi, i, slot)
            else:
                gen = s_family(sbi, i, slot)
            last_fam = (fi == len(fam_list) - 1)
            nsteps = 0
            for _ in gen:
                nsteps += 1
                if fi == 0 and nsteps == 1 and first_xt_rest:
                    for s_ in first_xt_rest:
                        xt_job(sbi, s_)
                if sbi == 0 and fi == 1 and nsteps == 1:
                    emit_const_b()
                if sbi == 0 and fi == 1 and nsteps == 2:
                    emit_const_c()
                    emit_bg2()
                yield
            emit_fam_load(fam_ctr[0] + 2)
            if FLAG_LATE_PIECES:
                if late_pieces:
                    late_pieces.pop(0)()
            elif fi == 1 and sbi == 0:
                while late_pieces:
                    late_pieces.pop(0)()
            fam_ctr[0] += 1
            if fi == 0 and sbi == 0:
                emit_const_compute()
        flush_pending()
        yield

    NCH = SB // 128
    out_toks = []

    def phase2(sbi):
        st = {}

        def loads(j):
            t0 = sbi * SB + j * 128
            rs = row_alloc()
            ps_ = j % 2

            def fn_x(e, h):
                e.dma_start(out=row[rs][:, :], in_=x_d[t0:t0 + 128, :]).then_inc(h, 16)
            emit(S_sp, fn_x, writes=[row_b[rs]], dma_sem=row_sem[rs])

            def fn_p(e, h):
                e.dma_start(out=pin[ps_][:, :], in_=p_d[t0:t0 + 128, :]).then_inc(h, 16)
            emit(S_sp, fn_p, writes=[pin_b[ps_]], dma_sem=pin_sem[ps_])
            st[j] = {"row": rs, "pin": ps_}

        def affine_first(j):
            return j == 0 or (sbi == NSB - 1 and j == NCH - 1)

        def stage_m(j):
            rs, pi = st[j]["row"], st[j]["pin"]
            b = (j * 128) // BLK
            tc0 = j * 128
            pst, pst_b = ps_alloc()

            def fn_pt(e):
                last = None
                for kk in range(2):
                    last = e.transpose(out=pst[:, kk * 128:(kk + 1) * 128],
                                       in_=pin[pi][:, kk * 128:(kk + 1) * 128], identity=ident_f[:])
                return last
            emit(S_pe, fn_pt, reads=[pin_b[pi], ident_buf], writes=[pst_b])
            pts = j % 3
            emit(S_act, lambda e: e.copy(out=pT[pts][:].rearrange("p k t -> p (k t)"), in_=pst[:, 0:256]),
                 reads=[pst_b, pT_b[pts]], writes=[pT_b[pts]])
            psm = [ps_alloc(), ps_alloc()]

            for hf_ in range(2):
                def fn_m(e, hf=hf_):
                    last = None
                    for k in range(16):
                        last = e.matmul(psm[hf][0][:], lhsT=yT[:, k, tc0:tc0 + 128],
                                        rhs=wout[:, k, hf * 512:(hf + 1) * 512], start=(k == 0), stop=(k == 15))
                    return last
                emit(S_pe, fn_m, reads=wout_bs + [yT_b[k][b] for k in range(16)], writes=[psm[hf_][1]])
            i2 = j % 2
            i3 = j % 3
            for hf in range(2):
                emit(S_dve, lambda e, hf=hf: e.scalar_tensor_tensor(
                    out=row[rs][:, hf * 512:(hf + 1) * 512], in0=row[rs][:, hf * 512:(hf + 1) * 512],
                    scalar=ALPHA, in1=psm[hf][0][:], op0=ALU.mult, op1=ALU.add),
                    reads=[psm[hf][1], row_b[rs]], fills=[row_b[rs]])
            for hf in range(2):
                emit(S_dve, lambda e, hf=hf: e.bn_stats(out=st6p[i3][:, hf, :],
                                                        in_=row[rs][:, hf * 512:(hf + 1) * 512]),
                     reads=[row_b[rs]], fills=[statp_b[i3]])
            emit(S_dve, lambda e: e.bn_aggr(out=mvp[i3][:], in_=st6p[i3][:].rearrange("p a b -> p (a b)")),
                 reads=[statp_b[i3]], writes=[mvp_b[i3]])
            emit(S_pool, lambda e: e.tensor_scalar(
                out=vep[i3][:], in0=mvp[i3][:, 1:2], scalar1=LN_EPS, scalar2=1.0, op0=ALU.add, op1=ALU.mult),
                reads=[mvp_b[i3]], writes=[vep_b[i3]])
            emit(S_pool, lambda e: e.tensor_tensor(
                out=rstdp[i3][:], in0=vep[i3][:], in1=negh[:, 0:1], op=ALU.pow),
                reads=[vep_b[i3], negh_buf], writes=[rstdp_b[i3]])
            aff = affine_first(j)
            st[j]["aff"] = aff
            if aff:
                emit(S_dve, lambda e: e.scalar_tensor_tensor(
                    out=row[rs][:], in0=row[rs][:], scalar=mvp[i3][:, 0:1], in1=gbc[:],
                    op0=ALU.subtract, op1=ALU.mult),
                    reads=[row_b[rs], mvp_b[i3], gb_buf], writes=[row_b[rs]])
                emit(S_dve, lambda e: e.scalar_tensor_tensor(
                    out=xhat[i2][:], in0=row[rs][:], scalar=rstdp[i3][:, 0:1], in1=bbc[:],
                    op0=ALU.mult, op1=ALU.add),
                    reads=[row_b[rs], rstdp_b[i3], gb_buf], writes=[xhat_b[i2]])
            else:
                emit(S_dve, lambda e: e.tensor_scalar(
                    out=xhat[i2][:], in0=row[rs][:], scalar1=mvp[i3][:, 0:1], scalar2=rstdp[i3][:, 0:1],
                    op0=ALU.subtract, op1=ALU.mult),
                    reads=[row_b[rs], mvp_b[i3], rstdp_b[i3]], writes=[xhat_b[i2]])
            st[j]["pT"] = pts
            st[j]["i2"] = i2
            st[j]["i3"] = i3

        def stage_t(j):
            i2 = st[j]["i2"]
            pst_, psb_b = ps_alloc()
            psb = pst_[:].bitcast(BF16)

            def fn_t(e):
                last = None
                for k in range(8):
                    last = e.transpose(out=psb[:, k * 128:(k + 1) * 128], in_=xhat[i2][:, k * 128:(k + 1) * 128],
                                       identity=ident_b[:])
                return last
            emit(S_pe, fn_t, reads=[xhat_b[i2], identb_buf], writes=[psb_b])
            fast_tail = False
            if st[j]["aff"]:
                emit(S_act, lambda e: e.copy(out=xnT[i2][:], in_=psb[:, :]), reads=[psb_b], writes=[xnT_b[i2]])
                return
            for k in range(8):
                if fast_tail and k >= 4:
                    emit(S_dve, lambda e, k=k: e.tensor_scalar(
                        out=xnT[i2][:, k * 128:(k + 1) * 128], in0=psb[:, k * 128:(k + 1) * 128],
                        scalar1=vecs[:, VEC_LG + k:VEC_LG + k + 1], scalar2=vecs[:, VEC_LB + k:VEC_LB + k + 1],
                        op0=ALU.mult, op1=ALU.add),
                        reads=[psb_b, vecs_buf], fills=[xnT_b[i2]])
                    continue
                emit(S_act, lambda e, k=k: e.activation(
                    out=xnT[i2][:, k * 128:(k + 1) * 128], in_=psb[:, k * 128:(k + 1) * 128], func=AF.Identity,
                    bias=vecs[:, VEC_LB + k:VEC_LB + k + 1], scale=vecs[:, VEC_LG + k:VEC_LG + k + 1]),
                    reads=[psb_b, vecs_buf], fills=[xnT_b[i2]])

        def stage_g(j):
            rs, i2, pts, i3 = st[j]["row"], st[j]["i2"], st[j]["pT"], st[j]["i3"]
            t0 = sbi * SB + j * 128
            g_t, g_b = f4c[i2], f4c_b[i2]
            psp = [ps_alloc(), ps_alloc()]

            def fn_pl(e):
                last = None
                for hf in range(2):
                    for kk in range(2):
                        last = e.matmul(psp[hf][0][:], lhsT=pT[pts][:, kk, :],
                                        rhs=wp[:, kk, hf * 512:(hf + 1) * 512], start=(kk == 0), stop=(kk == 1))
                return last
            emit(S_pe, fn_pl, reads=[pT_b[pts], wp_b], writes=[psp[0][1], psp[1][1]])
            psg = [ps_alloc(), ps_alloc()]

            for hf_ in range(2):
                def fn_g(e, hf=hf_):
                    for k in range(8):
                        e.matmul(psg[hf][0][:], lhsT=xnT[i2][:, k * 128:(k + 1) * 128],
                                 rhs=wg[:, k, hf * 512:(hf + 1) * 512], start=(k == 0), stop=False)
                    return e.matmul(psg[hf][0][:], lhsT=ones2[:, :], rhs=bg2[:, hf * 512:(hf + 1) * 512],
                                    start=False, stop=True)
                emit(S_pe, fn_g, reads=[xnT_b[i2], ones2_buf, bg2_buf] + wg_bs, writes=[psg[hf_][1]])
            for hf in range(2):
                emit(S_act, lambda e, hf=hf: e.activation(
                    out=g_t[:, hf * 512:(hf + 1) * 512], in_=psg[hf][0][:], func=AF.Sigmoid),
                    reads=[psg[hf][1]], fills=[g_b])
            if not st[j]["aff"]:
                emit(S_dve, lambda e: e.scalar_tensor_tensor(
                    out=row[rs][:], in0=row[rs][:], scalar=mvp[i3][:, 0:1], in1=gbc[:],
                    op0=ALU.subtract, op1=ALU.mult),
                    reads=[row_b[rs], mvp_b[i3], gb_buf], writes=[row_b[rs]])
            emit(S_dve, lambda e: e.scalar_tensor_tensor(
                out=row[rs][:], in0=row[rs][:], scalar=rstdp[i3][:, 0:1], in1=bbc[:],
                op0=ALU.mult, op1=ALU.add),
                reads=[row_b[rs], rstdp_b[i3], gb_buf], writes=[row_b[rs]])
            for hf in range(2):
                emit(S_dve, lambda e, hf=hf: e.tensor_tensor(
                    out=g_t[:, hf * 512:(hf + 1) * 512], in0=g_t[:, hf * 512:(hf + 1) * 512],
                    in1=psp[hf][0][:], op=ALU.mult),
                    reads=[g_b, psp[hf][1]], fills=[g_b])
            fin_eng = S_dve
            emit(fin_eng, lambda e: e.tensor_tensor(out=row[rs][:], in0=row[rs][:], in1=g_t[:], op=ALU.add),
                 reads=[row_b[rs], g_b], writes=[row_b[rs]])

            def fn_st(e, h):
                e.dma_start(out=out_d[t0:t0 + 128, :], in_=row[rs][:, :]).then_inc(h, 16)
            out_toks.append(emit(S_sp, fn_st, reads=[row_b[rs]], dma_sem=row_sem[rs]))

        loads(0)
        for j in range(NCH + 1):
            xt_def = []
            if sbi + 1 < NSB and j <= 6:
                todo = {0: ["h", 0], 1: [1, 2]}.get(j, [j + 1])
                for s_ in todo:
                    xt_def.append(xt_job(sbi + 1, s_, defer=True))
            if j + 1 < NCH:
                loads(j + 1)
            if j == 0:
                stage_m(0)
                yield j
                stage_m(1)
                yield j
            elif 2 <= j < NCH:
                stage_m(j)
                yield j
            if 1 <= j:
                stage_g(j - 1)
                yield j
            if j < NCH:
                stage_t(j)
                yield j
            for f_ in xt_def:
                f_()

    warm_b = Buf()
    emit(S_pool, lambda e: e.memset(warm[:], 0.0), writes=[warm_b])
    emit(S_act, lambda e: e.activation(out=warm[:], in_=warm[:], func=AF.Silu), writes=[warm_b])
    emit_fam_load(0)
    defs_ = [xt_job(0, s_, defer=True) for s_ in ["h", 0, 1, 2, 3]]
    for f_ in defs_:
        f_()
    misc_dma(S_pool, wpool[:], poolw_d.rearrange("g (kk p) d -> p g kk d", p=128), [wpool_b])
    emit_fam_load(1)
    emit_small_const_loads()
    if not FLAG_CONST_LATE:
        emit_const_compute()
        emit_bg2()
    g1 = phase1(0, [4, 5, 6, 7])
    for _ in g1:
        pass
    for sbi in range(NSB):
        if dbg and sbi == 0:
            def fn_dbg(e, h):
                e.dma_start(out=dbg_y, in_=yT[:].rearrange("p k t -> p (k t)")).then_inc(h, 16)
            out_toks.append(emit(S_sp, fn_dbg, reads=[yT_b[k][b] for k in range(16) for b in range(NBLK)],
                                 dma_sem=misc_sem[misc_ctr[0]]))
            misc_ctr[0] += 1
        g2 = phase2(sbi)
        g1n = phase1(sbi + 1, None) if sbi + 1 < NSB else None
        budget = 0
        for j in g2:
            if g1n is not None and j >= NCH - 1 and budget < INTERLEAVE_STEPS:
                budget += 1
                if next(g1n, "done") == "done":
                    g1n = None
        if g1n is not None:
            for _ in g1n:
                pass

    fin = {}
    for tok in out_toks:
        _merge(fin, tok)
    for s, v in fin.items():
        S_sp.items.append(("wait", s, v))

    with nc.Block() as block:
        @block.tensor
        def _(e):
            replay(S_pe, e)

        @block.scalar
        def _(e):
            replay(S_act, e)

        @block.vector
        def _(e):
            replay(S_dve, e)

        @block.gpsimd
        def _(e):
            replay(S_pool, e)

        @block.sync
        def _(e):
            replay(S_sp, e)
    es.close()
    return nc


def _chunkT(v):
    return np.ascontiguousarray(np.asarray(v, dtype=np.float32).reshape(-1, 128).T)


def make_in_maps(x, p, w_in, pool_w, pool_scale, sgu_ln_g, sgu_ln_b, sgu_w, sgu_b,
                 w_out, ln_g, ln_b, ple_w, ple_gate_w, ple_gate_b):
    f = np.float32
    x = np.asarray(x, f)
    p = np.asarray(p, f)[0]
    vecs = np.concatenate([_chunkT(pool_scale[0]), _chunkT(sgu_ln_g[0]), _chunkT(sgu_ln_b[0]),
                           _chunkT(ln_g[0]), _chunkT(ln_b[0])], axis=1)
    shared = {
        "w_in": np.ascontiguousarray(np.asarray(w_in, f)[0]),
        "pool_w": np.ascontiguousarray(np.asarray(pool_w, f)[0]),
        "w_out": np.ascontiguousarray(np.asarray(w_out, f)[0]),
        "ple_w": np.ascontiguousarray(np.asarray(ple_w, f)[0]),
        "ple_gate_w": np.ascontiguousarray(np.asarray(ple_gate_w, f)[0]),
        "ple_gate_b": np.ascontiguousarray(np.asarray(ple_gate_b, f)[0].reshape(1, D)),
        "vecs": np.ascontiguousarray(vecs),
        "gbc": np.ascontiguousarray(np.broadcast_to(np.asarray(ln_g, f)[0][None, :], (128, D))),
        "bbc": np.ascontiguousarray(np.broadcast_to(np.asarray(ln_b, f)[0][None, :], (128, D))),
        "bsb": np.ascontiguousarray(np.broadcast_to(np.asarray(sgu_b, f)[0].reshape(1, 512), (128, 512))),
        "tril": np.tril(np.ones((128, 128), f)),
        "ident": np.eye(128, dtype=f),
        "sgw": np.ascontiguousarray(np.asarray(sgu_w, f)[0].transpose(1, 0, 2).reshape(128, 512)),
    }
    in_maps = []
    for c in range(N_CORES):
        b, q = divmod(c, 4)
        t0 = q * TOK
        xc = np.ascontiguousarray(x[b, t0:t0 + TOK])
        if q == 0:
            xh = np.zeros((HALO, D), f)
        else:
            xh = np.ascontiguousarray(x[b, t0 - HALO:t0])
        ic = np.zeros((4, 16), f)
        for g, w in enumerate(WINDOWS):
            for t in range(16):
                ic[g, t] = 1.0 / (min(t + 1, w) if q == 0 else w)
        m = dict(shared)
        m["x"] = xc
        m["xh"] = xh
        m["p"] = np.ascontiguousarray(p[b, t0:t0 + TOK])
        m["icnt"] = np.ascontiguousarray(np.broadcast_to(ic.reshape(1, 64), (128, 64)))
        in_maps.append(m)
    return in_maps


_NC_CACHE = {}


def kernel(**inputs):
    in_maps = make_in_maps(**inputs)
    if "nc" not in _NC_CACHE:
        _NC_CACHE["nc"] = build_program()
    nc = _NC_CACHE["nc"]
    res = run_bass_kernel_spmd(nc, in_maps, core_ids=list(range(N_CORES)))
    out = np.empty((2, 4 * TOK, D), np.float32)
    for c in range(N_CORES):
        b, q = divmod(c, 4)
        out[b, q * TOK:(q + 1) * TOK] = res.results[c]["out"]
    return out
```
